# Optimizing a Trainium2 kernel written in Bass

```python
import math
import jax, jax.numpy as jnp
from jax import lax
import numpy as np

D_MODEL = 1024
BATCH = 32
SEQ = 2048
DEPTH = 4

GRID_W = 64
CTX_LEN = 256
EPS = 1e-6
D_FF = 4 * D_MODEL
D_FOURIER = D_MODEL // 2
FOURIER_GROUPS = 4
FOURIER_CH = D_FOURIER // FOURIER_GROUPS
D_SSM = D_MODEL - D_FOURIER
SSM_GROUP = 16
SSM_GROUPS = D_SSM // SSM_GROUP
SSM_STATE = 64
DT_MIN = 0.001
DT_MAX = 0.1
DA_HEAD_DIM = 64
DA_V_DIM = 2 * DA_HEAD_DIM
DA_HEADS = D_MODEL // DA_V_DIM
ROPE_THETA = 10000.0
Q_BLOCK = 128
N_EVEN = (DEPTH + 1) // 2
N_ODD = DEPTH // 2

kernel_name = "hybrid_fourier_s5_diffattn_prefix_dit"

F32 = jnp.float32


def rmsnorm(x, g):
    xf = x.astype(F32)
    y = xf * lax.rsqrt(jnp.mean(xf * xf, axis=-1, keepdims=True) + EPS)
    return (y * g.astype(F32)).astype(x.dtype)


def modulate(h, shift, scale):
    return h * (1.0 + scale) + shift


def adaln(cond, w, b):
    m = jax.nn.silu(cond) @ w + b
    return jnp.split(m[..., None, :], 6, axis=-1)


def sq_relu_mlp(h, w1, w2):
    return jnp.square(jax.nn.relu(h @ w1)) @ w2


def axial_rope_angles(rows):
    row = jnp.repeat(jnp.arange(rows, dtype=F32), GRID_W)
    col = jnp.tile(jnp.arange(GRID_W, dtype=F32), rows)
    half = DA_HEAD_DIM // 2
    inv_freq = ROPE_THETA ** (-jnp.arange(0, half, 2, dtype=F32) / half)
    ang_row = (row[:, None] * inv_freq)[:, None, None, :]
    ang_col = (col[:, None] * inv_freq)[:, None, None, :]
    return ang_row, ang_col


def rope_axis(x, ang):
    x1, x2 = jnp.split(x, 2, axis=-1)
    cos, sin = jnp.cos(ang), jnp.sin(ang)
    return jnp.concatenate([x1 * cos - x2 * sin, x1 * sin + x2 * cos], axis=-1).astype(x.dtype)


def apply_axial_rope(x, ang_row, ang_col):
    half = DA_HEAD_DIM // 2
    return jnp.concatenate([rope_axis(x[..., :half], ang_row), rope_axis(x[..., half:], ang_col)], axis=-1)


def fourier_mix(u):
    bsz, l, _ = u.shape
    ug = u.astype(F32).reshape(bsz, l, FOURIER_GROUPS, FOURIER_CH)
    y = jnp.fft.fft2(ug, axes=(1, 3), norm="ortho").real
    return y.reshape(bsz, l, D_FOURIER).astype(u.dtype)


def _linear_recurrence(e1, e2):
    a1, b1 = e1
    a2, b2 = e2
    return a1 * a2, a2 * b1 + b2


def s5_scan(u, a_re, a_im, log_dt, b_re, b_im, h0s):
    bsz, l, _ = u.shape
    ug = u.astype(F32).reshape(bsz, l, SSM_GROUPS, SSM_GROUP)
    states, finals = [], []
    for d in range(2):
        a = lax.complex(a_re[d].astype(F32), a_im[d].astype(F32))
        dt = jnp.exp(log_dt[d].astype(F32))[:, None]
        a_bar = jnp.exp(a * dt)
        b_c = lax.complex(b_re[d].astype(F32), b_im[d].astype(F32))
        b_bar = ((a_bar - 1.0) / a)[..., None] * b_c
        bu = jnp.einsum("blgh,gph->blgp", ug, b_bar)
        reverse = d == 1
        if h0s is not None:
            edge = -1 if reverse else 0
            bu = bu.at[:, edge].add(a_bar * h0s[d])
        h = lax.associative_scan(_linear_recurrence, (jnp.broadcast_to(a_bar, bu.shape), bu),
                                 axis=1, reverse=reverse)[1]
        states.append(h)
        finals.append(h[:, 0] if reverse else h[:, -1])
    return states, finals


def s5_readout(u, states, c_re, c_im, d_skip, w_glu, b_glu):
    bsz, l, _ = u.shape
    y = u.astype(F32) * d_skip.astype(F32)
    for d in range(2):
        cc = lax.complex(c_re[d].astype(F32), c_im[d].astype(F32))
        y = y + jnp.einsum("blgp,ghp->blgh", states[d], cc).real.reshape(bsz, l, D_SSM)
    y = jax.nn.gelu(y).astype(u.dtype)
    return y * jax.nn.sigmoid(y @ w_glu + b_glu)


def even_out(z, states, c_re, c_im, d_skip, w_glu, b_glu, w_out):
    ya = fourier_mix(z[..., :D_FOURIER])
    yb = s5_readout(z[..., D_FOURIER:], states, c_re, c_im, d_skip, w_glu, b_glu)
    return jnp.concatenate([ya, yb], axis=-1) @ w_out


def diff_qkv(h, w_in, gq, gk, ang):
    bsz, l, _ = h.shape
    q, k, v = jnp.split(h @ w_in, 3, axis=-1)
    q = rmsnorm(q.reshape(bsz, l, DA_HEADS, 2, DA_HEAD_DIM), gq)
    k = rmsnorm(k.reshape(bsz, l, DA_HEADS, 2, DA_HEAD_DIM), gk)
    v = v.reshape(bsz, l, DA_HEADS, DA_V_DIM)
    if ang is not None:
        q = apply_axial_rope(q, *ang)
        k = apply_axial_rope(k, *ang)
    return q, k, v


def diff_attend(q, k, v, lam):
    s = jnp.einsum("bqhmd,bkhmd->bhmqk", q, k).astype(F32) * (DA_HEAD_DIM ** -0.5)
    p = jax.nn.softmax(s, axis=-1)
    w = (p[:, :, 0] - lam * p[:, :, 1]).astype(v.dtype)
    return jnp.einsum("bhqk,bkhd->bqhd", w, v)


def blocked_diff_attention(q, k, v, lam):
    bsz, l = q.shape[:2]
    nb = l // Q_BLOCK
    qb = jnp.moveaxis(q.reshape(bsz, nb, Q_BLOCK, DA_HEADS, 2, DA_HEAD_DIM), 1, 0)
    ob = lax.map(lambda qi: diff_attend(qi, k, v, lam), qb)
    return jnp.moveaxis(ob, 0, 1).reshape(bsz, l, DA_HEADS, DA_V_DIM)


def diff_out(o, g_head, lam_init, w_out):
    bsz, l = o.shape[:2]
    o = rmsnorm(o, g_head) * (1.0 - lam_init)
    return o.reshape(bsz, l, DA_HEADS * DA_V_DIM) @ w_out


def setup_inputs(seed: int = 0) -> dict:
    key = jax.random.key(seed)
    keys = iter(jax.random.split(key, 40))

    def normal(shape, scale):
        return jax.random.normal(next(keys), shape, F32) * scale

    a_im_base = jnp.pi * jnp.arange(SSM_STATE, dtype=F32)
    return {
        "x": normal((BATCH, SEQ, D_MODEL), 1.0),
        "c": normal((BATCH, D_MODEL), 1.0),
        "ctx": normal((BATCH, CTX_LEN, D_MODEL), 1.0),
        "c_ctx": normal((D_MODEL,), 1.0),
        "norm1_g": 1.0 + normal((DEPTH, D_MODEL), 0.01),
        "norm2_g": 1.0 + normal((DEPTH, D_MODEL), 0.01),
        "ada_w": normal((DEPTH, D_MODEL, 6 * D_MODEL), 0.5 * D_MODEL ** -0.5),
        "ada_b": normal((DEPTH, 6 * D_MODEL), 0.01),
        "mlp_w1": normal((DEPTH, D_MODEL, D_FF), D_MODEL ** -0.5),
        "mlp_w2": normal((DEPTH, D_FF, D_MODEL), D_FF ** -0.5),
        "ev_w_in": normal((N_EVEN, D_MODEL, D_FOURIER + D_SSM), D_MODEL ** -0.5),
        "ev_w_out": normal((N_EVEN, D_FOURIER + D_SSM, D_MODEL), (D_FOURIER + D_SSM) ** -0.5),
        "ssm_a_re": -0.5 + normal((N_EVEN, 2, SSM_GROUPS, SSM_STATE), 0.01),
        "ssm_a_im": a_im_base + normal((N_EVEN, 2, SSM_GROUPS, SSM_STATE), 0.01),
        "ssm_log_dt": jax.random.uniform(next(keys), (N_EVEN, 2, SSM_GROUPS), F32,
                                         minval=math.log(DT_MIN), maxval=math.log(DT_MAX)),
        "ssm_b_re": normal((N_EVEN, 2, SSM_GROUPS, SSM_STATE, SSM_GROUP), (2 * SSM_GROUP) ** -0.5),
        "ssm_b_im": normal((N_EVEN, 2, SSM_GROUPS, SSM_STATE, SSM_GROUP), (2 * SSM_GROUP) ** -0.5),
        "ssm_c_re": normal((N_EVEN, 2, SSM_GROUPS, SSM_GROUP, SSM_STATE), (2 * SSM_STATE) ** -0.5),
        "ssm_c_im": normal((N_EVEN, 2, SSM_GROUPS, SSM_GROUP, SSM_STATE), (2 * SSM_STATE) ** -0.5),
        "ssm_d": normal((N_EVEN, D_SSM), 1.0),
        "ssm_w_glu": normal((N_EVEN, D_SSM, D_SSM), D_SSM ** -0.5),
        "ssm_b_glu": normal((N_EVEN, D_SSM), 0.01),
        "od_w_in": normal((N_ODD, D_MODEL, 3 * D_MODEL), D_MODEL ** -0.5),
        "od_w_out": normal((N_ODD, D_MODEL, D_MODEL), D_MODEL ** -0.5),
        "od_q_norm": 1.0 + normal((N_ODD, DA_HEAD_DIM), 0.01),
        "od_k_norm": 1.0 + normal((N_ODD, DA_HEAD_DIM), 0.01),
        "od_lambda": normal((N_ODD, 4, DA_HEAD_DIM), 0.1),
        "od_head_norm": 1.0 + normal((N_ODD, DA_V_DIM), 0.01),
    }


def reference(x, c, ctx, c_ctx, norm1_g, norm2_g, ada_w, ada_b, mlp_w1, mlp_w2,
              ev_w_in, ev_w_out, ssm_a_re, ssm_a_im, ssm_log_dt, ssm_b_re, ssm_b_im,
              ssm_c_re, ssm_c_im, ssm_d, ssm_w_glu, ssm_b_glu,
              od_w_in, od_w_out, od_q_norm, od_k_norm, od_lambda, od_head_norm):
    rows = x.shape[1] // GRID_W
    ang = axial_rope_angles(rows)
    for i in range(DEPTH):
        last = i == DEPTH - 1
        j = i // 2
        sh1, sc1, g1, sh2, sc2, g2 = adaln(c, ada_w[i], ada_b[i])
        csh1, csc1, cg1, csh2, csc2, cg2 = adaln(c_ctx, ada_w[i], ada_b[i])
        hx = modulate(rmsnorm(x, norm1_g[i]), sh1, sc1)
        hc = modulate(rmsnorm(ctx, norm1_g[i]), csh1, csc1)
        if i % 2 == 0:
            scan_p = (ssm_a_re[j], ssm_a_im[j], ssm_log_dt[j], ssm_b_re[j], ssm_b_im[j])
            read_p = (ssm_c_re[j], ssm_c_im[j], ssm_d[j], ssm_w_glu[j], ssm_b_glu[j], ev_w_out[j])
            zc = hc @ ev_w_in[j]
            zx = hx @ ev_w_in[j]
            st_c, fin_c = s5_scan(zc[..., D_FOURIER:], *scan_p, None)
            st_x, _ = s5_scan(zx[..., D_FOURIER:], *scan_p, fin_c)
            yx = even_out(zx, st_x, *read_p)
            yc = None if last else even_out(zc, st_c, *read_p)
        else:
            lam_init = 0.8 - 0.6 * math.exp(-0.3 * i)
            lp = od_lambda[j].astype(F32)
            lam = jnp.exp(jnp.sum(lp[0] * lp[1])) - jnp.exp(jnp.sum(lp[2] * lp[3])) + lam_init
            qx, kx, vx = diff_qkv(hx, od_w_in[j], od_q_norm[j], od_k_norm[j], ang)
            qc, kc, vc = diff_qkv(hc, od_w_in[j], od_q_norm[j], od_k_norm[j], None)
            k_all = jnp.concatenate([kx, kc], axis=1)
            v_all = jnp.concatenate([vx, vc], axis=1)
            ox = blocked_diff_attention(qx, k_all, v_all, lam)
            yx = diff_out(ox, od_head_norm[j], lam_init, od_w_out[j])
            yc = None if last else diff_out(diff_attend(qc, kc, vc, lam), od_head_norm[j], lam_init, od_w_out[j])
        x = x + g1 * yx
        x = x + g2 * sq_relu_mlp(modulate(rmsnorm(x, norm2_g[i]), sh2, sc2), mlp_w1[i], mlp_w2[i])
        if not last:
            ctx = ctx + cg1 * yc
            ctx = ctx + cg2 * sq_relu_mlp(modulate(rmsnorm(ctx, norm2_g[i]), csh2, csc2), mlp_w1[i], mlp_w2[i])
    return x
```

```python
import contextlib
import numpy as np
import concourse.bass as bass
import concourse.mybir as mybir

F32 = mybir.dt.float32
BF16 = mybir.dt.bfloat16
I32 = mybir.dt.int32
AF = mybir.ActivationFunctionType
ALU = mybir.AluOpType

COMPUTE = ("pe", "act", "dve", "pool")
NSLOT = 8


class Buf:
    __slots__ = ("name", "lw", "rd")

    def __init__(self, name):
        self.name = name
        self.lw = None
        self.rd = []


class Op:
    __slots__ = ("eng", "fn", "reads", "writes", "deps", "signal", "sigval", "waits",
                 "dma", "slot", "gen", "fence")

    def __init__(self, eng, fn, reads, writes, dma):
        self.eng, self.fn, self.reads, self.writes, self.dma = eng, fn, reads, writes, dma
        self.deps = []
        self.signal = False
        self.sigval = 0
        self.waits = []
        self.slot = None
        self.gen = 0
        self.fence = False


class Prog:
    def __init__(self, nc):
        self.nc = nc
        self.ops = []
        self.bufs = {}

    def buf(self, name):
        b = self.bufs.get(name)
        if b is None:
            b = self.bufs[name] = Buf(name)
        return b

    def _bl(self, xs):
        out = []
        for x in xs:
            if isinstance(x, (list, tuple)):
                out.extend(self._bl(x))
            elif isinstance(x, str):
                out.append(self.buf(x))
            elif x is not None:
                out.append(x)
        return out

    def add(self, eng, fn, reads=(), writes=(), dma=False):
        op = Op(eng, fn, self._bl(reads), self._bl(writes), dma)
        self.ops.append(op)
        return op

    def pe(self, fn, r=(), w=()):
        return self.add("pe", fn, r, w)

    def act(self, fn, r=(), w=()):
        return self.add("act", fn, r, w)

    def dve(self, fn, r=(), w=()):
        return self.add("dve", fn, r, w)

    def pool(self, fn, r=(), w=()):
        return self.add("pool", fn, r, w)

    def dma(self, out, in_, r=(), w=(), q="sp", **kw):
        return self.add(q, lambda e: e.dma_start(out=out, in_=in_, **kw), r, w, dma=True)

    def fence(self):
        op = Op("sp", None, [], [], False)
        op.fence = True
        self.ops.append(op)

    def finalize(self, stack):
        nc = self.nc
        ops = self.ops
        last_on_eng = {}
        dmas_since = []
        fence_deps = []
        first_after = {}
        for op in ops:
            if op.fence:
                fence_deps = list(last_on_eng.values()) + list(dmas_since)
                dmas_since = []
                first_after = {}
                continue
            raw = set(b.lw for b in op.reads if b.lw is not None)
            deps = set(raw)
            for b in op.writes:
                if b.lw is not None:
                    deps.add(b.lw)
                deps.update(b.rd)
            fdeps = ()
            if op.eng not in first_after:
                first_after[op.eng] = True
                fdeps = fence_deps
            for b in op.reads:
                b.rd.append(op)
            for b in op.writes:
                b.lw = op
                b.rd = []
            keep = []
            for d in list(deps) + list(fdeps):
                if d is op or d in keep:
                    continue
                if d.eng == op.eng and not d.dma and not op.dma:
                    if op.eng == "pe" or d not in raw:
                        continue
                keep.append(d)
            op.deps = keep
            for d in keep:
                d.signal = True
            if op.dma:
                dmas_since.append(op)
            else:
                last_on_eng[op.eng] = op
        cnt = {e: 0 for e in COMPUTE}
        slot_next = {}
        slot_gen = {}
        self.sems = {}
        for e in COMPUTE:
            self.sems[e] = stack.enter_context(nc.semaphore("s_" + e))
        self.dma_sems = {}
        for op in ops:
            if op.fence:
                continue
            if op.dma:
                q = op.eng
                if q not in slot_next:
                    slot_next[q] = 0
                    for i in range(NSLOT):
                        self.dma_sems[(q, i)] = stack.enter_context(nc.semaphore("d_%s%d" % (q, i)))
                        slot_gen[(q, i)] = 0
                s = slot_next[q]
                slot_next[q] = (s + 1) % NSLOT
                slot_gen[(q, s)] += 1
                op.slot = (q, s)
                op.gen = slot_gen[(q, s)]
            elif op.signal:
                cnt[op.eng] += 1
                op.sigval = cnt[op.eng]
        self.slot_gen = slot_gen
        waited = {}
        for op in ops:
            if op.fence:
                continue
            w = waited.setdefault(op.eng, {})
            need = {}
            for d in op.deps:
                if d.dma:
                    key = ("d",) + d.slot
                    val = 16 * d.gen
                else:
                    key = ("c", d.eng)
                    val = d.sigval
                if val > need.get(key, 0):
                    need[key] = val
            if op.dma and op.gen > 1:
                key = ("d",) + op.slot
                val = 16 * (op.gen - 1)
                if val > need.get(key, 0):
                    need[key] = val
            for key, val in need.items():
                if w.get(key, 0) < val:
                    w[key] = val
                    op.waits.append((key, val))
        self.n_ops = len(ops)

    def _sem(self, key):
        if key[0] == "c":
            return self.sems[key[1]]
        return self.dma_sems[(key[1], key[2])]

    def emit(self):
        nc = self.nc
        by_eng = {}
        for op in self.ops:
            if not op.fence:
                by_eng.setdefault(op.eng, []).append(op)
        handles = {"pe": "tensor", "act": "scalar", "dve": "vector", "pool": "gpsimd", "sp": "sync"}
        with nc.Block() as block:
            for eng, name in handles.items():
                lst = by_eng.get(eng, [])

                def body(e, lst=lst, eng=eng):
                    for op in lst:
                        for key, val in op.waits:
                            e.wait_ge(self._sem(key), val)
                        ins = op.fn(e)
                        if op.dma:
                            ins.then_inc(self.dma_sems[op.slot], 16)
                        elif op.signal:
                            ins.then_inc(self.sems[eng], 1)
                    if eng == "sp":
                        for (q, s), g in self.slot_gen.items():
                            if g > 0:
                                e.wait_ge(self.dma_sems[(q, s)], 16 * g)

                getattr(block, name)(body)


class Arena:
    def __init__(self, nc, stack, nbytes, name="arena"):
        self.nbytes = nbytes
        self.t = stack.enter_context(nc.sbuf_tensor(name, [128, nbytes // 4], F32))
        self.off = 0
        self.mark = 0

    def reset(self, to=None):
        self.off = self.mark if to is None else to

    def alloc(self, shape, dtype):
        n = int(np.prod(shape))
        esz = mybir.dt.size(dtype)
        nb = (n * esz + 31) // 32 * 32
        assert self.off + nb <= self.nbytes, ("SBUF arena overflow", self.off, nb, self.nbytes)
        w0 = self.off // 4
        ap = self.t[:, w0:w0 + nb // 4]
        self.off += nb
        if dtype != F32:
            ap = ap.bitcast(dtype)
        ap = ap[:, 0:n]
        if len(shape) == 2:
            ap = ap.rearrange("p (a b) -> p a b", a=shape[0])
        elif len(shape) == 3:
            ap = ap.rearrange("p (a b c) -> p a b c", a=shape[0], b=shape[1])
        return ap

import math
import ml_dtypes
from concourse.bass_utils import run_bass_kernel_spmd

D = 1024
DFF = 4096
EPS = 1e-6
TWO_PI = 2.0 * math.pi
CW1 = 6.28125
CW2 = TWO_PI - CW1


def build(NB, SEQ, CTX, DEPTH, GRID_W=64):
    T = CTX + SEQ
    NT = T // 128
    NTC = CTX // 128
    NCX = SEQ // 512
    NR = NB + 1
    chunks = [(0, CTX, True)] + [(CTX + i * 512, 512, False) for i in range(NCX)]
    n_even = (DEPTH + 1) // 2
    n_odd = DEPTH // 2
    nc = bass.Bass("TRN2", target_bir_lowering=False)

    def din(name, shape, dt=F32):
        return nc.dram_tensor(name, list(shape), dt, kind="ExternalInput").ap()

    def dscr(name, shape, dt=F32):
        return nc.dram_tensor(name, list(shape), dt, kind="Internal").ap()

    xin = din("xin", [NB * T, D])
    condT = din("condT", [128, 8, NR])
    norm_g = din("norm_g", [DEPTH, 2, D])
    ada_w = din("ada_w", [DEPTH, D, 6 * D])
    ada_b = din("ada_b", [DEPTH, 6 * D])
    mlp_w1 = din("mlp_w1", [DEPTH, D, DFF])
    mlp_w2 = din("mlp_w2", [DEPTH, DFF, D])
    ident_d = din("ident", [128, 128])
    if n_even:
        ev_w_in = din("ev_w_in", [n_even, D, D])
        ev_w_out = din("ev_w_out", [n_even, D, D])
        ssm_sc = din("ssm_sc", [n_even, 3, 128, 32])
        ssm_bblk = din("ssm_bblk", [n_even, 2, 32, 128, 128])
        ssm_cblk = din("ssm_cblk", [n_even, 2, 32, 128, 128])
        ssm_d = din("ssm_d", [n_even, 128, 4])
        ssm_bglu = din("ssm_bglu", [n_even, 128, 4])
        ssm_wglu = din("ssm_wglu", [n_even, 512, 512])
        csc_d = din("csc", [128, 256])
        dftx = din("dftx", [NCX, SEQ // 128, 128, 2, 512], BF16)
        dftc = din("dftc", [1, NTC, 128, 2, CTX], BF16)
        iota_d = din("iota", [1, 256])
    if n_odd:
        od_w_in = din("od_w_in", [n_odd, D, 3 * D])
        od_w_out = din("od_w_out", [n_odd, D, D])
        od_qk = din("od_qk", [n_odd, 128, 2])
        od_lam = din("od_lam", [n_odd, 256])
        od_hn = din("od_hn", [n_odd, 128, 1])
        rope_d = din("rope", [2, 128, SEQ])
        rmat_d = din("rmat", [128, 128])
        bones_d = din("bones", [128, 128])
    yout = nc.dram_tensor("yout", [NB * SEQ, D], F32, kind="ExternalOutput").ap()
    xs = dscr("xs", [NB * T, D])
    mods = dscr("mods", [DEPTH, NR, 6 * D])
    vtok = dscr("vtok", [T, D], BF16)
    NCH = T // 256
    cstab = dscr("cstab", [8, NCH, 128, 4 * 2 * 256], BF16)

    st = contextlib.ExitStack()
    P = Prog(nc)
    ar = Arena(nc, st, 206000)
    psall = st.enter_context(nc.psum_tensor("psall", [128, 4096], F32))
    ps = [psall[:, i * 512:(i + 1) * 512] for i in range(8)]
    uid = [0]

    def U(s):
        uid[0] += 1
        return "%s#%d" % (s, uid[0])

    ident = ar.alloc([128], BF16)
    ones = ar.alloc([128], BF16)
    halfpi = ar.alloc([1], F32)
    P.dma(ident, ident_d, w=["ident"], q="pool")
    P.dve(lambda e: e.memset(ones, 1.0), w=["ones"])
    P.dve(lambda e: e.memset(halfpi, math.pi / 2), w=["halfpi"])
    if n_even:
        csc = ar.alloc([256], BF16)
        P.dma(csc, csc_d, w=["csc"], q="pool")
        iot = ar.alloc([256], F32)
        P.dma(iot, iota_d[0, :].partition_broadcast(128), w=["iot"])
    if n_odd:
        rmat = ar.alloc([128], BF16)
        bones = ar.alloc([128], BF16)
        P.dma(rmat, rmat_d, w=["rmat"], q="pool")
        P.dma(bones, bones_d, w=["bones"], q="pool")
    ar.mark = ar.off

    def stage_adaln():
        ar.reset()
        ct = ar.alloc([8, NR], F32)
        cs_ = ar.alloc([8, NR], F32)
        P.dma(ct, condT, w=["ct"])
        P.act(lambda e: e.activation(out=cs_, in_=ct, func=AF.Silu), r=["ct"], w=["cs"])
        wt = [ar.alloc([8, 512], F32) for _ in range(2)]
        bt = [ar.alloc([512], F32) for _ in range(2)]
        ot = [ar.alloc([512], F32) for _ in range(2)]
        it = 0
        for l in range(DEPTH):
            for n in range(12):
                s = it % 2
                it += 1
                P.dma(wt[s], ada_w[l, :, n * 512:(n + 1) * 512].rearrange("(k p) n -> p k n", p=128),
                      w=["aw%d" % s])
                P.dma(bt[s][0:NR, :], ada_b[l, n * 512:(n + 1) * 512].partition_broadcast(NR), w=["ab%d" % s])
                for k in range(8):
                    P.pe(lambda e, s=s, k=k: e.matmul(ps[s][0:NR, :], lhsT=cs_[:, k, :], rhs=wt[s][:, k, :],
                                                      start=(k == 0), stop=(k == 7)),
                         r=["cs", "aw%d" % s], w=["ps%d" % s])
                P.dve(lambda e, s=s: e.tensor_tensor(out=ot[s][0:NR, :], in0=ps[s][0:NR, :], in1=bt[s][0:NR, :],
                                                     op=ALU.add), r=["ps%d" % s, "ab%d" % s], w=["ao%d" % s])
                P.dma(mods[l, :, n * 512:(n + 1) * 512], ot[s][0:NR, :], r=["ao%d" % s], w=["mods"])
        P.fence()

    ch256 = [(s0, 256, s0 < CTX) for s0 in range(0, T, 256)]
    PN = lambda b: "ps%d" % b
    XN = lambda r0: "xs%d" % r0

    def load_mod_tiles(l, row, which, pref, ngb, need):
        base = which * 3 * D
        out = {}
        if "G" in need:
            G = ar.alloc([D], F32)
            P.dma(ngb, norm_g[l, which, :].partition_broadcast(128), w=["ngb"])
            P.dma(G, mods[l, row, base + D:base + 2 * D].partition_broadcast(128), r=["mods"], w=[pref + "G"])
            P.dve(lambda e: e.scalar_tensor_tensor(out=G, in0=G, scalar=1.0, in1=ngb, op0=ALU.add, op1=ALU.mult),
                  r=[pref + "G", "ngb"], w=[pref + "G"])
            out["G"] = G
        if "S" in need:
            S = ar.alloc([D], F32)
            P.dma(S, mods[l, row, base:base + D].partition_broadcast(128), r=["mods"], w=[pref + "S"])
            out["S"] = S
        if "g" in need:
            Gt = ar.alloc([D], F32)
            P.dma(Gt, mods[l, row, base + 2 * D:base + 3 * D].partition_broadcast(128), r=["mods"], w=[pref + "Gt"])
            out["Gt"] = Gt
        return out

    class NormBufs:
        def __init__(self, with_xt=True):
            self.xt = [ar.alloc([D], F32) for _ in range(2)] if with_xt else None
            self.junk = ar.alloc([D], BF16)
            self.t1 = ar.alloc([D], F32)
            self.hb = [ar.alloc([D], BF16) for _ in range(2)]
            self.st = [ar.alloc([4], F32) for _ in range(2)]
            self.ng = ar.alloc([D], F32)
            self.i = 0

    def norm_tile(nb, src_rows, srcname, G, S, gname, hT_dst, hname, keep_x=None):
        s = nb.i % 2
        nb.i += 1
        xt = nb.xt[s] if keep_x is None else keep_x[0]
        xn = "xt%d" % s if keep_x is None else keep_x[1]
        stt = nb.st[s]
        sn = "nst%d" % s
        P.dma(xt, src_rows, r=[srcname], w=[xn])
        P.act(lambda e: e.activation(out=nb.junk, in_=xt, func=AF.Square, accum_out=stt[:, 0:1]),
              r=[xn], w=["njunk", sn])
        P.dve(lambda e: e.tensor_scalar(out=stt[:, 1:2], in0=stt[:, 0:1], scalar1=1.0 / D, scalar2=EPS,
                                        op0=ALU.mult, op1=ALU.add), r=[sn], w=[sn])
        P.act(lambda e: e.activation(out=stt[:, 2:3], in_=stt[:, 1:2], func=AF.Sqrt), r=[sn], w=[sn])
        P.dve(lambda e: e.reciprocal(out=stt[:, 3:4], in_=stt[:, 2:3]), r=[sn], w=[sn])
        P.dve(lambda e: e.scalar_tensor_tensor(out=nb.t1, in0=xt, scalar=stt[:, 3:4], in1=G,
                                               op0=ALU.mult, op1=ALU.mult), r=[xn, sn, gname + "G"], w=["nt1"])
        hb = nb.hb[s]
        P.pool(lambda e: e.tensor_tensor(out=hb, in0=nb.t1, in1=S, op=ALU.add),
               r=["nt1", gname + "S"], w=["nhb%d" % s])
        pb = 6 + s
        pst = ps[pb][:, :].bitcast(BF16).rearrange("p (k t) -> p k t", k=8)
        for k in range(8):
            P.pe(lambda e, k=k: e.transpose(out=pst[:, k, :], in_=hb[:, k * 128:(k + 1) * 128], identity=ident),
                 r=["nhb%d" % s, "ident"], w=[PN(pb)])
        P.act(lambda e: e.activation(out=hT_dst, in_=pst, func=AF.Copy), r=[PN(pb)], w=[hname])
        return s

    def load_w(dst, src, name, kt):
        N = src.shape[1]
        v = src.rearrange("(k p) n -> p k n", p=128)
        cb = min(N, 2048)
        for c0 in range(0, N, cb):
            c1 = min(N, c0 + cb)
            P.dma(dst[:, :, c0:c1], v[:, :, c0:c1], w=[name], q="pool")

    def out_proj(l, b, last, yT, yname, wsrc):
        mk = ar.off
        wout = ar.alloc([8, D], BF16)
        load_w(wout, wsrc, "wout", 8)
        nb = NormBufs()
        Gtx = load_mod_tiles(l, b, 0, "x", nb.ng, "g")["Gt"]
        Gtc = None if last else load_mod_tiles(l, NB, 0, "c", nb.ng, "g")["Gt"]
        for tt in range(NT):
            isc = tt < NTC
            if last and isc:
                continue
            Gt = Gtc if isc else Gtx
            gn = "c" if isc else "x"
            s = nb.i % 2
            nb.i += 1
            xt = nb.xt[s]
            xn = "xt%d" % s
            r0 = b * T + tt * 128
            P.dma(xt, xs[r0:r0 + 128, :], r=[XN(r0)], w=[xn])
            for hf in range(2):
                pb = 4 + hf
                for k in range(8):
                    P.pe(lambda e, k=k, hf=hf, pb=pb, tt=tt: e.matmul(
                        ps[pb][:, :], lhsT=yT[:, k, tt * 128:(tt + 1) * 128], rhs=wout[:, k, hf * 512:(hf + 1) * 512],
                        start=(k == 0), stop=(k == 7)), r=[yname, "wout"], w=[PN(pb)])
                P.dve(lambda e, hf=hf, pb=pb, Gt=Gt: e.tensor_tensor(
                    out=nb.t1[:, hf * 512:(hf + 1) * 512], in0=ps[pb][:, :], in1=Gt[:, hf * 512:(hf + 1) * 512],
                    op=ALU.mult), r=[PN(pb), gn + "Gt"], w=["ot1%d" % hf])
                P.pool(lambda e, hf=hf, xt=xt: e.tensor_tensor(
                    out=xt[:, hf * 512:(hf + 1) * 512], in0=xt[:, hf * 512:(hf + 1) * 512],
                    in1=nb.t1[:, hf * 512:(hf + 1) * 512], op=ALU.add), r=["ot1%d" % hf, xn], w=[xn])
            P.dma(xs[r0:r0 + 128, :], xt, r=[xn], w=[XN(r0)])
        P.fence()
        ar.reset(mk)

    def ssm_prep(j, w):
        rho = ar.alloc([32], F32)
        th = ar.alloc([32], F32)
        nth = ar.alloc([32], F32)
        BT = ar.alloc([2, 32, 128], BF16)
        CT = ar.alloc([2, 32, 128], BF16)
        mk = ar.off
        sc3 = ar.alloc([3, 32], F32)
        P.dma(sc3, ssm_sc[j].rearrange("a p t -> p a t"), w=["sc3"])
        tmp = ar.alloc([16, 32], F32)
        ti = ar.alloc([32], I32)
        cre = ar.alloc([32], F32)
        cim = ar.alloc([32], F32)
        ncim = ar.alloc([32], F32)
        a_re, a_im = sc3[:, 0, :], sc3[:, 1, :]
        t = lambda i: tmp[:, i, :]
        R, W = ["sc3", "stmp"], ["stmp"]
        P.act(lambda e: e.activation(out=t(0), in_=sc3[:, 2, :], func=AF.Exp), r=R, w=W)
        P.dve(lambda e: e.tensor_tensor(out=t(1), in0=a_re, in1=t(0), op=ALU.mult), r=R, w=W)
        P.dve(lambda e: e.tensor_tensor(out=th, in0=a_im, in1=t(0), op=ALU.mult), r=R, w=["sth"])
        P.dve(lambda e: e.tensor_scalar(out=nth, in0=th, scalar1=-1.0, scalar2=None, op0=ALU.mult),
              r=["sth"], w=["snth"])
        P.act(lambda e: e.activation(out=rho, in_=t(1), func=AF.Exp), r=R, w=["srho"])
        P.dve(lambda e: e.tensor_scalar(out=ti, in0=th, scalar1=1.0 / TWO_PI, scalar2=None, op0=ALU.mult),
              r=["sth"], w=["sti"])
        P.dve(lambda e: e.scalar_tensor_tensor(out=t(2), in0=ti, scalar=-CW1, in1=th, op0=ALU.mult, op1=ALU.add),
              r=["sti", "sth"], w=W)
        P.dve(lambda e: e.scalar_tensor_tensor(out=t(2), in0=ti, scalar=-CW2, in1=t(2), op0=ALU.mult, op1=ALU.add),
              r=["sti"] + R, w=W)
        P.dve(lambda e: e.tensor_scalar(out=t(2), in0=t(2), scalar1=math.pi, scalar2=-math.pi, op0=ALU.min,
                                        op1=ALU.max), r=R, w=W)
        P.act(lambda e: e.activation(out=t(3), in_=t(2), func=AF.Sin), r=R, w=W)
        P.act(lambda e: e.activation(out=t(4), in_=t(2), func=AF.Abs), r=R, w=W)
        P.act(lambda e: e.activation(out=t(5), in_=t(4), func=AF.Sin, scale=-1.0, bias=halfpi[:, 0:1]),
              r=R + ["halfpi"], w=W)
        P.dve(lambda e: e.tensor_tensor(out=t(6), in0=rho, in1=t(5), op=ALU.mult), r=R + ["srho"], w=W)
        P.dve(lambda e: e.tensor_tensor(out=t(7), in0=rho, in1=t(3), op=ALU.mult), r=R + ["srho"], w=W)
        P.dve(lambda e: e.tensor_scalar(out=t(6), in0=t(6), scalar1=-1.0, scalar2=None, op0=ALU.add), r=R, w=W)
        P.dve(lambda e: e.tensor_tensor(out=t(8), in0=a_re, in1=a_re, op=ALU.mult), r=R, w=W)
        P.dve(lambda e: e.tensor_tensor(out=t(9), in0=a_im, in1=a_im, op=ALU.mult), r=R, w=W)
        P.dve(lambda e: e.tensor_tensor(out=t(8), in0=t(8), in1=t(9), op=ALU.add), r=R, w=W)
        P.dve(lambda e: e.reciprocal(out=t(8), in_=t(8)), r=R, w=W)
        P.dve(lambda e: e.tensor_tensor(out=t(9), in0=t(6), in1=a_re, op=ALU.mult), r=R, w=W)
        P.dve(lambda e: e.tensor_tensor(out=t(10), in0=t(7), in1=a_im, op=ALU.mult), r=R, w=W)
        P.dve(lambda e: e.tensor_tensor(out=t(9), in0=t(9), in1=t(10), op=ALU.add), r=R, w=W)
        P.dve(lambda e: e.tensor_tensor(out=cre, in0=t(9), in1=t(8), op=ALU.mult), r=R, w=["scre"])
        P.dve(lambda e: e.tensor_tensor(out=t(11), in0=t(7), in1=a_re, op=ALU.mult), r=R, w=W)
        P.dve(lambda e: e.tensor_tensor(out=t(12), in0=t(6), in1=a_im, op=ALU.mult), r=R, w=W)
        P.dve(lambda e: e.tensor_tensor(out=t(11), in0=t(11), in1=t(12), op=ALU.subtract), r=R, w=W)
        P.dve(lambda e: e.tensor_tensor(out=cim, in0=t(11), in1=t(8), op=ALU.mult), r=R, w=["scim"])
        P.dve(lambda e: e.tensor_scalar(out=ncim, in0=cim, scalar1=-1.0, scalar2=None, op0=ALU.mult),
              r=["scim"], w=["sncim"])
        for c in range(2):
            for r0 in range(0, 32, 8):
                P.dma(CT[:, c, r0:r0 + 8, :], ssm_cblk[j, c, r0:r0 + 8].rearrange("t p n -> p t n"),
                      w=["sCT"], q="pool")
        braw = [ar.alloc([2, 128], F32) for _ in range(2)]
        bb = [ar.alloc([2, 128], BF16) for _ in range(2)]
        bt1 = [ar.alloc([128], F32) for _ in range(2)]
        for r in range(32):
            s = r % 2
            P.dma(braw[s], ssm_bblk[j, :, r].rearrange("c p n -> p c n"), w=["braw%d" % s])
            P.dve(lambda e, s=s, r=r: e.tensor_scalar(out=bt1[s], in0=braw[s][:, 1, :], scalar1=ncim[:, r:r + 1],
                                                      scalar2=None, op0=ALU.mult),
                  r=["braw%d" % s, "sncim"], w=["bt1%d" % s])
            P.dve(lambda e, s=s, r=r: e.scalar_tensor_tensor(out=bb[s][:, 0, :], in0=braw[s][:, 0, :],
                                                             scalar=cre[:, r:r + 1], in1=bt1[s], op0=ALU.mult,
                                                             op1=ALU.add),
                  r=["braw%d" % s, "scre", "bt1%d" % s], w=["bb%d" % s])
            P.dve(lambda e, s=s, r=r: e.tensor_scalar(out=bt1[s], in0=braw[s][:, 0, :], scalar1=cim[:, r:r + 1],
                                                      scalar2=None, op0=ALU.mult),
                  r=["braw%d" % s, "scim", "bb%d" % s], w=["bt1%d" % s])
            P.dve(lambda e, s=s, r=r: e.scalar_tensor_tensor(out=bb[s][:, 1, :], in0=braw[s][:, 1, :],
                                                             scalar=cre[:, r:r + 1], in1=bt1[s], op0=ALU.mult,
                                                             op1=ALU.add),
                  r=["braw%d" % s, "scre", "bt1%d" % s], w=["bb%d" % s])
            pst = ps[s][:, 0:128].bitcast(BF16).rearrange("p (c n) -> p c n", c=2)
            for c in range(2):
                P.pe(lambda e, s=s, c=c, pst=pst: e.transpose(out=pst[:, c, :], in_=bb[s][:, c, :], identity=ident),
                     r=["bb%d" % s, "ident"], w=[PN(s)])
            P.act(lambda e, r=r, pst=pst: e.activation(out=BT[:, :, r, :], in_=pst, func=AF.Copy),
                  r=[PN(s)], w=["sBT"])
        stg = [ar.alloc([2, 256], BF16) for _ in range(2)]
        tw = [dict(ph=ar.alloc([256], F32), ki=ar.alloc([256], I32), ab=ar.alloc([256], F32)) for _ in range(2)]
        tht = ar.alloc([2, NCH, 32], F32)
        for d in range(2):
            for oi, (s0, _, isc) in enumerate(ch256):
                tau0 = float(s0) if d == 0 else (float(CTX - 1 - s0) if isc else float(CTX + T - 1 - s0))
                P.dve(lambda e, d=d, oi=oi, tau0=tau0: e.tensor_scalar(
                    out=tht[:, d, oi, :], in0=th, scalar1=tau0, scalar2=None, op0=ALU.mult),
                    r=["sth"], w=["tht"])
        gi = 0
        for r in range(32):
            d = r // 16
            thd = th if d == 0 else nth
            for cidx in range(NCH):
                s_ = gi % 2
                gi += 1
                ph, ki, ab = tw[s_]["ph"], tw[s_]["ki"], tw[s_]["ab"]
                tn = "tw%d" % s_
                P.dve(lambda e, r=r, d=d, cidx=cidx, ph=ph, thd=thd: e.tensor_scalar(
                    out=ph, in0=iot, scalar1=thd[:, r:r + 1], scalar2=tht[:, d, cidx, r:r + 1],
                    op0=ALU.mult, op1=ALU.add), r=["iot", "sth", "snth", "tht"], w=[tn + "ph"])
                P.dve(lambda e, ph=ph, ki=ki: e.tensor_scalar(out=ki, in0=ph, scalar1=1.0 / TWO_PI, scalar2=None,
                                                              op0=ALU.mult), r=[tn + "ph"], w=[tn + "ki"])
                P.dve(lambda e, ph=ph, ki=ki: e.scalar_tensor_tensor(out=ph, in0=ki, scalar=-CW1, in1=ph,
                                                                     op0=ALU.mult, op1=ALU.add),
                      r=[tn + "ph", tn + "ki"], w=[tn + "ph"])
                P.dve(lambda e, ph=ph, ki=ki: e.scalar_tensor_tensor(out=ph, in0=ki, scalar=-CW2, in1=ph,
                                                                     op0=ALU.mult, op1=ALU.add),
                      r=[tn + "ph", tn + "ki"], w=[tn + "ph"])
                P.pool(lambda e, ph=ph: e.tensor_scalar(out=ph, in0=ph, scalar1=math.pi, scalar2=-math.pi,
                                                        op0=ALU.min, op1=ALU.max), r=[tn + "ph"], w=[tn + "ph"])
                P.act(lambda e, ph=ph, s_=s_: e.activation(out=stg[s_][:, 1, :], in_=ph, func=AF.Sin),
                      r=[tn + "ph"], w=["stg%d" % s_])
                P.act(lambda e, ph=ph, ab=ab: e.activation(out=ab, in_=ph, func=AF.Abs), r=[tn + "ph"], w=[tn + "ab"])
                P.act(lambda e, ab=ab, s_=s_: e.activation(out=stg[s_][:, 0, :], in_=ab, func=AF.Sin, scale=-1.0,
                                                           bias=halfpi[:, 0:1]),
                      r=[tn + "ab", "halfpi"], w=["stg%d" % s_])
                dstv = cstab[r // 4, cidx].rearrange("p (a c t) -> p a c t", a=4, c=2)[:, r % 4, :, :]
                P.dma(dstv, stg[s_], r=["stg%d" % s_], w=["cstab"])
        P.fence()
        ar.reset(mk)
        w.update(rho=rho, th=th, nth=nth, BT=BT, CT=CT)

    def even_mixer(l, j, b, last, w):
        mk0 = ar.off
        zbuf = ar.alloc([8 * T], BF16)
        zT = zbuf.rearrange("p (a b) -> p a b", a=8)
        yT = ar.alloc([8, T], BF16)
        mark2 = ar.off
        win = ar.alloc([8, D], BF16)
        load_w(win, ev_w_in[j], "win", 8)
        nb = NormBufs()
        mx = load_mod_tiles(l, b, 0, "x", nb.ng, "GS")
        mc = load_mod_tiles(l, NB, 0, "c", nb.ng, "GS")
        hTc = [ar.alloc([8, 512], BF16) for _ in range(2)]
        src = xin if l == 0 else xs
        for ci, (s0, L, isc) in enumerate(chunks):
            hs = ci % 2
            mm_ = mc if isc else mx
            for ti in range(L // 128):
                r0 = b * T + s0 + ti * 128
                s = norm_tile(nb, src[r0:r0 + 128, :], XN(r0), mm_["G"], mm_["S"], "c" if isc else "x",
                              hTc[hs][:, :, ti * 128:(ti + 1) * 128], "hTc%d" % hs)
                if l == 0:
                    P.dma(xs[r0:r0 + 128, :], nb.xt[s], r=["xt%d" % s], w=[XN(r0)])
            for m in range(8):
                pb = m % 4
                for k in range(8):
                    P.pe(lambda e, m=m, k=k, pb=pb, hs=hs, L=L: e.matmul(
                        ps[pb][:, 0:L], lhsT=win[:, k, m * 128:(m + 1) * 128], rhs=hTc[hs][:, k, 0:L],
                        start=(k == 0), stop=(k == 7)), r=["win", "hTc%d" % hs], w=[PN(pb)])
                if m % 2 == 0:
                    P.act(lambda e, m=m, pb=pb, s0=s0, L=L: e.activation(out=zT[:, m, s0:s0 + L], in_=ps[pb][:, 0:L],
                                                                         func=AF.Copy),
                          r=[PN(pb)], w=["zT%d.%d" % (m, ci)])
                else:
                    P.dve(lambda e, m=m, pb=pb, s0=s0, L=L: e.tensor_copy(out=zT[:, m, s0:s0 + L], in_=ps[pb][:, 0:L]),
                          r=[PN(pb)], w=["zT%d.%d" % (m, ci)])
        P.fence()
        ar.reset(mark2)
        ABt = ar.alloc([NT, 4, 256], BF16)
        dring = [ar.alloc([2, 512], BF16) for _ in range(6)]
        zall = ["zT%d.%d" % (m, ci) for m in range(8) for ci in range(len(chunks))]
        for tt in range(NT):
            for gp in range(2):
                pb = 4 + gp
                pv = ps[pb][:, :].rearrange("p (g n) -> p g n", g=2)
                for gi in range(2):
                    g = gp * 2 + gi
                    P.pe(lambda e, g=g, gi=gi, tt=tt, pv=pv: e.matmul(
                        pv[:, gi, :], lhsT=zT[:, g, tt * 128:(tt + 1) * 128], rhs=csc, start=True, stop=True),
                        r=zall + ["csc"], w=[PN(pb)])
                if gp == 0:
                    P.act(lambda e, tt=tt, gp=gp, pv=pv: e.activation(out=ABt[:, tt, gp * 2:gp * 2 + 2, :], in_=pv,
                                                                      func=AF.Copy),
                          r=[PN(pb)], w=["ABt"])
                else:
                    P.dve(lambda e, tt=tt, gp=gp, pv=pv: e.tensor_copy(out=ABt[:, tt, gp * 2:gp * 2 + 2, :], in_=pv),
                          r=[PN(pb)], w=["ABt"])
        di = 0
        for (s0, L, isc) in chunks:
            if last and isc:
                continue
            tts = list(range(NTC)) if isc else list(range(NTC, NT))
            ntt = len(tts)
            for ii, tt in enumerate(tts):
                sl = di % 6
                di += 1
                srcd = dftc[0, ii] if isc else dftx[(s0 - CTX) // 512, ii]
                P.dma(dring[sl][:, :, 0:L], srcd, w=["dr%d" % sl])
                for g in range(4):
                    for c in range(2):
                        P.pe(lambda e, g=g, c=c, tt=tt, sl=sl, L=L, ii=ii, ntt=ntt: e.matmul(
                            ps[g][:, 0:L], lhsT=ABt[:, tt, g, c * 128:(c + 1) * 128], rhs=dring[sl][:, c, 0:L],
                            start=(ii == 0 and c == 0), stop=(ii == ntt - 1 and c == 1)),
                            r=["ABt", "dr%d" % sl], w=[PN(g)])
            for g in range(4):
                if g % 2 == 0:
                    P.act(lambda e, g=g, s0=s0, L=L: e.activation(out=yT[:, g, s0:s0 + L], in_=ps[g][:, 0:L],
                                                                  func=AF.Copy), r=[PN(g)], w=["yT"])
                else:
                    P.dve(lambda e, g=g, s0=s0, L=L: e.tensor_copy(out=yT[:, g, s0:s0 + L], in_=ps[g][:, 0:L]),
                          r=[PN(g)], w=["yT"])
        P.fence()
        ar.reset(mark2)
        yacc = ar.alloc([4, T], F32)
        wglu = ar.alloc([4, 512], BF16)
        load_w(wglu, ssm_wglu[j], "wglu", 4)
        dsk = ar.alloc([4], F32)
        bglu = ar.alloc([4], F32)
        P.dma(dsk, ssm_d[j], w=["dsk"])
        P.dma(bglu, ssm_bglu[j], w=["bglu"])
        L = 256
        NW = 2
        wk = []
        for i in range(NW):
            wk.append(dict(tab=ar.alloc([4, 2, L], BF16), pb=[ar.alloc([4, L], BF16) for _ in range(2)],
                           m=[ar.alloc([4, L], BF16) for _ in range(4)],
                           hr=ar.alloc([4, L], BF16), hi=ar.alloc([4, L], BF16)))
        gst = [ar.alloc([2, 4, L], BF16) for _ in range(2)]
        BT, CT, rho = w["BT"], w["CT"], w["rho"]
        P1 = psall[:, 0:1024].rearrange("p (a t) -> p a t", a=4)
        P2 = psall[:, 1024:2048].rearrange("p (a t) -> p a t", a=4)
        PN1, PN2 = [PN(0), PN(1)], [PN(2), PN(3)]
        nctx = CTX // 256
        groups = []
        for d in range(2):
            idx = list(range(NCH))
            order = idx if d == 0 else idx[:nctx][::-1] + idx[nctx:][::-1]
            for kt in range(4):
                for oi, cidx in enumerate(order):
                    groups.append((d, kt, oi, cidx))

        def ph_a(gi):
            d, kt, oi, cidx = groups[gi]
            s0 = ch256[cidx][0]
            k_ = wk[gi % NW]
            wn = "wk%d" % (gi % NW)
            tab, pb, mm = k_["tab"], k_["pb"], k_["m"]
            cs, sn = tab[:, :, 0, :], tab[:, :, 1, :]
            P.dma(tab.rearrange("p a c t -> p (a c t)"), cstab[d * 4 + kt, cidx], r=["cstab"], w=[wn + "tab"])
            for rr in range(4):
                r = d * 16 + kt * 4 + rr
                P.pe(lambda e, r=r, rr=rr, kt=kt, s0=s0: e.matmul(
                    P1[:, rr, :], lhsT=BT[:, 0, r, :], rhs=zT[:, 4 + kt, s0:s0 + L], start=True, stop=True),
                    r=["sBT"] + zall, w=[PN(rr // 2)])
                P.pe(lambda e, r=r, rr=rr, kt=kt, s0=s0: e.matmul(
                    P2[:, rr, :], lhsT=BT[:, 1, r, :], rhs=zT[:, 4 + kt, s0:s0 + L], start=True, stop=True),
                    r=["sBT"] + zall, w=[PN(2 + rr // 2)])
            P.act(lambda e, pb=pb: e.activation(out=pb[0], in_=P1, func=AF.Copy), r=PN1, w=[wn + "pb0"])
            P.act(lambda e, pb=pb: e.activation(out=pb[1], in_=P2, func=AF.Copy), r=PN2, w=[wn + "pb1"])

        def ph_a2(gi):
            k_ = wk[gi % NW]
            wn = "wk%d" % (gi % NW)
            tab, pb, mm = k_["tab"], k_["pb"], k_["m"]
            cs, sn = tab[:, :, 0, :], tab[:, :, 1, :]
            P.dve(lambda e, cs=cs, mm=mm, pb=pb: e.tensor_tensor(out=mm[0], in0=pb[0], in1=cs, op=ALU.mult),
                  r=[wn + "pb0", wn + "tab"], w=[wn + "m0"])
            P.dve(lambda e, sn=sn, mm=mm, pb=pb: e.tensor_tensor(out=mm[1], in0=pb[1], in1=sn, op=ALU.mult),
                  r=[wn + "pb1", wn + "tab"], w=[wn + "m1"])
            P.dve(lambda e, mm=mm: e.tensor_tensor(out=mm[0], in0=mm[0], in1=mm[1], op=ALU.add),
                  r=[wn + "m0", wn + "m1"], w=[wn + "m0"])
            P.dve(lambda e, cs=cs, mm=mm, pb=pb: e.tensor_tensor(out=mm[2], in0=pb[1], in1=cs, op=ALU.mult),
                  r=[wn + "pb1", wn + "tab"], w=[wn + "m2"])
            P.dve(lambda e, sn=sn, mm=mm, pb=pb: e.tensor_tensor(out=mm[3], in0=pb[0], in1=sn, op=ALU.mult),
                  r=[wn + "pb0", wn + "tab"], w=[wn + "m3"])
            P.dve(lambda e, mm=mm: e.tensor_tensor(out=mm[2], in0=mm[2], in1=mm[3], op=ALU.subtract),
                  r=[wn + "m2", wn + "m3"], w=[wn + "m2"])

        def ph_b(gi):
            d, kt, oi, cidx = groups[gi]
            k_ = wk[gi % NW]
            wn = "wk%d" % (gi % NW)
            mm = k_["m"]
            g = gst[gi % 2]
            gn = "gst%d" % (gi % 2)
            rv = (lambda a: a) if d == 0 else (lambda a: a[:, ::-1])
            for rr in range(4):
                r = d * 16 + kt * 4 + rr
                for c in range(2):
                    if oi == 0:
                        init = 0.0
                        rdi = []
                    else:
                        pg = gst[(gi - 1) % 2]
                        init = (pg[:, c, rr, L - 1:L] if d == 0 else pg[:, c, rr, 0:1])
                        rdi = ["gst%d" % ((gi - 1) % 2)]
                    P.dve(lambda e, c=c, rr=rr, g=g, mm=mm, init=init, r=r, rv=rv: e.tensor_tensor_scan(
                        out=rv(g[:, c, rr, :]), data0=rho[:, r:r + 1].to_broadcast([128, L]),
                        data1=rv(mm[2 * c][:, rr, :]), initial=init, op0=ALU.mult, op1=ALU.add),
                        r=["srho", wn + "m%d" % (2 * c)] + rdi, w=[gn])

        def ph_c(gi):
            d, kt, oi, cidx = groups[gi]
            s0 = ch256[cidx][0]
            k_ = wk[gi % NW]
            wn = "wk%d" % (gi % NW)
            tab, mm, hr, hi = k_["tab"], k_["m"], k_["hr"], k_["hi"]
            cs, sn = tab[:, :, 0, :], tab[:, :, 1, :]
            g = gst[gi % 2]
            gn = "gst%d" % (gi % 2)
            ypb = 4 + (gi % 2)
            P.dve(lambda e, g=g, cs=cs, mm=mm: e.tensor_tensor(out=mm[1], in0=g[:, 0], in1=cs, op=ALU.mult),
                  r=[gn, wn + "tab", wn + "m1"], w=[wn + "m1"])
            P.dve(lambda e, g=g, sn=sn, mm=mm: e.tensor_tensor(out=mm[3], in0=g[:, 1], in1=sn, op=ALU.mult),
                  r=[gn, wn + "tab", wn + "m3"], w=[wn + "m3"])
            P.dve(lambda e, mm=mm, hr=hr: e.tensor_tensor(out=hr, in0=mm[1], in1=mm[3], op=ALU.subtract),
                  r=[wn + "m1", wn + "m3"], w=[wn + "hr"])
            P.dve(lambda e, g=g, sn=sn, mm=mm: e.tensor_tensor(out=mm[0], in0=g[:, 0], in1=sn, op=ALU.mult),
                  r=[gn, wn + "tab", wn + "m0"], w=[wn + "m0"])
            P.dve(lambda e, g=g, cs=cs, mm=mm: e.tensor_tensor(out=mm[2], in0=g[:, 1], in1=cs, op=ALU.mult),
                  r=[gn, wn + "tab", wn + "m2"], w=[wn + "m2"])
            P.dve(lambda e, mm=mm, hi=hi: e.scalar_tensor_tensor(
                out=hi, in0=mm[0], scalar=-1.0, in1=mm[2], op0=ALU.mult, op1=ALU.subtract),
                r=[wn + "m0", wn + "m2"], w=[wn + "hi"])
            for rr in range(4):
                r = d * 16 + kt * 4 + rr
                P.pe(lambda e, r=r, hr=hr, rr=rr, ypb=ypb: e.matmul(
                    ps[ypb][:, 0:L], lhsT=CT[:, 0, r, :], rhs=hr[:, rr, :], start=(rr == 0), stop=False),
                    r=["sCT", wn + "hr"], w=[PN(ypb)])
                P.pe(lambda e, r=r, hi=hi, rr=rr, ypb=ypb: e.matmul(
                    ps[ypb][:, 0:L], lhsT=CT[:, 1, r, :], rhs=hi[:, rr, :], start=False, stop=(rr == 3)),
                    r=["sCT", wn + "hi"], w=[PN(ypb)])
            yn = "yacc%d" % kt
            if d == 0:
                P.act(lambda e, kt=kt, s0=s0, ypb=ypb: e.activation(
                    out=yacc[:, kt, s0:s0 + L], in_=ps[ypb][:, 0:L], func=AF.Copy),
                    r=[PN(ypb)], w=[yn])
            else:
                P.dve(lambda e, kt=kt, s0=s0, ypb=ypb: e.tensor_tensor(
                    out=yacc[:, kt, s0:s0 + L], in0=ps[ypb][:, 0:L], in1=yacc[:, kt, s0:s0 + L], op=ALU.add),
                    r=[PN(ypb), yn], w=[yn])

        ph_a(0)
        ph_a2(0)
        for gi in range(len(groups)):
            if gi + 1 < len(groups):
                ph_a(gi + 1)
            ph_b(gi)
            if gi + 1 < len(groups):
                ph_a2(gi + 1)
            ph_c(gi)
        yb = yT[:, 4:8, :]
        for kt in range(4):
            P.dve(lambda e, kt=kt: e.scalar_tensor_tensor(
                out=yacc[:, kt, :], in0=zT[:, 4 + kt, :], scalar=dsk[:, kt:kt + 1], in1=yacc[:, kt, :],
                op0=ALU.mult, op1=ALU.add), r=zall + ["dsk", "yacc%d" % kt], w=["yacc%d" % kt])
            P.act(lambda e, kt=kt: e.activation(out=yb[:, kt, :], in_=yacc[:, kt, :], func=AF.Gelu_apprx_tanh),
                  r=["yacc%d" % kt], w=["yT"])
        sg = [ar.alloc([512], BF16) for _ in range(4)]
        for (s0, Lc, isc) in chunks:
            if last and isc:
                continue
            for m in range(4):
                pb = m
                for k in range(4):
                    P.pe(lambda e, m=m, k=k, pb=pb, s0=s0, Lc=Lc: e.matmul(
                        ps[pb][:, 0:Lc], lhsT=wglu[:, k, m * 128:(m + 1) * 128], rhs=yb[:, k, s0:s0 + Lc],
                        start=(k == 0), stop=(k == 3)), r=["wglu", "yT"], w=[PN(pb)])
                P.act(lambda e, m=m, pb=pb, Lc=Lc: e.activation(out=sg[m][:, 0:Lc], in_=ps[pb][:, 0:Lc],
                                                                func=AF.Sigmoid, bias=bglu[:, m:m + 1]),
                      r=[PN(pb), "bglu"], w=["sg%d" % m])
            for m in range(4):
                P.dve(lambda e, m=m, s0=s0, Lc=Lc: e.tensor_tensor(
                    out=yb[:, m, s0:s0 + Lc], in0=yb[:, m, s0:s0 + Lc], in1=sg[m][:, 0:Lc], op=ALU.mult),
                    r=["yT"] + ["sg%d" % i for i in range(4)], w=["yT"])
        P.fence()
        ar.reset(mark2)
        out_proj(l, b, last, yT, "yT", ev_w_out[j])
        ar.reset(mk0)

    def odd_mixer(l, j, b, last, w):
        mk0 = ar.off
        qk = ar.alloc([16, T], BF16)
        mark2 = ar.off
        win = ar.alloc([8, 3 * D], BF16)
        load_w(win, od_w_in[j], "win", 8)
        nb = NormBufs()
        mx = load_mod_tiles(l, b, 0, "x", nb.ng, "GS")
        mc = load_mod_tiles(l, NB, 0, "c", nb.ng, "GS")
        hTc = [ar.alloc([8, 256], BF16) for _ in range(2)]
        sq = [ar.alloc([256], BF16) for _ in range(2)]
        rs = [ar.alloc([256], F32) for _ in range(2)]
        qn = [ar.alloc([256], BF16) for _ in range(2)]
        tq = [ar.alloc([256], F32) for _ in range(2)]
        uq = [ar.alloc([256], F32) for _ in range(2)]
        vst = [ar.alloc([1024], BF16) for _ in range(2)]
        rope = w["rope"]
        it = 0
        L = 256
        for ci, (s0, _, isc) in enumerate(ch256):
            hs = ci % 2
            mm_ = mc if isc else mx
            for ti in range(2):
                r0 = b * T + s0 + ti * 128
                norm_tile(nb, xs[r0:r0 + 128, :], XN(r0), mm_["G"], mm_["S"], "c" if isc else "x",
                          hTc[hs][:, :, ti * 128:(ti + 1) * 128], "hTc%d" % hs)
            for ti in range(2):
                vs = (s0 // 128 + ti) % 2
                for hf in range(2):
                    pb = 4 + hf
                    for k in range(8):
                        P.pe(lambda e, k=k, hf=hf, pb=pb, hs=hs, ti=ti: e.matmul(
                            ps[pb][:, :], lhsT=hTc[hs][:, k, ti * 128:(ti + 1) * 128],
                            rhs=win[:, k, 2048 + hf * 512:2048 + (hf + 1) * 512], start=(k == 0), stop=(k == 7)),
                            r=["win", "hTc%d" % hs], w=[PN(pb)])
                    if hf == 0:
                        P.act(lambda e, vs=vs, pb=pb: e.activation(out=vst[vs][:, 0:512], in_=ps[pb][:, :],
                                                                   func=AF.Copy), r=[PN(pb)], w=["vst%d" % vs])
                    else:
                        P.dve(lambda e, vs=vs, pb=pb: e.tensor_copy(out=vst[vs][:, 512:1024], in_=ps[pb][:, :]),
                              r=[PN(pb)], w=["vst%d" % vs])
                t0 = s0 + ti * 128
                P.dma(vtok[t0:t0 + 128, :], vst[vs], r=["vst%d" % vs], w=["vtok"])
            for m in range(16):
                s = it % 2
                it += 1
                pq = s
                pn_ = 2 + s
                for k in range(8):
                    P.pe(lambda e, m=m, k=k, pq=pq, hs=hs: e.matmul(
                        ps[pq][:, 0:L], lhsT=win[:, k, m * 128:(m + 1) * 128], rhs=hTc[hs][:, k, :],
                        start=(k == 0), stop=(k == 7)), r=["win", "hTc%d" % hs], w=[PN(pq)])
                P.act(lambda e, s=s, pq=pq: e.activation(out=sq[s], in_=ps[pq][:, 0:L], func=AF.Square),
                      r=[PN(pq)], w=["sq%d" % s])
                P.pe(lambda e, s=s, pn_=pn_: e.matmul(ps[pn_][:, 0:L], lhsT=bones, rhs=sq[s], start=True, stop=True),
                     r=["bones", "sq%d" % s], w=[PN(pn_)])
                P.dve(lambda e, s=s, pn_=pn_: e.tensor_scalar(out=rs[s], in0=ps[pn_][:, 0:L], scalar1=1.0 / 64,
                                                              scalar2=EPS, op0=ALU.mult, op1=ALU.add),
                      r=[PN(pn_)], w=["rs%d" % s])
                P.act(lambda e, s=s: e.activation(out=rs[s], in_=rs[s], func=AF.Sqrt), r=["rs%d" % s], w=["rs%d" % s])
                P.dve(lambda e, s=s: e.reciprocal(out=rs[s], in_=rs[s]), r=["rs%d" % s], w=["rs%d" % s])
                gcol = 0 if m < 8 else 1
                dst = qk[:, m, s0:s0 + L]
                dn = "qk%d.%d" % (m, ci)
                tgt = dst if isc else qn[s]
                tn = dn if isc else "qn%d" % s
                P.dve(lambda e, s=s, pq=pq, gcol=gcol, tgt=tgt: e.scalar_tensor_tensor(
                    out=tgt, in0=ps[pq][:, 0:L], scalar=w["gqk"][:, gcol:gcol + 1], in1=rs[s],
                    op0=ALU.mult, op1=ALU.mult), r=[PN(pq), "gqk", "rs%d" % s], w=[tn])
                if not isc:
                    x0 = s0 - CTX
                    P.pe(lambda e, s=s, pn_=pn_: e.matmul(ps[pn_][:, 0:L], lhsT=rmat, rhs=qn[s], start=True, stop=True),
                         r=["rmat", "qn%d" % s], w=[PN(pn_)])
                    P.dve(lambda e, s=s, pn_=pn_, x0=x0: e.tensor_tensor(
                        out=tq[s], in0=ps[pn_][:, 0:L], in1=rope[:, 1, x0:x0 + L], op=ALU.mult),
                        r=[PN(pn_), "rope"], w=["tq%d" % s])
                    P.pool(lambda e, s=s, x0=x0: e.tensor_tensor(
                        out=uq[s], in0=qn[s], in1=rope[:, 0, x0:x0 + L], op=ALU.mult),
                        r=["qn%d" % s, "rope"], w=["uq%d" % s])
                    P.pool(lambda e, s=s, dst=dst: e.tensor_tensor(out=dst, in0=tq[s], in1=uq[s], op=ALU.add),
                           r=["tq%d" % s, "uq%d" % s], w=[dn])
        P.fence()
        ar.reset(mark2)
        yT = ar.alloc([8, T], BF16)
        mark3 = ar.off
        vh = [ar.alloc([NT, 128], BF16) for _ in range(2)]
        eb = [ar.alloc([512], BF16) for _ in range(4)]
        acc = [[ar.alloc([512], F32) for _ in range(2)] for _ in range(2)]
        accb = [ar.alloc([512], BF16) for _ in range(2)]
        rc = [ar.alloc([512], F32) for _ in range(2)]
        oa = [ar.alloc([512], F32) for _ in range(2)]
        of = ar.alloc([512], F32)
        o2 = ar.alloc([512], BF16)
        rs2 = ar.alloc([512], F32)
        qall = ["qk%d.%d" % (m, ci) for m in range(16) for ci in range(len(ch256))]
        ei = 0
        si = 0
        vt3 = vtok.rearrange("(t p) f -> p t f", p=128)
        for h in range(8):
            vs = h % 2
            P.dma(vh[vs], vt3[:, :, h * 128:(h + 1) * 128], r=["vtok"], w=["vh%d" % vs])
            for (s0, Lc, isc) in chunks:
                if last and isc:
                    continue
                kts = list(range(NTC)) if isc else list(range(NT))
                nk = len(kts)
                items = [(m, ki, kt) for m in range(2) for ki, kt in enumerate(kts)]
                SB = [0, 1, 7]

                def s_mm(ii, h=h, s0=s0, Lc=Lc, items=items):
                    m, ki, kt = items[ii]
                    sb = SB[ii % 3]
                    P.pe(lambda e, m=m, kt=kt, sb=sb: e.matmul(
                        ps[sb][:, 0:Lc], lhsT=qk[64 * m:64 * m + 64, 8 + h, kt * 128:(kt + 1) * 128],
                        rhs=qk[64 * m:64 * m + 64, h, s0:s0 + Lc], start=True, stop=True),
                        r=qall, w=[PN(sb)])

                s_mm(0)
                if len(items) > 1:
                    s_mm(1)
                for ii, (m, ki, kt) in enumerate(items):
                    if ii + 2 < len(items):
                        s_mm(ii + 2)
                    sb = SB[ii % 3]
                    es = ei % 4
                    ei += 1
                    P.act(lambda e, sb=sb, es=es, Lc=Lc: e.activation(out=eb[es][:, 0:Lc], in_=ps[sb][:, 0:Lc],
                                                                      func=AF.Exp, scale=0.125),
                          r=[PN(sb)], w=["eb%d" % es])
                    P.pe(lambda e, m=m, kt=kt, es=es, vs=vs, Lc=Lc, ki=ki, nk=nk: e.matmul(
                        ps[2 + m][:, 0:Lc], lhsT=vh[vs][:, kt, :], rhs=eb[es][:, 0:Lc], start=(ki == 0),
                        stop=(ki == nk - 1)), r=["vh%d" % vs, "eb%d" % es], w=[PN(2 + m)])
                    a_ = acc[m][ki % 2]
                    an = "acc%d.%d" % (m, ki % 2)
                    eng = P.pool if ki % 2 == 0 else P.dve
                    if ki < 2:
                        eng(lambda e, a_=a_, es=es, Lc=Lc: e.tensor_copy(out=a_[:, 0:Lc], in_=eb[es][:, 0:Lc]),
                            r=["eb%d" % es], w=[an])
                    else:
                        eng(lambda e, a_=a_, es=es, Lc=Lc: e.tensor_tensor(out=a_[:, 0:Lc], in0=a_[:, 0:Lc],
                                                                          in1=eb[es][:, 0:Lc], op=ALU.add),
                            r=["eb%d" % es, an], w=[an])
                    if ki == nk - 1:
                        P.pool(lambda e, m=m, Lc=Lc: e.tensor_tensor(out=accb[m][:, 0:Lc], in0=acc[m][0][:, 0:Lc],
                                                                     in1=acc[m][1][:, 0:Lc], op=ALU.add),
                               r=["acc%d.0" % m, "acc%d.1" % m], w=["accb%d" % m])
                        P.pe(lambda e, m=m, Lc=Lc: e.matmul(ps[4 + m][:, 0:Lc], lhsT=ones, rhs=accb[m][:, 0:Lc],
                                                            start=True, stop=True),
                             r=["ones", "accb%d" % m], w=[PN(4 + m)])
                for m in range(2):
                    P.dve(lambda e, m=m, Lc=Lc: e.reciprocal(out=rc[m][:, 0:Lc], in_=ps[4 + m][:, 0:Lc]),
                          r=[PN(4 + m)], w=["rc%d" % m])
                    P.dve(lambda e, m=m, Lc=Lc: e.tensor_tensor(out=oa[m][:, 0:Lc], in0=ps[2 + m][:, 0:Lc],
                                                                in1=rc[m][:, 0:Lc], op=ALU.mult),
                          r=[PN(2 + m), "rc%d" % m], w=["oa%d" % m])
                P.dve(lambda e, Lc=Lc: e.scalar_tensor_tensor(out=of[:, 0:Lc], in0=oa[1][:, 0:Lc],
                                                              scalar=w["nlam"][:, 0:1], in1=oa[0][:, 0:Lc],
                                                              op0=ALU.mult, op1=ALU.add),
                      r=["oa0", "oa1", "nlam"], w=["of"])
                P.act(lambda e, Lc=Lc: e.activation(out=o2[:, 0:Lc], in_=of[:, 0:Lc], func=AF.Square),
                      r=["of"], w=["o2"])
                P.pe(lambda e, Lc=Lc: e.matmul(ps[6][:, 0:Lc], lhsT=ones, rhs=o2[:, 0:Lc], start=True, stop=True),
                     r=["ones", "o2"], w=[PN(6)])
                P.dve(lambda e, Lc=Lc: e.tensor_scalar(out=rs2[:, 0:Lc], in0=ps[6][:, 0:Lc], scalar1=1.0 / 128,
                                                       scalar2=EPS, op0=ALU.mult, op1=ALU.add), r=[PN(6)], w=["rs2"])
                P.act(lambda e, Lc=Lc: e.activation(out=rs2[:, 0:Lc], in_=rs2[:, 0:Lc], func=AF.Sqrt),
                      r=["rs2"], w=["rs2"])
                P.dve(lambda e, Lc=Lc: e.reciprocal(out=rs2[:, 0:Lc], in_=rs2[:, 0:Lc]), r=["rs2"], w=["rs2"])
                P.dve(lambda e, h=h, s0=s0, Lc=Lc: e.scalar_tensor_tensor(
                    out=yT[:, h, s0:s0 + Lc], in0=of[:, 0:Lc], scalar=w["ghs"][:, 0:1], in1=rs2[:, 0:Lc],
                    op0=ALU.mult, op1=ALU.mult), r=["of", "ghs", "rs2"], w=["yT"])
        P.fence()
        ar.reset(mark3)
        out_proj(l, b, last, yT, "yT", od_w_out[j])
        ar.reset(mk0)

    def mlp_stage(l, last):
        ar.reset()
        w1 = ar.alloc([8, DFF], BF16)
        w2 = ar.alloc([32, D], BF16)
        load_w(w1, mlp_w1[l], "w1", 8)
        load_w(w2, mlp_w2[l], "w2", 32)
        nb = NormBufs(with_xt=False)
        xk = [[ar.alloc([D], F32) for _ in range(2)] for _ in range(2)]
        hTc = [ar.alloc([8, 256], BF16) for _ in range(2)]
        hid = ar.alloc([32, 256], BF16)
        rl = [ar.alloc([256], F32) for _ in range(2)]
        mk = ar.off
        ci = 0
        for b in range(NB):
            for isc in (True, False):
                if last and isc:
                    continue
                ar.reset(mk)
                md = load_mod_tiles(l, NB if isc else b, 1, "mm", nb.ng, "GSg")
                G, S, Gt = md["G"], md["S"], md["Gt"]
                t_lo, t_hi = (0, CTX) if isc else (CTX, T)
                for c0 in range(t_lo, t_hi, 256):
                    cs_ = ci % 2
                    ci += 1
                    for ti in range(2):
                        r0 = b * T + c0 + ti * 128
                        norm_tile(nb, xs[r0:r0 + 128, :], XN(r0), G, S, "mm",
                                  hTc[cs_][:, :, ti * 128:(ti + 1) * 128],
                                  "mh%d" % cs_, keep_x=(xk[cs_][ti], "xk%d.%d" % (cs_, ti)))
                    for jf in range(32):
                        pb = jf % 2
                        for k in range(8):
                            P.pe(lambda e, jf=jf, k=k, pb=pb, cs_=cs_: e.matmul(
                                ps[pb][:, 0:256], lhsT=w1[:, k, jf * 128:(jf + 1) * 128], rhs=hTc[cs_][:, k, :],
                                start=(k == 0), stop=(k == 7)), r=["w1", "mh%d" % cs_], w=[PN(pb)])
                        P.act(lambda e, pb=pb: e.activation(out=rl[pb], in_=ps[pb][:, 0:256], func=AF.Relu),
                              r=[PN(pb)], w=["rl%d" % pb])
                        P.dve(lambda e, pb=pb, jf=jf: e.tensor_tensor(out=hid[:, jf, :], in0=ps[pb][:, 0:256],
                                                                      in1=rl[pb], op=ALU.mult),
                              r=[PN(pb), "rl%d" % pb], w=["hid%d" % jf])
                    hall = ["hid%d" % jf for jf in range(32)]
                    for ti in range(2):
                        xt = xk[cs_][ti]
                        xn = "xk%d.%d" % (cs_, ti)
                        r0 = b * T + c0 + ti * 128
                        for hf in range(2):
                            pb = 2 + (ti * 2 + hf) % 4
                            for jf in range(32):
                                P.pe(lambda e, jf=jf, hf=hf, pb=pb, ti=ti: e.matmul(
                                    ps[pb][:, :], lhsT=hid[:, jf, ti * 128:(ti + 1) * 128],
                                    rhs=w2[:, jf, hf * 512:(hf + 1) * 512], start=(jf == 0), stop=(jf == 31)),
                                    r=hall + ["w2"], w=[PN(pb)])
                            P.dve(lambda e, hf=hf, pb=pb, Gt=Gt: e.tensor_tensor(
                                out=nb.t1[:, hf * 512:(hf + 1) * 512], in0=ps[pb][:, :],
                                in1=Gt[:, hf * 512:(hf + 1) * 512], op=ALU.mult),
                                r=[PN(pb), "mmGt"], w=["mt1%d" % hf])
                            P.pool(lambda e, hf=hf, xt=xt: e.tensor_tensor(
                                out=xt[:, hf * 512:(hf + 1) * 512], in0=xt[:, hf * 512:(hf + 1) * 512],
                                in1=nb.t1[:, hf * 512:(hf + 1) * 512], op=ALU.add), r=["mt1%d" % hf, xn], w=[xn])
                        if last:
                            o0 = b * SEQ + (c0 - CTX) + ti * 128
                            P.dma(yout[o0:o0 + 128, :], xt, r=[xn], w=["youtd"])
                        else:
                            P.dma(xs[r0:r0 + 128, :], xt, r=[xn], w=[XN(r0)])
        P.fence()

    stage_adaln()
    for l in range(DEPTH):
        last = l == DEPTH - 1
        j = l // 2
        even = l % 2 == 0
        ar.reset()
        w = {}
        if even:
            ssm_prep(j, w)
        else:
            w["rope"] = ar.alloc([2, SEQ], BF16)
            P.dma(w["rope"], rope_d.rearrange("c p t -> p c t"), w=["rope"], q="pool")
            w["gqk"] = ar.alloc([2], F32)
            w["ghs"] = ar.alloc([1], F32)
            w["nlam"] = ar.alloc([1], F32)
            P.dma(w["gqk"], od_qk[j], w=["gqk"])
            P.dma(w["ghs"], od_hn[j], w=["ghs"])
            lp = ar.alloc([256], F32)
            lt = ar.alloc([8], F32)
            P.dma(lp, od_lam[j, :].partition_broadcast(128), w=["lp"])
            lam_init = 0.8 - 0.6 * math.exp(-0.3 * l)
            P.dve(lambda e, lp=lp: e.tensor_tensor(out=lp[:, 0:64], in0=lp[:, 0:64], in1=lp[:, 64:128], op=ALU.mult),
                  r=["lp"], w=["lp"])
            P.dve(lambda e, lp=lp: e.tensor_tensor(out=lp[:, 128:192], in0=lp[:, 128:192], in1=lp[:, 192:256], op=ALU.mult),
                  r=["lp"], w=["lp"])
            P.dve(lambda e, lp=lp, lt=lt: e.tensor_reduce(out=lt[:, 0:1], in_=lp[:, 0:64], op=ALU.add,
                                            axis=mybir.AxisListType.X), r=["lp"], w=["lt"])
            P.dve(lambda e, lp=lp, lt=lt: e.tensor_reduce(out=lt[:, 1:2], in_=lp[:, 128:192], op=ALU.add,
                                            axis=mybir.AxisListType.X), r=["lp"], w=["lt"])
            P.act(lambda e, lt=lt: e.activation(out=lt[:, 2:4], in_=lt[:, 0:2], func=AF.Exp), r=["lt"], w=["lt"])
            P.dve(lambda e, lt=lt: e.tensor_tensor(out=lt[:, 4:5], in0=lt[:, 3:4], in1=lt[:, 2:3], op=ALU.subtract),
                  r=["lt"], w=["lt"])
            P.dve(lambda e, lam_init=lam_init, w=w, lt=lt: e.tensor_scalar(out=w["nlam"], in0=lt[:, 4:5], scalar1=-lam_init,
                                                               scalar2=None, op0=ALU.add), r=["lt"], w=["nlam"])
            P.dve(lambda e, lam_init=lam_init, w=w: e.tensor_scalar(out=w["ghs"], in0=w["ghs"], scalar1=1.0 - lam_init,
                                                               scalar2=None, op0=ALU.mult), r=["ghs"], w=["ghs"])
        for b in range(NB):
            (even_mixer if even else odd_mixer)(l, j, b, last, w)
        P.fence()
        mlp_stage(l, last)
    P.finalize(st)
    P.emit()
    st.close()
    _STATS['ops'] = len(P.ops)
    return nc


def _consts(SEQ, CTX, GRID_W=64):
    bf = ml_dtypes.bfloat16
    c = np.arange(128)
    ang = 2 * np.pi * np.outer(c, c) / 128
    csc = np.concatenate([np.cos(ang), np.sin(ang)], axis=1).astype(np.float32)

    def dft(L):
        t = np.arange(L, dtype=np.float64)
        a = 2 * np.pi * (np.outer(t, t) % L) / L
        nrm = 1.0 / math.sqrt(L * 128)
        return np.cos(a) * nrm, -np.sin(a) * nrm

    Cx, Sx = dft(SEQ)
    ncx = SEQ // 512
    dftx = np.stack([Cx, Sx], 0).reshape(2, SEQ // 128, 128, ncx, 512).transpose(3, 1, 2, 0, 4)
    Cc, Sc = dft(CTX)
    dftc = np.stack([Cc, Sc], 0).reshape(2, CTX // 128, 128, 1, CTX).transpose(3, 1, 2, 0, 4)
    T = CTX + SEQ
    j = np.arange(T)
    tau_f = j.astype(np.float32)
    tau_b = np.where(j < CTX, CTX - 1 - j, CTX + (T - 1 - j)).astype(np.float32)
    tau = np.stack([tau_f, tau_b], 0)
    half = 32
    inv = 10000.0 ** (-np.arange(0, half, 2, dtype=np.float32) / half)
    t = np.arange(SEQ)
    row = (t // GRID_W).astype(np.float32)
    col = (t % GRID_W).astype(np.float32)
    ar_ = row[None, :] * inv[:, None]
    ac_ = col[None, :] * inv[:, None]
    a64 = np.concatenate([ar_, ar_, ac_, ac_], 0)
    a128 = np.concatenate([a64, a64], 0).astype(np.float32)
    rope = np.stack([np.cos(a128), np.sin(a128)], 0).astype(np.float32)
    R = np.zeros((128, 128), np.float32)
    for blk in range(4):
        o = blk * 32
        for i in range(16):
            R[o + i, o + i + 16] = -1.0
            R[o + i + 16, o + i] = 1.0
    rmat = np.ascontiguousarray(R.T)
    bones = np.zeros((128, 128), np.float32)
    bones[:64, :64] = 1
    bones[64:, 64:] = 1
    return dict(csc=csc, dftx=np.ascontiguousarray(dftx).astype(bf), dftc=np.ascontiguousarray(dftc).astype(bf),
                iota=np.arange(256, dtype=np.float32)[None, :], rope=rope, rmat=rmat, bones=bones, ident=np.eye(128, dtype=np.float32))


def _prep_weights(inp, DEPTH):
    f = lambda a: np.ascontiguousarray(np.asarray(a, dtype=np.float32))
    n_even = (DEPTH + 1) // 2
    n_odd = DEPTH // 2
    out = dict(norm_g=f(np.stack([inp["norm1_g"][:DEPTH], inp["norm2_g"][:DEPTH]], 1)),
               ada_w=f(inp["ada_w"][:DEPTH]), ada_b=f(inp["ada_b"][:DEPTH]),
               mlp_w1=f(inp["mlp_w1"][:DEPTH]), mlp_w2=f(inp["mlp_w2"][:DEPTH]))
    if n_even:
        out["ev_w_in"] = f(inp["ev_w_in"][:n_even])
        out["ev_w_out"] = f(inp["ev_w_out"][:n_even])
        are = np.asarray(inp["ssm_a_re"])[:n_even].reshape(n_even, 32, 128).transpose(0, 2, 1)
        aim = np.asarray(inp["ssm_a_im"])[:n_even].reshape(n_even, 32, 128).transpose(0, 2, 1)
        ldt = np.repeat(np.asarray(inp["ssm_log_dt"])[:n_even].reshape(n_even, 64), 64, axis=1)
        ldt = ldt.reshape(n_even, 32, 128).transpose(0, 2, 1)
        out["ssm_sc"] = f(np.stack([are, aim, ldt], 1))
        bblk = np.zeros((n_even, 2, 32, 128, 128), np.float32)
        cblk = np.zeros((n_even, 2, 32, 128, 128), np.float32)
        for c, (bn, cn) in enumerate((("ssm_b_re", "ssm_c_re"), ("ssm_b_im", "ssm_c_im"))):
            B = np.asarray(inp[bn])[:n_even]
            C = np.asarray(inp[cn])[:n_even]
            for d in range(2):
                for g in range(32):
                    r = d * 16 + g // 2
                    p0 = (g % 2) * 64
                    c0 = (g % 8) * 16
                    bblk[:, c, r, p0:p0 + 64, c0:c0 + 16] = B[:, d, g]
                    cblk[:, c, r, p0:p0 + 64, c0:c0 + 16] = C[:, d, g].transpose(0, 2, 1)
        out["ssm_bblk"] = bblk
        out["ssm_cblk"] = cblk
        out["ssm_d"] = f(np.asarray(inp["ssm_d"])[:n_even].reshape(n_even, 4, 128).transpose(0, 2, 1))
        out["ssm_bglu"] = f(np.asarray(inp["ssm_b_glu"])[:n_even].reshape(n_even, 4, 128).transpose(0, 2, 1))
        out["ssm_wglu"] = f(inp["ssm_w_glu"][:n_even])
    if n_odd:
        out["od_w_in"] = f(inp["od_w_in"][:n_odd])
        out["od_w_out"] = f(inp["od_w_out"][:n_odd])
        gq = np.tile(np.asarray(inp["od_q_norm"])[:n_odd], (1, 2))
        gk = np.tile(np.asarray(inp["od_k_norm"])[:n_odd], (1, 2))
        out["od_qk"] = f(np.stack([gq, gk], -1))
        out["od_lam"] = f(np.asarray(inp["od_lambda"])[:n_odd].reshape(n_odd, 256))
        out["od_hn"] = f(np.asarray(inp["od_head_norm"])[:n_odd].reshape(n_odd, 128, 1))
    return out


_CACHE = {}
_STATS = {}


def run(inputs, DEPTH=4, n_cores=8, GRID_W=64):
    x = np.asarray(inputs["x"], dtype=np.float32)
    ctx = np.asarray(inputs["ctx"], dtype=np.float32)
    c = np.asarray(inputs["c"], dtype=np.float32)
    c_ctx = np.asarray(inputs["c_ctx"], dtype=np.float32)
    B, SEQ, _ = x.shape
    CTX = ctx.shape[1]
    NB = B // n_cores
    T = SEQ + CTX
    key = (NB, SEQ, CTX, DEPTH)
    if key not in _CACHE:
        _CACHE[key] = build(NB, SEQ, CTX, DEPTH, GRID_W)
    nc = _CACHE[key]
    shared = _prep_weights(inputs, DEPTH)
    consts = _consts(SEQ, CTX, GRID_W)
    n_even = (DEPTH + 1) // 2
    n_odd = DEPTH // 2
    if not n_even:
        for k in ("csc", "dftx", "dftc", "iota"):
            consts.pop(k)
    if not n_odd:
        for k in ("rope", "rmat", "bones"):
            consts.pop(k)
    shared.update(consts)
    in_maps = []
    for i in range(n_cores):
        sl = slice(i * NB, (i + 1) * NB)
        xin = np.concatenate([ctx[sl], x[sl]], axis=1).reshape(NB * T, D)
        cond = np.concatenate([c[sl], c_ctx[None, :]], 0)
        condT = np.ascontiguousarray(cond.T.reshape(8, 128, NB + 1).transpose(1, 0, 2))
        m = dict(shared)
        m["xin"] = np.ascontiguousarray(xin)
        m["condT"] = condT
        in_maps.append(m)
    res = run_bass_kernel_spmd(nc, in_maps, core_ids=list(range(n_cores)))
    outs = [np.asarray(r["yout"]).reshape(NB, SEQ, D) for r in res.results]
    return np.concatenate(outs, 0).astype(np.float32)


def kernel(**inputs):
    return run(inputs, DEPTH=4, n_cores=8)
```

```python
import contextlib
import numpy as np
import concourse.bass as bass
import concourse.mybir as mybir

F32 = mybir.dt.float32
BF16 = mybir.dt.bfloat16
I32 = mybir.dt.int32
AF = mybir.ActivationFunctionType
ALU = mybir.AluOpType

COMPUTE = ("pe", "act", "dve", "pool")
NSLOT = 8


class Buf:
    __slots__ = ("name", "lw", "rd")

    def __init__(self, name):
        self.name = name
        self.lw = None
        self.rd = []


class Op:
    __slots__ = ("eng", "fn", "reads", "writes", "deps", "signal", "sigval", "waits",
                 "dma", "slot", "gen", "fence")

    def __init__(self, eng, fn, reads, writes, dma):
        self.eng, self.fn, self.reads, self.writes, self.dma = eng, fn, reads, writes, dma
        self.deps = []
        self.signal = False
        self.sigval = 0
        self.waits = []
        self.slot = None
        self.gen = 0
        self.fence = False


class Prog:
    def __init__(self, nc):
        self.nc = nc
        self.ops = []
        self.bufs = {}

    def buf(self, name):
        b = self.bufs.get(name)
        if b is None:
            b = self.bufs[name] = Buf(name)
        return b

    def _bl(self, xs):
        out = []
        for x in xs:
            if isinstance(x, (list, tuple)):
                out.extend(self._bl(x))
            elif isinstance(x, str):
                out.append(self.buf(x))
            elif x is not None:
                out.append(x)
        return out

    def add(self, eng, fn, reads=(), writes=(), dma=False):
        op = Op(eng, fn, self._bl(reads), self._bl(writes), dma)
        self.ops.append(op)
        return op

    def pe(self, fn, r=(), w=()):
        return self.add("pe", fn, r, w)

    def act(self, fn, r=(), w=()):
        return self.add("act", fn, r, w)

    def dve(self, fn, r=(), w=()):
        return self.add("dve", fn, r, w)

    def pool(self, fn, r=(), w=()):
        return self.add("pool", fn, r, w)

    def dma(self, out, in_, r=(), w=(), q="sp", **kw):
        return self.add(q, lambda e: e.dma_start(out=out, in_=in_, **kw), r, w, dma=True)

    def fence(self):
        op = Op("sp", None, [], [], False)
        op.fence = True
        self.ops.append(op)

    def finalize(self, stack):
        nc = self.nc
        ops = self.ops
        last_on_eng = {}
        dmas_since = []
        fence_deps = []
        first_after = {}
        for op in ops:
            if op.fence:
                fence_deps = list(last_on_eng.values()) + list(dmas_since)
                dmas_since = []
                first_after = {}
                continue
            raw = set(b.lw for b in op.reads if b.lw is not None)
            deps = set(raw)
            for b in op.writes:
                if b.lw is not None:
                    deps.add(b.lw)
                deps.update(b.rd)
            fdeps = ()
            if op.eng not in first_after:
                first_after[op.eng] = True
                fdeps = fence_deps
            for b in op.reads:
                b.rd.append(op)
            for b in op.writes:
                b.lw = op
                b.rd = []
            keep = []
            for d in list(deps) + list(fdeps):
                if d is op or d in keep:
                    continue
                if d.eng == op.eng and not d.dma and not op.dma:
                    if op.eng == "pe" or d not in raw:
                        continue
                keep.append(d)
            op.deps = keep
            for d in keep:
                d.signal = True
            if op.dma:
                dmas_since.append(op)
            else:
                last_on_eng[op.eng] = op
        cnt = {e: 0 for e in COMPUTE}
        slot_next = {}
        slot_gen = {}
        self.sems = {}
        for e in COMPUTE:
            self.sems[e] = stack.enter_context(nc.semaphore("s_" + e))
        self.dma_sems = {}
        for op in ops:
            if op.fence:
                continue
            if op.dma:
                q = op.eng
                if q not in slot_next:
                    slot_next[q] = 0
                    for i in range(NSLOT):
                        self.dma_sems[(q, i)] = stack.enter_context(nc.semaphore("d_%s%d" % (q, i)))
                        slot_gen[(q, i)] = 0
                s = slot_next[q]
                slot_next[q] = (s + 1) % NSLOT
                slot_gen[(q, s)] += 1
                op.slot = (q, s)
                op.gen = slot_gen[(q, s)]
            elif op.signal:
                cnt[op.eng] += 1
                op.sigval = cnt[op.eng]
        self.slot_gen = slot_gen
        waited = {}
        for op in ops:
            if op.fence:
                continue
            w = waited.setdefault(op.eng, {})
            need = {}
            for d in op.deps:
                if d.dma:
                    key = ("d",) + d.slot
                    val = 16 * d.gen
                else:
                    key = ("c", d.eng)
                    val = d.sigval
                if val > need.get(key, 0):
                    need[key] = val
            if op.dma and op.gen > 1:
                key = ("d",) + op.slot
                val = 16 * (op.gen - 1)
                if val > need.get(key, 0):
                    need[key] = val
            for key, val in need.items():
                if w.get(key, 0) < val:
                    w[key] = val
                    op.waits.append((key, val))
        self.n_ops = len(ops)

    def _sem(self, key):
        if key[0] == "c":
            return self.sems[key[1]]
        return self.dma_sems[(key[1], key[2])]

    def emit(self):
        nc = self.nc
        by_eng = {}
        for op in self.ops:
            if not op.fence:
                by_eng.setdefault(op.eng, []).append(op)
        handles = {"pe": "tensor", "act": "scalar", "dve": "vector", "pool": "gpsimd", "sp": "sync"}
        with nc.Block() as block:
            for eng, name in handles.items():
                lst = by_eng.get(eng, [])

                def body(e, lst=lst, eng=eng):
                    for op in lst:
                        for key, val in op.waits:
                            e.wait_ge(self._sem(key), val)
                        ins = op.fn(e)
                        if op.dma:
                            ins.then_inc(self.dma_sems[op.slot], 16)
                        elif op.signal:
                            ins.then_inc(self.sems[eng], 1)
                    if eng == "sp":
                        for (q, s), g in self.slot_gen.items():
                            if g > 0:
                                e.wait_ge(self.dma_sems[(q, s)], 16 * g)

                getattr(block, name)(body)


class Arena:
    def __init__(self, nc, stack, nbytes, name="arena"):
        self.nbytes = nbytes
        self.t = stack.enter_context(nc.sbuf_tensor(name, [128, nbytes // 4], F32))
        self.off = 0
        self.mark = 0

    def reset(self, to=None):
        self.off = self.mark if to is None else to

    def alloc(self, shape, dtype):
        n = int(np.prod(shape))
        esz = mybir.dt.size(dtype)
        nb = (n * esz + 31) // 32 * 32
        assert self.off + nb <= self.nbytes, ("SBUF arena overflow", self.off, nb, self.nbytes)
        w0 = self.off // 4
        ap = self.t[:, w0:w0 + nb // 4]
        self.off += nb
        if dtype != F32:
            ap = ap.bitcast(dtype)
        ap = ap[:, 0:n]
        if len(shape) == 2:
            ap = ap.rearrange("p (a b) -> p a b", a=shape[0])
        elif len(shape) == 3:
            ap = ap.rearrange("p (a b c) -> p a b c", a=shape[0], b=shape[1])
        return ap

import math
import ml_dtypes
from concourse.bass_utils import run_bass_kernel_spmd

D = 1024
DFF = 4096
EPS = 1e-6
TWO_PI = 2.0 * math.pi
CW1 = 6.28125
CW2 = TWO_PI - CW1


def build(NB, SEQ, CTX, DEPTH, GRID_W=64):
    T = CTX + SEQ
    NT = T // 128
    NTC = CTX // 128
    NCX = SEQ // 512
    NR = NB + 1
    chunks = [(0, CTX, True)] + [(CTX + i * 512, 512, False) for i in range(NCX)]
    n_even = (DEPTH + 1) // 2
    n_odd = DEPTH // 2
    nc = bass.Bass("TRN2", target_bir_lowering=False)

    def din(name, shape, dt=F32):
        return nc.dram_tensor(name, list(shape), dt, kind="ExternalInput").ap()

    def dscr(name, shape, dt=F32):
        return nc.dram_tensor(name, list(shape), dt, kind="Internal").ap()

    xin = din("xin", [NB * T, D])
    condT = din("condT", [128, 8, NR])
    norm_g = din("norm_g", [DEPTH, 2, D])
    ada_w = din("ada_w", [DEPTH, D, 6 * D])
    ada_b = din("ada_b", [DEPTH, 6 * D])
    mlp_w1 = din("mlp_w1", [DEPTH, D, DFF])
    mlp_w2 = din("mlp_w2", [DEPTH, DFF, D])
    ident_d = din("ident", [128, 128])
    if n_even:
        ev_w_in = din("ev_w_in", [n_even, D, D])
        ev_w_out = din("ev_w_out", [n_even, D, D])
        ssm_sc = din("ssm_sc", [n_even, 3, 128, 32])
        ssm_bblk = din("ssm_bblk", [n_even, 2, 32, 128, 128])
        ssm_cblk = din("ssm_cblk", [n_even, 2, 32, 128, 128])
        ssm_d = din("ssm_d", [n_even, 128, 4])
        ssm_bglu = din("ssm_bglu", [n_even, 128, 4])
        ssm_wglu = din("ssm_wglu", [n_even, 512, 512])
        csc_d = din("csc", [128, 256])
        dftx = din("dftx", [NCX, SEQ // 128, 128, 2, 512], BF16)
        dftc = din("dftc", [1, NTC, 128, 2, CTX], BF16)
        iota_d = din("iota", [1, 256])
    if n_odd:
        od_w_in = din("od_w_in", [n_odd, D, 3 * D])
        od_w_out = din("od_w_out", [n_odd, D, D])
        od_qk = din("od_qk", [n_odd, 128, 2])
        od_lam = din("od_lam", [n_odd, 256])
        od_hn = din("od_hn", [n_odd, 128, 1])
        rope_d = din("rope", [2, 128, SEQ])
        rmat_d = din("rmat", [128, 128])
        bones_d = din("bones", [128, 128])
    yout = nc.dram_tensor("yout", [NB * SEQ, D], F32, kind="ExternalOutput").ap()
    xs = dscr("xs", [NB * T, D])
    mods = dscr("mods", [DEPTH, NR, 6 * D])
    vtok = dscr("vtok", [T, D], BF16)
    NCH = T // 256
    cstab = dscr("cstab", [8, NCH, 128, 4 * 2 * 256], BF16)

    st = contextlib.ExitStack()
    P = Prog(nc)
    ar = Arena(nc, st, 206000)
    psall = st.enter_context(nc.psum_tensor("psall", [128, 4096], F32))
    ps = [psall[:, i * 512:(i + 1) * 512] for i in range(8)]
    uid = [0]

    def U(s):
        uid[0] += 1
        return "%s#%d" % (s, uid[0])

    ident = ar.alloc([128], BF16)
    ones = ar.alloc([128], BF16)
    halfpi = ar.alloc([1], F32)
    P.dma(ident, ident_d, w=["ident"], q="pool")
    P.dve(lambda e: e.memset(ones, 1.0), w=["ones"])
    P.dve(lambda e: e.memset(halfpi, math.pi / 2), w=["halfpi"])
    if n_even:
        csc = ar.alloc([256], BF16)
        P.dma(csc, csc_d, w=["csc"], q="pool")
        iot = ar.alloc([256], F32)
        P.dma(iot, iota_d[0, :].partition_broadcast(128), w=["iot"])
    if n_odd:
        rmat = ar.alloc([128], BF16)
        bones = ar.alloc([128], BF16)
        P.dma(rmat, rmat_d, w=["rmat"], q="pool")
        P.dma(bones, bones_d, w=["bones"], q="pool")
    ar.mark = ar.off

    def stage_adaln():
        ar.reset()
        ct = ar.alloc([8, NR], F32)
        cs_ = ar.alloc([8, NR], F32)
        P.dma(ct, condT, w=["ct"])
        P.act(lambda e: e.activation(out=cs_, in_=ct, func=AF.Silu), r=["ct"], w=["cs"])
        wt = [ar.alloc([8, 512], F32) for _ in range(2)]
        bt = [ar.alloc([512], F32) for _ in range(2)]
        ot = [ar.alloc([512], F32) for _ in range(2)]
        it = 0
        for l in range(DEPTH):
            for n in range(12):
                s = it % 2
                it += 1
                P.dma(wt[s], ada_w[l, :, n * 512:(n + 1) * 512].rearrange("(k p) n -> p k n", p=128),
                      w=["aw%d" % s])
                P.dma(bt[s][0:NR, :], ada_b[l, n * 512:(n + 1) * 512].partition_broadcast(NR), w=["ab%d" % s])
                for k in range(8):
                    P.pe(lambda e, s=s, k=k: e.matmul(ps[s][0:NR, :], lhsT=cs_[:, k, :], rhs=wt[s][:, k, :],
                                                      start=(k == 0), stop=(k == 7)),
                         r=["cs", "aw%d" % s], w=["ps%d" % s])
                P.dve(lambda e, s=s: e.tensor_tensor(out=ot[s][0:NR, :], in0=ps[s][0:NR, :], in1=bt[s][0:NR, :],
                                                     op=ALU.add), r=["ps%d" % s, "ab%d" % s], w=["ao%d" % s])
                P.dma(mods[l, :, n * 512:(n + 1) * 512], ot[s][0:NR, :], r=["ao%d" % s], w=["mods"])
        P.fence()

    ch256 = [(s0, 256, s0 < CTX) for s0 in range(0, T, 256)]
    PN = lambda b: "ps%d" % b
    XN = lambda r0: "xs%d" % r0

    def load_mod_tiles(l, row, which, pref, ngb, need):
        base = which * 3 * D
        out = {}
        if "G" in need:
            G = ar.alloc([D], F32)
            P.dma(ngb, norm_g[l, which, :].partition_broadcast(128), w=["ngb"])
            P.dma(G, mods[l, row, base + D:base + 2 * D].partition_broadcast(128), r=["mods"], w=[pref + "G"])
            P.dve(lambda e: e.scalar_tensor_tensor(out=G, in0=G, scalar=1.0, in1=ngb, op0=ALU.add, op1=ALU.mult),
                  r=[pref + "G", "ngb"], w=[pref + "G"])
            out["G"] = G
        if "S" in need:
            S = ar.alloc([D], F32)
            P.dma(S, mods[l, row, base:base + D].partition_broadcast(128), r=["mods"], w=[pref + "S"])
            out["S"] = S
        if "g" in need:
            Gt = ar.alloc([D], F32)
            P.dma(Gt, mods[l, row, base + 2 * D:base + 3 * D].partition_broadcast(128), r=["mods"], w=[pref + "Gt"])
            out["Gt"] = Gt
        return out

    class NormBufs:
        def __init__(self, with_xt=True):
            self.xt = [ar.alloc([D], F32) for _ in range(2)] if with_xt else None
            self.junk = ar.alloc([D], BF16)
            self.t1 = ar.alloc([D], F32)
            self.hb = [ar.alloc([D], BF16) for _ in range(2)]
            self.st = [ar.alloc([4], F32) for _ in range(2)]
            self.ng = ar.alloc([D], F32)
            self.i = 0

    def norm_tile(nb, src_rows, srcname, G, S, gname, hT_dst, hname, keep_x=None):
        s = nb.i % 2
        nb.i += 1
        xt = nb.xt[s] if keep_x is None else keep_x[0]
        xn = "xt%d" % s if keep_x is None else keep_x[1]
        stt = nb.st[s]
        sn = "nst%d" % s
        P.dma(xt, src_rows, r=[srcname], w=[xn])
        P.act(lambda e: e.activation(out=nb.junk, in_=xt, func=AF.Square, accum_out=stt[:, 0:1]),
              r=[xn], w=["njunk", sn])
        P.dve(lambda e: e.tensor_scalar(out=stt[:, 1:2], in0=stt[:, 0:1], scalar1=1.0 / D, scalar2=EPS,
                                        op0=ALU.mult, op1=ALU.add), r=[sn], w=[sn])
        P.act(lambda e: e.activation(out=stt[:, 2:3], in_=stt[:, 1:2], func=AF.Sqrt), r=[sn], w=[sn])
        P.dve(lambda e: e.reciprocal(out=stt[:, 3:4], in_=stt[:, 2:3]), r=[sn], w=[sn])
        P.dve(lambda e: e.scalar_tensor_tensor(out=nb.t1, in0=xt, scalar=stt[:, 3:4], in1=G,
                                               op0=ALU.mult, op1=ALU.mult), r=[xn, sn, gname + "G"], w=["nt1"])
        hb = nb.hb[s]
        P.pool(lambda e: e.tensor_tensor(out=hb, in0=nb.t1, in1=S, op=ALU.add),
               r=["nt1", gname + "S"], w=["nhb%d" % s])
        pb = 6 + s
        pst = ps[pb][:, :].bitcast(BF16).rearrange("p (k t) -> p k t", k=8)
        for k in range(8):
            P.pe(lambda e, k=k: e.transpose(out=pst[:, k, :], in_=hb[:, k * 128:(k + 1) * 128], identity=ident),
                 r=["nhb%d" % s, "ident"], w=[PN(pb)])
        P.act(lambda e: e.activation(out=hT_dst, in_=pst, func=AF.Copy), r=[PN(pb)], w=[hname])
        return s

    def load_w(dst, src, name, kt):
        N = src.shape[1]
        v = src.rearrange("(k p) n -> p k n", p=128)
        cb = min(N, 2048)
        for c0 in range(0, N, cb):
            c1 = min(N, c0 + cb)
            P.dma(dst[:, :, c0:c1], v[:, :, c0:c1], w=[name], q="pool")

    def out_proj(l, b, last, yT, yname, wsrc):
        mk = ar.off
        wout = ar.alloc([8, D], BF16)
        load_w(wout, wsrc, "wout", 8)
        nb = NormBufs()
        Gtx = load_mod_tiles(l, b, 0, "x", nb.ng, "g")["Gt"]
        Gtc = None if last else load_mod_tiles(l, NB, 0, "c", nb.ng, "g")["Gt"]
        for tt in range(NT):
            isc = tt < NTC
            if last and isc:
                continue
            Gt = Gtc if isc else Gtx
            gn = "c" if isc else "x"
            s = nb.i % 2
            nb.i += 1
            xt = nb.xt[s]
            xn = "xt%d" % s
            r0 = b * T + tt * 128
            P.dma(xt, xs[r0:r0 + 128, :], r=[XN(r0)], w=[xn])
            for hf in range(2):
                pb = 4 + hf
                for k in range(8):
                    P.pe(lambda e, k=k, hf=hf, pb=pb, tt=tt: e.matmul(
                        ps[pb][:, :], lhsT=yT[:, k, tt * 128:(tt + 1) * 128], rhs=wout[:, k, hf * 512:(hf + 1) * 512],
                        start=(k == 0), stop=(k == 7)), r=[yname, "wout"], w=[PN(pb)])
                P.dve(lambda e, hf=hf, pb=pb, Gt=Gt: e.tensor_tensor(
                    out=nb.t1[:, hf * 512:(hf + 1) * 512], in0=ps[pb][:, :], in1=Gt[:, hf * 512:(hf + 1) * 512],
                    op=ALU.mult), r=[PN(pb), gn + "Gt"], w=["ot1%d" % hf])
                P.pool(lambda e, hf=hf, xt=xt: e.tensor_tensor(
                    out=xt[:, hf * 512:(hf + 1) * 512], in0=xt[:, hf * 512:(hf + 1) * 512],
                    in1=nb.t1[:, hf * 512:(hf + 1) * 512], op=ALU.add), r=["ot1%d" % hf, xn], w=[xn])
            P.dma(xs[r0:r0 + 128, :], xt, r=[xn], w=[XN(r0)])
        P.fence()
        ar.reset(mk)

    def ssm_prep(j, w):
        rho = ar.alloc([32], F32)
        th = ar.alloc([32], F32)
        nth = ar.alloc([32], F32)
        BT = ar.alloc([2, 32, 128], BF16)
        CT = ar.alloc([2, 32, 128], BF16)
        mk = ar.off
        sc3 = ar.alloc([3, 32], F32)
        P.dma(sc3, ssm_sc[j].rearrange("a p t -> p a t"), w=["sc3"])
        tmp = ar.alloc([16, 32], F32)
        ti = ar.alloc([32], I32)
        cre = ar.alloc([32], F32)
        cim = ar.alloc([32], F32)
        ncim = ar.alloc([32], F32)
        a_re, a_im = sc3[:, 0, :], sc3[:, 1, :]
        t = lambda i: tmp[:, i, :]
        R, W = ["sc3", "stmp"], ["stmp"]
        P.act(lambda e: e.activation(out=t(0), in_=sc3[:, 2, :], func=AF.Exp), r=R, w=W)
        P.dve(lambda e: e.tensor_tensor(out=t(1), in0=a_re, in1=t(0), op=ALU.mult), r=R, w=W)
        P.dve(lambda e: e.tensor_tensor(out=th, in0=a_im, in1=t(0), op=ALU.mult), r=R, w=["sth"])
        P.dve(lambda e: e.tensor_scalar(out=nth, in0=th, scalar1=-1.0, scalar2=None, op0=ALU.mult),
              r=["sth"], w=["snth"])
        P.act(lambda e: e.activation(out=rho, in_=t(1), func=AF.Exp), r=R, w=["srho"])
        P.dve(lambda e: e.tensor_scalar(out=ti, in0=th, scalar1=1.0 / TWO_PI, scalar2=None, op0=ALU.mult),
              r=["sth"], w=["sti"])
        P.dve(lambda e: e.scalar_tensor_tensor(out=t(2), in0=ti, scalar=-CW1, in1=th, op0=ALU.mult, op1=ALU.add),
              r=["sti", "sth"], w=W)
        P.dve(lambda e: e.scalar_tensor_tensor(out=t(2), in0=ti, scalar=-CW2, in1=t(2), op0=ALU.mult, op1=ALU.add),
              r=["sti"] + R, w=W)
        P.dve(lambda e: e.tensor_scalar(out=t(2), in0=t(2), scalar1=math.pi, scalar2=-math.pi, op0=ALU.min,
                                        op1=ALU.max), r=R, w=W)
        P.act(lambda e: e.activation(out=t(3), in_=t(2), func=AF.Sin), r=R, w=W)
        P.act(lambda e: e.activation(out=t(4), in_=t(2), func=AF.Abs), r=R, w=W)
        P.act(lambda e: e.activation(out=t(5), in_=t(4), func=AF.Sin, scale=-1.0, bias=halfpi[:, 0:1]),
              r=R + ["halfpi"], w=W)
        P.dve(lambda e: e.tensor_tensor(out=t(6), in0=rho, in1=t(5), op=ALU.mult), r=R + ["srho"], w=W)
        P.dve(lambda e: e.tensor_tensor(out=t(7), in0=rho, in1=t(3), op=ALU.mult), r=R + ["srho"], w=W)
        P.dve(lambda e: e.tensor_scalar(out=t(6), in0=t(6), scalar1=-1.0, scalar2=None, op0=ALU.add), r=R, w=W)
        P.dve(lambda e: e.tensor_tensor(out=t(8), in0=a_re, in1=a_re, op=ALU.mult), r=R, w=W)
        P.dve(lambda e: e.tensor_tensor(out=t(9), in0=a_im, in1=a_im, op=ALU.mult), r=R, w=W)
        P.dve(lambda e: e.tensor_tensor(out=t(8), in0=t(8), in1=t(9), op=ALU.add), r=R, w=W)
        P.dve(lambda e: e.reciprocal(out=t(8), in_=t(8)), r=R, w=W)
        P.dve(lambda e: e.tensor_tensor(out=t(9), in0=t(6), in1=a_re, op=ALU.mult), r=R, w=W)
        P.dve(lambda e: e.tensor_tensor(out=t(10), in0=t(7), in1=a_im, op=ALU.mult), r=R, w=W)
        P.dve(lambda e: e.tensor_tensor(out=t(9), in0=t(9), in1=t(10), op=ALU.add), r=R, w=W)
        P.dve(lambda e: e.tensor_tensor(out=cre, in0=t(9), in1=t(8), op=ALU.mult), r=R, w=["scre"])
        P.dve(lambda e: e.tensor_tensor(out=t(11), in0=t(7), in1=a_re, op=ALU.mult), r=R, w=W)
        P.dve(lambda e: e.tensor_tensor(out=t(12), in0=t(6), in1=a_im, op=ALU.mult), r=R, w=W)
        P.dve(lambda e: e.tensor_tensor(out=t(11), in0=t(11), in1=t(12), op=ALU.subtract), r=R, w=W)
        P.dve(lambda e: e.tensor_tensor(out=cim, in0=t(11), in1=t(8), op=ALU.mult), r=R, w=["scim"])
        P.dve(lambda e: e.tensor_scalar(out=ncim, in0=cim, scalar1=-1.0, scalar2=None, op0=ALU.mult),
              r=["scim"], w=["sncim"])
        for c in range(2):
            for r0 in range(0, 32, 8):
                P.dma(CT[:, c, r0:r0 + 8, :], ssm_cblk[j, c, r0:r0 + 8].rearrange("t p n -> p t n"),
                      w=["sCT"], q="pool")
        braw = [ar.alloc([2, 128], F32) for _ in range(2)]
        bb = [ar.alloc([2, 128], BF16) for _ in range(2)]
        bt1 = [ar.alloc([128], F32) for _ in range(2)]
        for r in range(32):
            s = r % 2
            P.dma(braw[s], ssm_bblk[j, :, r].rearrange("c p n -> p c n"), w=["braw%d" % s])
            P.dve(lambda e, s=s, r=r: e.tensor_scalar(out=bt1[s], in0=braw[s][:, 1, :], scalar1=ncim[:, r:r + 1],
                                                      scalar2=None, op0=ALU.mult),
                  r=["braw%d" % s, "sncim"], w=["bt1%d" % s])
            P.dve(lambda e, s=s, r=r: e.scalar_tensor_tensor(out=bb[s][:, 0, :], in0=braw[s][:, 0, :],
                                                             scalar=cre[:, r:r + 1], in1=bt1[s], op0=ALU.mult,
                                                             op1=ALU.add),
                  r=["braw%d" % s, "scre", "bt1%d" % s], w=["bb%d" % s])
            P.dve(lambda e, s=s, r=r: e.tensor_scalar(out=bt1[s], in0=braw[s][:, 0, :], scalar1=cim[:, r:r + 1],
                                                      scalar2=None, op0=ALU.mult),
                  r=["braw%d" % s, "scim", "bb%d" % s], w=["bt1%d" % s])
            P.dve(lambda e, s=s, r=r: e.scalar_tensor_tensor(out=bb[s][:, 1, :], in0=braw[s][:, 1, :],
                                                             scalar=cre[:, r:r + 1], in1=bt1[s], op0=ALU.mult,
                                                             op1=ALU.add),
                  r=["braw%d" % s, "scre", "bt1%d" % s], w=["bb%d" % s])
            pst = ps[s][:, 0:128].bitcast(BF16).rearrange("p (c n) -> p c n", c=2)
            for c in range(2):
                P.pe(lambda e, s=s, c=c, pst=pst: e.transpose(out=pst[:, c, :], in_=bb[s][:, c, :], identity=ident),
                     r=["bb%d" % s, "ident"], w=[PN(s)])
            P.act(lambda e, r=r, pst=pst: e.activation(out=BT[:, :, r, :], in_=pst, func=AF.Copy),
                  r=[PN(s)], w=["sBT"])
        stg = [ar.alloc([2, 256], BF16) for _ in range(2)]
        tw = [dict(ph=ar.alloc([256], F32), ki=ar.alloc([256], I32), ab=ar.alloc([256], F32)) for _ in range(2)]
        tht = ar.alloc([2, NCH, 32], F32)
        for d in range(2):
            for oi, (s0, _, isc) in enumerate(ch256):
                tau0 = float(s0) if d == 0 else (float(CTX - 1 - s0) if isc else float(CTX + T - 1 - s0))
                P.dve(lambda e, d=d, oi=oi, tau0=tau0: e.tensor_scalar(
                    out=tht[:, d, oi, :], in0=th, scalar1=tau0, scalar2=None, op0=ALU.mult),
                    r=["sth"], w=["tht"])
        gi = 0
        for r in range(32):
            d = r // 16
            thd = th if d == 0 else nth
            for cidx in range(NCH):
                s_ = gi % 2
                gi += 1
                ph, ki, ab = tw[s_]["ph"], tw[s_]["ki"], tw[s_]["ab"]
                tn = "tw%d" % s_
                P.dve(lambda e, r=r, d=d, cidx=cidx, ph=ph, thd=thd: e.tensor_scalar(
                    out=ph, in0=iot, scalar1=thd[:, r:r + 1], scalar2=tht[:, d, cidx, r:r + 1],
                    op0=ALU.mult, op1=ALU.add), r=["iot", "sth", "snth", "tht"], w=[tn + "ph"])
                P.dve(lambda e, ph=ph, ki=ki: e.tensor_scalar(out=ki, in0=ph, scalar1=1.0 / TWO_PI, scalar2=None,
                                                              op0=ALU.mult), r=[tn + "ph"], w=[tn + "ki"])
                P.dve(lambda e, ph=ph, ki=ki: e.scalar_tensor_tensor(out=ph, in0=ki, scalar=-CW1, in1=ph,
                                                                     op0=ALU.mult, op1=ALU.add),
                      r=[tn + "ph", tn + "ki"], w=[tn + "ph"])
                P.dve(lambda e, ph=ph, ki=ki: e.scalar_tensor_tensor(out=ph, in0=ki, scalar=-CW2, in1=ph,
                                                                     op0=ALU.mult, op1=ALU.add),
                      r=[tn + "ph", tn + "ki"], w=[tn + "ph"])
                P.pool(lambda e, ph=ph: e.tensor_scalar(out=ph, in0=ph, scalar1=math.pi, scalar2=-math.pi,
                                                        op0=ALU.min, op1=ALU.max), r=[tn + "ph"], w=[tn + "ph"])
                P.act(lambda e, ph=ph, s_=s_: e.activation(out=stg[s_][:, 1, :], in_=ph, func=AF.Sin),
                      r=[tn + "ph"], w=["stg%d" % s_])
                P.act(lambda e, ph=ph, ab=ab: e.activation(out=ab, in_=ph, func=AF.Abs), r=[tn + "ph"], w=[tn + "ab"])
                P.act(lambda e, ab=ab, s_=s_: e.activation(out=stg[s_][:, 0, :], in_=ab, func=AF.Sin, scale=-1.0,
                                                           bias=halfpi[:, 0:1]),
                      r=[tn + "ab", "halfpi"], w=["stg%d" % s_])
                dstv = cstab[r // 4, cidx].rearrange("p (a c t) -> p a c t", a=4, c=2)[:, r % 4, :, :]
                P.dma(dstv, stg[s_], r=["stg%d" % s_], w=["cstab"])
        P.fence()
        ar.reset(mk)
        w.update(rho=rho, th=th, nth=nth, BT=BT, CT=CT)

    def even_mixer(l, j, b, last, w):
        mk0 = ar.off
        zbuf = ar.alloc([8 * T], BF16)
        zT = zbuf.rearrange("p (a b) -> p a b", a=8)
        yT = ar.alloc([8, T], BF16)
        mark2 = ar.off
        win = ar.alloc([8, D], BF16)
        load_w(win, ev_w_in[j], "win", 8)
        nb = NormBufs()
        mx = load_mod_tiles(l, b, 0, "x", nb.ng, "GS")
        mc = load_mod_tiles(l, NB, 0, "c", nb.ng, "GS")
        hTc = [ar.alloc([8, 512], BF16) for _ in range(2)]
        src = xin if l == 0 else xs
        for ci, (s0, L, isc) in enumerate(chunks):
            hs = ci % 2
            mm_ = mc if isc else mx
            for ti in range(L // 128):
                r0 = b * T + s0 + ti * 128
                s = norm_tile(nb, src[r0:r0 + 128, :], XN(r0), mm_["G"], mm_["S"], "c" if isc else "x",
                              hTc[hs][:, :, ti * 128:(ti + 1) * 128], "hTc%d" % hs)
                if l == 0:
                    P.dma(xs[r0:r0 + 128, :], nb.xt[s], r=["xt%d" % s], w=[XN(r0)])
            for m in range(8):
                pb = m % 4
                for k in range(8):
                    P.pe(lambda e, m=m, k=k, pb=pb, hs=hs, L=L: e.matmul(
                        ps[pb][:, 0:L], lhsT=win[:, k, m * 128:(m + 1) * 128], rhs=hTc[hs][:, k, 0:L],
                        start=(k == 0), stop=(k == 7)), r=["win", "hTc%d" % hs], w=[PN(pb)])
                if m % 2 == 0:
                    P.act(lambda e, m=m, pb=pb, s0=s0, L=L: e.activation(out=zT[:, m, s0:s0 + L], in_=ps[pb][:, 0:L],
                                                                         func=AF.Copy),
                          r=[PN(pb)], w=["zT%d.%d" % (m, ci)])
                else:
                    P.dve(lambda e, m=m, pb=pb, s0=s0, L=L: e.tensor_copy(out=zT[:, m, s0:s0 + L], in_=ps[pb][:, 0:L]),
                          r=[PN(pb)], w=["zT%d.%d" % (m, ci)])
        P.fence()
        ar.reset(mark2)
        ABt = ar.alloc([NT, 4, 256], BF16)
        dring = [ar.alloc([2, 512], BF16) for _ in range(6)]
        zall = ["zT%d.%d" % (m, ci) for m in range(8) for ci in range(len(chunks))]
        for tt in range(NT):
            for gp in range(2):
                pb = 4 + gp
                pv = ps[pb][:, :].rearrange("p (g n) -> p g n", g=2)
                for gi in range(2):
                    g = gp * 2 + gi
                    P.pe(lambda e, g=g, gi=gi, tt=tt, pv=pv: e.matmul(
                        pv[:, gi, :], lhsT=zT[:, g, tt * 128:(tt + 1) * 128], rhs=csc, start=True, stop=True),
                        r=zall + ["csc"], w=[PN(pb)])
                if gp == 0:
                    P.act(lambda e, tt=tt, gp=gp, pv=pv: e.activation(out=ABt[:, tt, gp * 2:gp * 2 + 2, :], in_=pv,
                                                                      func=AF.Copy),
                          r=[PN(pb)], w=["ABt"])
                else:
                    P.dve(lambda e, tt=tt, gp=gp, pv=pv: e.tensor_copy(out=ABt[:, tt, gp * 2:gp * 2 + 2, :], in_=pv),
                          r=[PN(pb)], w=["ABt"])
        di = 0
        for (s0, L, isc) in chunks:
            if last and isc:
                continue
            tts = list(range(NTC)) if isc else list(range(NTC, NT))
            ntt = len(tts)
            for ii, tt in enumerate(tts):
                sl = di % 6
                di += 1
                srcd = dftc[0, ii] if isc else dftx[(s0 - CTX) // 512, ii]
                P.dma(dring[sl][:, :, 0:L], srcd, w=["dr%d" % sl])
                for g in range(4):
                    for c in range(2):
                        P.pe(lambda e, g=g, c=c, tt=tt, sl=sl, L=L, ii=ii, ntt=ntt: e.matmul(
                            ps[g][:, 0:L], lhsT=ABt[:, tt, g, c * 128:(c + 1) * 128], rhs=dring[sl][:, c, 0:L],
                            start=(ii == 0 and c == 0), stop=(ii == ntt - 1 and c == 1)),
                            r=["ABt", "dr%d" % sl], w=[PN(g)])
            for g in range(4):
                if g % 2 == 0:
                    P.act(lambda e, g=g, s0=s0, L=L: e.activation(out=yT[:, g, s0:s0 + L], in_=ps[g][:, 0:L],
                                                                  func=AF.Copy), r=[PN(g)], w=["yT"])
                else:
                    P.dve(lambda e, g=g, s0=s0, L=L: e.tensor_copy(out=yT[:, g, s0:s0 + L], in_=ps[g][:, 0:L]),
                          r=[PN(g)], w=["yT"])
        P.fence()
        ar.reset(mark2)
        yacc = ar.alloc([4, T], F32)
        wglu = ar.alloc([4, 512], BF16)
        load_w(wglu, ssm_wglu[j], "wglu", 4)
        dsk = ar.alloc([4], F32)
        bglu = ar.alloc([4], F32)
        P.dma(dsk, ssm_d[j], w=["dsk"])
        P.dma(bglu, ssm_bglu[j], w=["bglu"])
        L = 256
        NW = 2
        wk = []
        for i in range(NW):
            wk.append(dict(tab=ar.alloc([4, 2, L], BF16), pb=[ar.alloc([4, L], BF16) for _ in range(2)],
                           m=[ar.alloc([4, L], BF16) for _ in range(4)],
                           hr=ar.alloc([4, L], BF16), hi=ar.alloc([4, L], BF16)))
        gst = [ar.alloc([2, 4, L], BF16) for _ in range(2)]
        BT, CT, rho = w["BT"], w["CT"], w["rho"]
        P1 = psall[:, 0:1024].rearrange("p (a t) -> p a t", a=4)
        P2 = psall[:, 1024:2048].rearrange("p (a t) -> p a t", a=4)
        PN1, PN2 = [PN(0), PN(1)], [PN(2), PN(3)]
        nctx = CTX // 256
        groups = []
        for d in range(2):
            idx = list(range(NCH))
            order = idx if d == 0 else idx[:nctx][::-1] + idx[nctx:][::-1]
            for kt in range(4):
                for oi, cidx in enumerate(order):
                    groups.append((d, kt, oi, cidx))

        def ph_a(gi):
            d, kt, oi, cidx = groups[gi]
            s0 = ch256[cidx][0]
            k_ = wk[gi % NW]
            wn = "wk%d" % (gi % NW)
            tab, pb, mm = k_["tab"], k_["pb"], k_["m"]
            cs, sn = tab[:, :, 0, :], tab[:, :, 1, :]
            P.dma(tab.rearrange("p a c t -> p (a c t)"), cstab[d * 4 + kt, cidx], r=["cstab"], w=[wn + "tab"])
            for rr in range(4):
                r = d * 16 + kt * 4 + rr
                P.pe(lambda e, r=r, rr=rr, kt=kt, s0=s0: e.matmul(
                    P1[:, rr, :], lhsT=BT[:, 0, r, :], rhs=zT[:, 4 + kt, s0:s0 + L], start=True, stop=True),
                    r=["sBT"] + zall, w=[PN(rr // 2)])
                P.pe(lambda e, r=r, rr=rr, kt=kt, s0=s0: e.matmul(
                    P2[:, rr, :], lhsT=BT[:, 1, r, :], rhs=zT[:, 4 + kt, s0:s0 + L], start=True, stop=True),
                    r=["sBT"] + zall, w=[PN(2 + rr // 2)])
            P.act(lambda e, pb=pb: e.activation(out=pb[0], in_=P1, func=AF.Copy), r=PN1, w=[wn + "pb0"])
            P.act(lambda e, pb=pb: e.activation(out=pb[1], in_=P2, func=AF.Copy), r=PN2, w=[wn + "pb1"])

        def ph_a2(gi):
            k_ = wk[gi % NW]
            wn = "wk%d" % (gi % NW)
            tab, pb, mm = k_["tab"], k_["pb"], k_["m"]
            cs, sn = tab[:, :, 0, :], tab[:, :, 1, :]
            P.dve(lambda e, cs=cs, mm=mm, pb=pb: e.tensor_tensor(out=mm[0], in0=pb[0], in1=cs, op=ALU.mult),
                  r=[wn + "pb0", wn + "tab"], w=[wn + "m0"])
            P.dve(lambda e, sn=sn, mm=mm, pb=pb: e.tensor_tensor(out=mm[1], in0=pb[1], in1=sn, op=ALU.mult),
                  r=[wn + "pb1", wn + "tab"], w=[wn + "m1"])
            P.dve(lambda e, mm=mm: e.tensor_tensor(out=mm[0], in0=mm[0], in1=mm[1], op=ALU.add),
                  r=[wn + "m0", wn + "m1"], w=[wn + "m0"])
            P.dve(lambda e, cs=cs, mm=mm, pb=pb: e.tensor_tensor(out=mm[2], in0=pb[1], in1=cs, op=ALU.mult),
                  r=[wn + "pb1", wn + "tab"], w=[wn + "m2"])
            P.dve(lambda e, sn=sn, mm=mm, pb=pb: e.tensor_tensor(out=mm[3], in0=pb[0], in1=sn, op=ALU.mult),
                  r=[wn + "pb0", wn + "tab"], w=[wn + "m3"])
            P.dve(lambda e, mm=mm: e.tensor_tensor(out=mm[2], in0=mm[2], in1=mm[3], op=ALU.subtract),
                  r=[wn + "m2", wn + "m3"], w=[wn + "m2"])

        def ph_b(gi):
            d, kt, oi, cidx = groups[gi]
            k_ = wk[gi % NW]
            wn = "wk%d" % (gi % NW)
            mm = k_["m"]
            g = gst[gi % 2]
            gn = "gst%d" % (gi % 2)
            rv = (lambda a: a) if d == 0 else (lambda a: a[:, ::-1])
            for rr in range(4):
                r = d * 16 + kt * 4 + rr
                for c in range(2):
                    if oi == 0:
                        init = 0.0
                        rdi = []
                    else:
                        pg = gst[(gi - 1) % 2]
                        init = (pg[:, c, rr, L - 1:L] if d == 0 else pg[:, c, rr, 0:1])
                        rdi = ["gst%d" % ((gi - 1) % 2)]
                    P.dve(lambda e, c=c, rr=rr, g=g, mm=mm, init=init, r=r, rv=rv: e.tensor_tensor_scan(
                        out=rv(g[:, c, rr, :]), data0=rho[:, r:r + 1].to_broadcast([128, L]),
                        data1=rv(mm[2 * c][:, rr, :]), initial=init, op0=ALU.mult, op1=ALU.add),
                        r=["srho", wn + "m%d" % (2 * c)] + rdi, w=[gn])

        def ph_c(gi):
            d, kt, oi, cidx = groups[gi]
            s0 = ch256[cidx][0]
            k_ = wk[gi % NW]
            wn = "wk%d" % (gi % NW)
            tab, mm, hr, hi = k_["tab"], k_["m"], k_["hr"], k_["hi"]
            cs, sn = tab[:, :, 0, :], tab[:, :, 1, :]
            g = gst[gi % 2]
            gn = "gst%d" % (gi % 2)
            ypb = 4 + (gi % 2)
            P.dve(lambda e, g=g, cs=cs, mm=mm: e.tensor_tensor(out=mm[1], in0=g[:, 0], in1=cs, op=ALU.mult),
                  r=[gn, wn + "tab", wn + "m1"], w=[wn + "m1"])
            P.dve(lambda e, g=g, sn=sn, mm=mm: e.tensor_tensor(out=mm[3], in0=g[:, 1], in1=sn, op=ALU.mult),
                  r=[gn, wn + "tab", wn + "m3"], w=[wn + "m3"])
            P.dve(lambda e, mm=mm, hr=hr: e.tensor_tensor(out=hr, in0=mm[1], in1=mm[3], op=ALU.subtract),
                  r=[wn + "m1", wn + "m3"], w=[wn + "hr"])
            P.dve(lambda e, g=g, sn=sn, mm=mm: e.tensor_tensor(out=mm[0], in0=g[:, 0], in1=sn, op=ALU.mult),
                  r=[gn, wn + "tab", wn + "m0"], w=[wn + "m0"])
            P.dve(lambda e, g=g, cs=cs, mm=mm: e.tensor_tensor(out=mm[2], in0=g[:, 1], in1=cs, op=ALU.mult),
                  r=[gn, wn + "tab", wn + "m2"], w=[wn + "m2"])
            P.dve(lambda e, mm=mm, hi=hi: e.scalar_tensor_tensor(
                out=hi, in0=mm[0], scalar=-1.0, in1=mm[2], op0=ALU.mult, op1=ALU.subtract),
                r=[wn + "m0", wn + "m2"], w=[wn + "hi"])
            for rr in range(4):
                r = d * 16 + kt * 4 + rr
                P.pe(lambda e, r=r, hr=hr, rr=rr, ypb=ypb: e.matmul(
                    ps[ypb][:, 0:L], lhsT=CT[:, 0, r, :], rhs=hr[:, rr, :], start=(rr == 0), stop=False),
                    r=["sCT", wn + "hr"], w=[PN(ypb)])
                P.pe(lambda e, r=r, hi=hi, rr=rr, ypb=ypb: e.matmul(
                    ps[ypb][:, 0:L], lhsT=CT[:, 1, r, :], rhs=hi[:, rr, :], start=False, stop=(rr == 3)),
                    r=["sCT", wn + "hi"], w=[PN(ypb)])
            yn = "yacc%d" % kt
            if d == 0:
                P.act(lambda e, kt=kt, s0=s0, ypb=ypb: e.activation(
                    out=yacc[:, kt, s0:s0 + L], in_=ps[ypb][:, 0:L], func=AF.Copy),
                    r=[PN(ypb)], w=[yn])
            else:
                P.dve(lambda e, kt=kt, s0=s0, ypb=ypb: e.tensor_tensor(
                    out=yacc[:, kt, s0:s0 + L], in0=ps[ypb][:, 0:L], in1=yacc[:, kt, s0:s0 + L], op=ALU.add),
                    r=[PN(ypb), yn], w=[yn])

        ph_a(0)
        ph_a2(0)
        for gi in range(len(groups)):
            if gi + 1 < len(groups):
                ph_a(gi + 1)
            ph_b(gi)
            if gi + 1 < len(groups):
                ph_a2(gi + 1)
            ph_c(gi)
        yb = yT[:, 4:8, :]
        for kt in range(4):
            P.dve(lambda e, kt=kt: e.scalar_tensor_tensor(
                out=yacc[:, kt, :], in0=zT[:, 4 + kt, :], scalar=dsk[:, kt:kt + 1], in1=yacc[:, kt, :],
                op0=ALU.mult, op1=ALU.add), r=zall + ["dsk", "yacc%d" % kt], w=["yacc%d" % kt])
            P.act(lambda e, kt=kt: e.activation(out=yb[:, kt, :], in_=yacc[:, kt, :], func=AF.Gelu_apprx_tanh),
                  r=["yacc%d" % kt], w=["yT"])
        sg = [ar.alloc([512], BF16) for _ in range(4)]
        for (s0, Lc, isc) in chunks:
            if last and isc:
                continue
            for m in range(4):
                pb = m
                for k in range(4):
                    P.pe(lambda e, m=m, k=k, pb=pb, s0=s0, Lc=Lc: e.matmul(
                        ps[pb][:, 0:Lc], lhsT=wglu[:, k, m * 128:(m + 1) * 128], rhs=yb[:, k, s0:s0 + Lc],
                        start=(k == 0), stop=(k == 3)), r=["wglu", "yT"], w=[PN(pb)])
                P.act(lambda e, m=m, pb=pb, Lc=Lc: e.activation(out=sg[m][:, 0:Lc], in_=ps[pb][:, 0:Lc],
                                                                func=AF.Sigmoid, bias=bglu[:, m:m + 1]),
                      r=[PN(pb), "bglu"], w=["sg%d" % m])
            for m in range(4):
                P.dve(lambda e, m=m, s0=s0, Lc=Lc: e.tensor_tensor(
                    out=yb[:, m, s0:s0 + Lc], in0=yb[:, m, s0:s0 + Lc], in1=sg[m][:, 0:Lc], op=ALU.mult),
                    r=["yT"] + ["sg%d" % i for i in range(4)], w=["yT"])
        P.fence()
        ar.reset(mark2)
        out_proj(l, b, last, yT, "yT", ev_w_out[j])
        ar.reset(mk0)

    def odd_mixer(l, j, b, last, w):
        mk0 = ar.off
        qk = ar.alloc([16, T], BF16)
        mark2 = ar.off
        win = ar.alloc([8, 3 * D], BF16)
        load_w(win, od_w_in[j], "win", 8)
        nb = NormBufs()
        mx = load_mod_tiles(l, b, 0, "x", nb.ng, "GS")
        mc = load_mod_tiles(l, NB, 0, "c", nb.ng, "GS")
        hTc = [ar.alloc([8, 256], BF16) for _ in range(2)]
        sq = [ar.alloc([256], BF16) for _ in range(2)]
        rs = [ar.alloc([256], F32) for _ in range(2)]
        qn = [ar.alloc([256], BF16) for _ in range(2)]
        tq = [ar.alloc([256], F32) for _ in range(2)]
        uq = [ar.alloc([256], F32) for _ in range(2)]
        vst = [ar.alloc([1024], BF16) for _ in range(2)]
        rope = w["rope"]
        it = 0
        L = 256
        for ci, (s0, _, isc) in enumerate(ch256):
            hs = ci % 2
            mm_ = mc if isc else mx
            for ti in range(2):
                r0 = b * T + s0 + ti * 128
                norm_tile(nb, xs[r0:r0 + 128, :], XN(r0), mm_["G"], mm_["S"], "c" if isc else "x",
                          hTc[hs][:, :, ti * 128:(ti + 1) * 128], "hTc%d" % hs)
            for ti in range(2):
                vs = (s0 // 128 + ti) % 2
                for hf in range(2):
                    pb = 4 + hf
                    for k in range(8):
                        P.pe(lambda e, k=k, hf=hf, pb=pb, hs=hs, ti=ti: e.matmul(
                            ps[pb][:, :], lhsT=hTc[hs][:, k, ti * 128:(ti + 1) * 128],
                            rhs=win[:, k, 2048 + hf * 512:2048 + (hf + 1) * 512], start=(k == 0), stop=(k == 7)),
                            r=["win", "hTc%d" % hs], w=[PN(pb)])
                    if hf == 0:
                        P.act(lambda e, vs=vs, pb=pb: e.activation(out=vst[vs][:, 0:512], in_=ps[pb][:, :],
                                                                   func=AF.Copy), r=[PN(pb)], w=["vst%d" % vs])
                    else:
                        P.dve(lambda e, vs=vs, pb=pb: e.tensor_copy(out=vst[vs][:, 512:1024], in_=ps[pb][:, :]),
                              r=[PN(pb)], w=["vst%d" % vs])
                t0 = s0 + ti * 128
                P.dma(vtok[t0:t0 + 128, :], vst[vs], r=["vst%d" % vs], w=["vtok"])
            def q_proj(m, hs=hs):
                pq = m % 2
                for k in range(8):
                    P.pe(lambda e, m=m, k=k, pq=pq, hs=hs: e.matmul(
                        ps[pq][:, 0:L], lhsT=win[:, k, m * 128:(m + 1) * 128], rhs=hTc[hs][:, k, :],
                        start=(k == 0), stop=(k == 7)), r=["win", "hTc%d" % hs], w=[PN(pq)])

            def q_norm(m, ci=ci, s0=s0, isc=isc):
                s = m % 2
                pq = s
                pn_ = 2 + s
                P.act(lambda e, s=s, pq=pq: e.activation(out=sq[s], in_=ps[pq][:, 0:L], func=AF.Square),
                      r=[PN(pq)], w=["sq%d" % s])
                P.pe(lambda e, s=s, pn_=pn_: e.matmul(ps[pn_][:, 0:L], lhsT=bones, rhs=sq[s], start=True, stop=True),
                     r=["bones", "sq%d" % s], w=[PN(pn_)])
                P.dve(lambda e, s=s, pn_=pn_: e.tensor_scalar(out=rs[s], in0=ps[pn_][:, 0:L], scalar1=1.0 / 64,
                                                              scalar2=EPS, op0=ALU.mult, op1=ALU.add),
                      r=[PN(pn_)], w=["rs%d" % s])
                P.act(lambda e, s=s: e.activation(out=rs[s], in_=rs[s], func=AF.Sqrt), r=["rs%d" % s], w=["rs%d" % s])
                P.dve(lambda e, s=s: e.reciprocal(out=rs[s], in_=rs[s]), r=["rs%d" % s], w=["rs%d" % s])
                gcol = 0 if m < 8 else 1
                dst = qk[:, m, s0:s0 + L]
                dn = "qk%d.%d" % (m, ci)
                tgt = dst if isc else qn[s]
                tn = dn if isc else "qn%d" % s
                P.dve(lambda e, s=s, pq=pq, gcol=gcol, tgt=tgt: e.scalar_tensor_tensor(
                    out=tgt, in0=ps[pq][:, 0:L], scalar=w["gqk"][:, gcol:gcol + 1], in1=rs[s],
                    op0=ALU.mult, op1=ALU.mult), r=[PN(pq), "gqk", "rs%d" % s], w=[tn])

            def q_rope(m, ci=ci, s0=s0, isc=isc):
                if isc:
                    return
                s = m % 2
                pn_ = 2 + s
                dst = qk[:, m, s0:s0 + L]
                dn = "qk%d.%d" % (m, ci)
                x0 = s0 - CTX
                P.pe(lambda e, s=s, pn_=pn_: e.matmul(ps[pn_][:, 0:L], lhsT=rmat, rhs=qn[s], start=True, stop=True),
                     r=["rmat", "qn%d" % s], w=[PN(pn_)])
                P.dve(lambda e, s=s, pn_=pn_, x0=x0: e.tensor_tensor(
                    out=tq[s], in0=ps[pn_][:, 0:L], in1=rope[:, 1, x0:x0 + L], op=ALU.mult),
                    r=[PN(pn_), "rope"], w=["tq%d" % s])
                P.pool(lambda e, s=s, x0=x0: e.tensor_tensor(
                    out=uq[s], in0=qn[s], in1=rope[:, 0, x0:x0 + L], op=ALU.mult),
                    r=["qn%d" % s, "rope"], w=["uq%d" % s])
                P.pool(lambda e, s=s, dst=dst: e.tensor_tensor(out=dst, in0=tq[s], in1=uq[s], op=ALU.add),
                       r=["tq%d" % s, "uq%d" % s], w=[dn])

            q_proj(0)
            for m in range(16):
                if m + 1 < 16:
                    q_proj(m + 1)
                q_norm(m)
                if m >= 1:
                    q_rope(m - 1)
            q_rope(15)
        P.fence()
        ar.reset(mark2)
        yT = ar.alloc([8, T], BF16)
        mark3 = ar.off
        vh = [ar.alloc([NT, 128], BF16) for _ in range(2)]
        eb = [ar.alloc([512], BF16) for _ in range(4)]
        acc = [[[ar.alloc([512], F32) for _ in range(2)] for _ in range(2)] for _ in range(2)]
        accb = [ar.alloc([512], BF16) for _ in range(2)]
        rc = [ar.alloc([512], F32) for _ in range(2)]
        oa = [ar.alloc([512], F32) for _ in range(2)]
        of = ar.alloc([512], F32)
        o2 = ar.alloc([512], BF16)
        rs2 = ar.alloc([512], F32)
        qall = ["qk%d.%d" % (m, ci) for m in range(16) for ci in range(len(ch256))]
        ei = 0
        vt3 = vtok.rearrange("(t p) f -> p t f", p=128)
        units = [(h, s0, Lc, isc) for h in range(8) for (s0, Lc, isc) in chunks if not (last and isc)]

        def make_epi(u, h, s0, Lc):
            up = u % 2
            pvb = [2, 3] if up == 0 else [4, 5]

            def part1():
                for m in range(2):
                    P.pool(lambda e, m=m: e.tensor_tensor(out=accb[m][:, 0:Lc], in0=acc[up][m][0][:, 0:Lc],
                                                          in1=acc[up][m][1][:, 0:Lc], op=ALU.add),
                           r=["acc%d.%d.0" % (up, m), "acc%d.%d.1" % (up, m)], w=["accb%d" % m])
                    P.pe(lambda e, m=m: e.matmul(ps[6 + m][:, 0:Lc], lhsT=ones, rhs=accb[m][:, 0:Lc],
                                                 start=True, stop=True),
                         r=["ones", "accb%d" % m], w=[PN(6 + m)])
                for m in range(2):
                    P.dve(lambda e, m=m: e.reciprocal(out=rc[m][:, 0:Lc], in_=ps[6 + m][:, 0:Lc]),
                          r=[PN(6 + m)], w=["rc%d" % m])
                    P.dve(lambda e, m=m: e.tensor_tensor(out=oa[m][:, 0:Lc], in0=ps[pvb[m]][:, 0:Lc],
                                                         in1=rc[m][:, 0:Lc], op=ALU.mult),
                          r=[PN(pvb[m]), "rc%d" % m], w=["oa%d" % m])
                P.dve(lambda e: e.scalar_tensor_tensor(out=of[:, 0:Lc], in0=oa[1][:, 0:Lc],
                                                       scalar=w["nlam"][:, 0:1], in1=oa[0][:, 0:Lc],
                                                       op0=ALU.mult, op1=ALU.add),
                      r=["oa0", "oa1", "nlam"], w=["of"])
                P.act(lambda e: e.activation(out=o2[:, 0:Lc], in_=of[:, 0:Lc], func=AF.Square),
                      r=["of"], w=["o2"])

            def part2():
                P.pe(lambda e: e.matmul(ps[6][:, 0:Lc], lhsT=ones, rhs=o2[:, 0:Lc], start=True, stop=True),
                     r=["ones", "o2"], w=[PN(6)])
                P.dve(lambda e: e.tensor_scalar(out=rs2[:, 0:Lc], in0=ps[6][:, 0:Lc], scalar1=1.0 / 128,
                                                scalar2=EPS, op0=ALU.mult, op1=ALU.add), r=[PN(6)], w=["rs2"])
                P.act(lambda e: e.activation(out=rs2[:, 0:Lc], in_=rs2[:, 0:Lc], func=AF.Sqrt),
                      r=["rs2"], w=["rs2"])
                P.dve(lambda e: e.reciprocal(out=rs2[:, 0:Lc], in_=rs2[:, 0:Lc]), r=["rs2"], w=["rs2"])
                P.dve(lambda e: e.scalar_tensor_tensor(
                    out=yT[:, h, s0:s0 + Lc], in0=of[:, 0:Lc], scalar=w["ghs"][:, 0:1], in1=rs2[:, 0:Lc],
                    op0=ALU.mult, op1=ALU.mult), r=["of", "ghs", "rs2"], w=["yT"])

            return [part1, part2]

        pending = []
        cur_h = -1
        for u, (h, s0, Lc, isc) in enumerate(units):
            vs = h % 2
            up = u % 2
            pvb = [2, 3] if up == 0 else [4, 5]
            if h != cur_h:
                cur_h = h
                P.dma(vh[vs], vt3[:, :, h * 128:(h + 1) * 128], r=["vtok"], w=["vh%d" % vs])
            kts = list(range(NTC)) if isc else list(range(NT))
            nk = len(kts)
            items = [(m, ki, kt) for m in range(2) for ki, kt in enumerate(kts)]

            def s_mm(ii, h=h, s0=s0, Lc=Lc, items=items):
                m, ki, kt = items[ii]
                sb = ii % 2
                P.pe(lambda e, m=m, kt=kt, sb=sb: e.matmul(
                    ps[sb][:, 0:Lc], lhsT=qk[64 * m:64 * m + 64, 8 + h, kt * 128:(kt + 1) * 128],
                    rhs=qk[64 * m:64 * m + 64, h, s0:s0 + Lc], start=True, stop=True),
                    r=qall, w=[PN(sb)])

            s_mm(0)
            for ii, (m, ki, kt) in enumerate(items):
                if ii + 1 < len(items):
                    s_mm(ii + 1)
                if pending and ii == 2:
                    pending.pop(0)()
                if pending and ii == 10:
                    pending.pop(0)()
                sb = ii % 2
                es = ei % 4
                ei += 1
                P.act(lambda e, sb=sb, es=es, Lc=Lc: e.activation(out=eb[es][:, 0:Lc], in_=ps[sb][:, 0:Lc],
                                                                  func=AF.Exp, scale=0.125),
                      r=[PN(sb)], w=["eb%d" % es])
                P.pe(lambda e, m=m, kt=kt, es=es, vs=vs, Lc=Lc, ki=ki, nk=nk, pvb=pvb: e.matmul(
                    ps[pvb[m]][:, 0:Lc], lhsT=vh[vs][:, kt, :], rhs=eb[es][:, 0:Lc], start=(ki == 0),
                    stop=(ki == nk - 1)), r=["vh%d" % vs, "eb%d" % es], w=[PN(pvb[m])])
                a_ = acc[up][m][ki % 2]
                an = "acc%d.%d.%d" % (up, m, ki % 2)
                eng = P.pool if ki % 2 == 0 else P.dve
                if ki < 2:
                    eng(lambda e, a_=a_, es=es, Lc=Lc: e.tensor_copy(out=a_[:, 0:Lc], in_=eb[es][:, 0:Lc]),
                        r=["eb%d" % es], w=[an])
                else:
                    eng(lambda e, a_=a_, es=es, Lc=Lc: e.tensor_tensor(out=a_[:, 0:Lc], in0=a_[:, 0:Lc],
                                                                      in1=eb[es][:, 0:Lc], op=ALU.add),
                        r=["eb%d" % es, an], w=[an])
            while pending:
                pending.pop(0)()
            pending = make_epi(u, h, s0, Lc)
        while pending:
            pending.pop(0)()
        P.fence()
        ar.reset(mark3)
        out_proj(l, b, last, yT, "yT", od_w_out[j])
        ar.reset(mk0)

    def mlp_stage(l, last):
        ar.reset()
        w1 = ar.alloc([8, DFF], BF16)
        w2 = ar.alloc([32, D], BF16)
        load_w(w1, mlp_w1[l], "w1", 8)
        load_w(w2, mlp_w2[l], "w2", 32)
        nb = NormBufs(with_xt=False)
        xk = [[ar.alloc([D], F32) for _ in range(2)] for _ in range(2)]
        hTc = [ar.alloc([8, 256], BF16) for _ in range(2)]
        hid = ar.alloc([32, 256], BF16)
        rl = [ar.alloc([256], F32) for _ in range(2)]
        mk = ar.off
        ci = 0
        for b in range(NB):
            for isc in (True, False):
                if last and isc:
                    continue
                ar.reset(mk)
                md = load_mod_tiles(l, NB if isc else b, 1, "mm", nb.ng, "GSg")
                G, S, Gt = md["G"], md["S"], md["Gt"]
                t_lo, t_hi = (0, CTX) if isc else (CTX, T)
                for c0 in range(t_lo, t_hi, 256):
                    cs_ = ci % 2
                    ci += 1
                    for ti in range(2):
                        r0 = b * T + c0 + ti * 128
                        norm_tile(nb, xs[r0:r0 + 128, :], XN(r0), G, S, "mm",
                                  hTc[cs_][:, :, ti * 128:(ti + 1) * 128],
                                  "mh%d" % cs_, keep_x=(xk[cs_][ti], "xk%d.%d" % (cs_, ti)))
                    for jf in range(32):
                        pb = jf % 2
                        for k in range(8):
                            P.pe(lambda e, jf=jf, k=k, pb=pb, cs_=cs_: e.matmul(
                                ps[pb][:, 0:256], lhsT=w1[:, k, jf * 128:(jf + 1) * 128], rhs=hTc[cs_][:, k, :],
                                start=(k == 0), stop=(k == 7)), r=["w1", "mh%d" % cs_], w=[PN(pb)])
                        P.act(lambda e, pb=pb: e.activation(out=rl[pb], in_=ps[pb][:, 0:256], func=AF.Relu),
                              r=[PN(pb)], w=["rl%d" % pb])
                        P.dve(lambda e, pb=pb, jf=jf: e.tensor_tensor(out=hid[:, jf, :], in0=ps[pb][:, 0:256],
                                                                      in1=rl[pb], op=ALU.mult),
                              r=[PN(pb), "rl%d" % pb], w=["hid%d" % jf])
                    hall = ["hid%d" % jf for jf in range(32)]
                    for ti in range(2):
                        xt = xk[cs_][ti]
                        xn = "xk%d.%d" % (cs_, ti)
                        r0 = b * T + c0 + ti * 128
                        for hf in range(2):
                            pb = 2 + (ti * 2 + hf) % 4
                            for jf in range(32):
                                P.pe(lambda e, jf=jf, hf=hf, pb=pb, ti=ti: e.matmul(
                                    ps[pb][:, :], lhsT=hid[:, jf, ti * 128:(ti + 1) * 128],
                                    rhs=w2[:, jf, hf * 512:(hf + 1) * 512], start=(jf == 0), stop=(jf == 31)),
                                    r=hall + ["w2"], w=[PN(pb)])
                            P.dve(lambda e, hf=hf, pb=pb, Gt=Gt: e.tensor_tensor(
                                out=nb.t1[:, hf * 512:(hf + 1) * 512], in0=ps[pb][:, :],
                                in1=Gt[:, hf * 512:(hf + 1) * 512], op=ALU.mult),
                                r=[PN(pb), "mmGt"], w=["mt1%d" % hf])
                            P.pool(lambda e, hf=hf, xt=xt: e.tensor_tensor(
                                out=xt[:, hf * 512:(hf + 1) * 512], in0=xt[:, hf * 512:(hf + 1) * 512],
                                in1=nb.t1[:, hf * 512:(hf + 1) * 512], op=ALU.add), r=["mt1%d" % hf, xn], w=[xn])
                        if last:
                            o0 = b * SEQ + (c0 - CTX) + ti * 128
                            P.dma(yout[o0:o0 + 128, :], xt, r=[xn], w=["youtd"])
                        else:
                            P.dma(xs[r0:r0 + 128, :], xt, r=[xn], w=[XN(r0)])
        P.fence()

    stage_adaln()
    for l in range(DEPTH):
        last = l == DEPTH - 1
        j = l // 2
        even = l % 2 == 0
        ar.reset()
        w = {}
        if even:
            ssm_prep(j, w)
        else:
            w["rope"] = ar.alloc([2, SEQ], BF16)
            P.dma(w["rope"], rope_d.rearrange("c p t -> p c t"), w=["rope"], q="pool")
            w["gqk"] = ar.alloc([2], F32)
            w["ghs"] = ar.alloc([1], F32)
            w["nlam"] = ar.alloc([1], F32)
            P.dma(w["gqk"], od_qk[j], w=["gqk"])
            P.dma(w["ghs"], od_hn[j], w=["ghs"])
            lp = ar.alloc([256], F32)
            lt = ar.alloc([8], F32)
            P.dma(lp, od_lam[j, :].partition_broadcast(128), w=["lp"])
            lam_init = 0.8 - 0.6 * math.exp(-0.3 * l)
            P.dve(lambda e, lp=lp: e.tensor_tensor(out=lp[:, 0:64], in0=lp[:, 0:64], in1=lp[:, 64:128], op=ALU.mult),
                  r=["lp"], w=["lp"])
            P.dve(lambda e, lp=lp: e.tensor_tensor(out=lp[:, 128:192], in0=lp[:, 128:192], in1=lp[:, 192:256], op=ALU.mult),
                  r=["lp"], w=["lp"])
            P.dve(lambda e, lp=lp, lt=lt: e.tensor_reduce(out=lt[:, 0:1], in_=lp[:, 0:64], op=ALU.add,
                                            axis=mybir.AxisListType.X), r=["lp"], w=["lt"])
            P.dve(lambda e, lp=lp, lt=lt: e.tensor_reduce(out=lt[:, 1:2], in_=lp[:, 128:192], op=ALU.add,
                                            axis=mybir.AxisListType.X), r=["lp"], w=["lt"])
            P.act(lambda e, lt=lt: e.activation(out=lt[:, 2:4], in_=lt[:, 0:2], func=AF.Exp), r=["lt"], w=["lt"])
            P.dve(lambda e, lt=lt: e.tensor_tensor(out=lt[:, 4:5], in0=lt[:, 3:4], in1=lt[:, 2:3], op=ALU.subtract),
                  r=["lt"], w=["lt"])
            P.dve(lambda e, lam_init=lam_init, w=w, lt=lt: e.tensor_scalar(out=w["nlam"], in0=lt[:, 4:5], scalar1=-lam_init,
                                                               scalar2=None, op0=ALU.add), r=["lt"], w=["nlam"])
            P.dve(lambda e, lam_init=lam_init, w=w: e.tensor_scalar(out=w["ghs"], in0=w["ghs"], scalar1=1.0 - lam_init,
                                                               scalar2=None, op0=ALU.mult), r=["ghs"], w=["ghs"])
        for b in range(NB):
            (even_mixer if even else odd_mixer)(l, j, b, last, w)
        P.fence()
        mlp_stage(l, last)
    P.finalize(st)
    P.emit()
    st.close()
    _STATS['ops'] = len(P.ops)
    return nc


def _consts(SEQ, CTX, GRID_W=64):
    bf = ml_dtypes.bfloat16
    c = np.arange(128)
    ang = 2 * np.pi * np.outer(c, c) / 128
    csc = np.concatenate([np.cos(ang), np.sin(ang)], axis=1).astype(np.float32)

    def dft(L):
        t = np.arange(L, dtype=np.float64)
        a = 2 * np.pi * (np.outer(t, t) % L) / L
        nrm = 1.0 / math.sqrt(L * 128)
        return np.cos(a) * nrm, -np.sin(a) * nrm

    Cx, Sx = dft(SEQ)
    ncx = SEQ // 512
    dftx = np.stack([Cx, Sx], 0).reshape(2, SEQ // 128, 128, ncx, 512).transpose(3, 1, 2, 0, 4)
    Cc, Sc = dft(CTX)
    dftc = np.stack([Cc, Sc], 0).reshape(2, CTX // 128, 128, 1, CTX).transpose(3, 1, 2, 0, 4)
    T = CTX + SEQ
    j = np.arange(T)
    tau_f = j.astype(np.float32)
    tau_b = np.where(j < CTX, CTX - 1 - j, CTX + (T - 1 - j)).astype(np.float32)
    tau = np.stack([tau_f, tau_b], 0)
    half = 32
    inv = 10000.0 ** (-np.arange(0, half, 2, dtype=np.float32) / half)
    t = np.arange(SEQ)
    row = (t // GRID_W).astype(np.float32)
    col = (t % GRID_W).astype(np.float32)
    ar_ = row[None, :] * inv[:, None]
    ac_ = col[None, :] * inv[:, None]
    a64 = np.concatenate([ar_, ar_, ac_, ac_], 0)
    a128 = np.concatenate([a64, a64], 0).astype(np.float32)
    rope = np.stack([np.cos(a128), np.sin(a128)], 0).astype(np.float32)
    R = np.zeros((128, 128), np.float32)
    for blk in range(4):
        o = blk * 32
        for i in range(16):
            R[o + i, o + i + 16] = -1.0
            R[o + i + 16, o + i] = 1.0
    rmat = np.ascontiguousarray(R.T)
    bones = np.zeros((128, 128), np.float32)
    bones[:64, :64] = 1
    bones[64:, 64:] = 1
    return dict(csc=csc, dftx=np.ascontiguousarray(dftx).astype(bf), dftc=np.ascontiguousarray(dftc).astype(bf),
                iota=np.arange(256, dtype=np.float32)[None, :], rope=rope, rmat=rmat, bones=bones, ident=np.eye(128, dtype=np.float32))


def _prep_weights(inp, DEPTH):
    f = lambda a: np.ascontiguousarray(np.asarray(a, dtype=np.float32))
    n_even = (DEPTH + 1) // 2
    n_odd = DEPTH // 2
    out = dict(norm_g=f(np.stack([inp["norm1_g"][:DEPTH], inp["norm2_g"][:DEPTH]], 1)),
               ada_w=f(inp["ada_w"][:DEPTH]), ada_b=f(inp["ada_b"][:DEPTH]),
               mlp_w1=f(inp["mlp_w1"][:DEPTH]), mlp_w2=f(inp["mlp_w2"][:DEPTH]))
    if n_even:
        out["ev_w_in"] = f(inp["ev_w_in"][:n_even])
        out["ev_w_out"] = f(inp["ev_w_out"][:n_even])
        are = np.asarray(inp["ssm_a_re"])[:n_even].reshape(n_even, 32, 128).transpose(0, 2, 1)
        aim = np.asarray(inp["ssm_a_im"])[:n_even].reshape(n_even, 32, 128).transpose(0, 2, 1)
        ldt = np.repeat(np.asarray(inp["ssm_log_dt"])[:n_even].reshape(n_even, 64), 64, axis=1)
        ldt = ldt.reshape(n_even, 32, 128).transpose(0, 2, 1)
        out["ssm_sc"] = f(np.stack([are, aim, ldt], 1))
        bblk = np.zeros((n_even, 2, 32, 128, 128), np.float32)
        cblk = np.zeros((n_even, 2, 32, 128, 128), np.float32)
        for c, (bn, cn) in enumerate((("ssm_b_re", "ssm_c_re"), ("ssm_b_im", "ssm_c_im"))):
            B = np.asarray(inp[bn])[:n_even]
            C = np.asarray(inp[cn])[:n_even]
            for d in range(2):
                for g in range(32):
                    r = d * 16 + g // 2
                    p0 = (g % 2) * 64
                    c0 = (g % 8) * 16
                    bblk[:, c, r, p0:p0 + 64, c0:c0 + 16] = B[:, d, g]
                    cblk[:, c, r, p0:p0 + 64, c0:c0 + 16] = C[:, d, g].transpose(0, 2, 1)
        out["ssm_bblk"] = bblk
        out["ssm_cblk"] = cblk
        out["ssm_d"] = f(np.asarray(inp["ssm_d"])[:n_even].reshape(n_even, 4, 128).transpose(0, 2, 1))
        out["ssm_bglu"] = f(np.asarray(inp["ssm_b_glu"])[:n_even].reshape(n_even, 4, 128).transpose(0, 2, 1))
        out["ssm_wglu"] = f(inp["ssm_w_glu"][:n_even])
    if n_odd:
        out["od_w_in"] = f(inp["od_w_in"][:n_odd])
        out["od_w_out"] = f(inp["od_w_out"][:n_odd])
        gq = np.tile(np.asarray(inp["od_q_norm"])[:n_odd], (1, 2))
        gk = np.tile(np.asarray(inp["od_k_norm"])[:n_odd], (1, 2))
        out["od_qk"] = f(np.stack([gq, gk], -1))
        out["od_lam"] = f(np.asarray(inp["od_lambda"])[:n_odd].reshape(n_odd, 256))
        out["od_hn"] = f(np.asarray(inp["od_head_norm"])[:n_odd].reshape(n_odd, 128, 1))
    return out


_CACHE = {}
_STATS = {}


def run(inputs, DEPTH=4, n_cores=8, GRID_W=64):
    x = np.asarray(inputs["x"], dtype=np.float32)
    ctx = np.asarray(inputs["ctx"], dtype=np.float32)
    c = np.asarray(inputs["c"], dtype=np.float32)
    c_ctx = np.asarray(inputs["c_ctx"], dtype=np.float32)
    B, SEQ, _ = x.shape
    CTX = ctx.shape[1]
    NB = B // n_cores
    T = SEQ + CTX
    key = (NB, SEQ, CTX, DEPTH)
    if key not in _CACHE:
        _CACHE[key] = build(NB, SEQ, CTX, DEPTH, GRID_W)
    nc = _CACHE[key]
    shared = _prep_weights(inputs, DEPTH)
    consts = _consts(SEQ, CTX, GRID_W)
    n_even = (DEPTH + 1) // 2
    n_odd = DEPTH // 2
    if not n_even:
        for k in ("csc", "dftx", "dftc", "iota"):
            consts.pop(k)
    if not n_odd:
        for k in ("rope", "rmat", "bones"):
            consts.pop(k)
    shared.update(consts)
    in_maps = []
    for i in range(n_cores):
        sl = slice(i * NB, (i + 1) * NB)
        xin = np.concatenate([ctx[sl], x[sl]], axis=1).reshape(NB * T, D)
        cond = np.concatenate([c[sl], c_ctx[None, :]], 0)
        condT = np.ascontiguousarray(cond.T.reshape(8, 128, NB + 1).transpose(1, 0, 2))
        m = dict(shared)
        m["xin"] = np.ascontiguousarray(xin)
        m["condT"] = condT
        in_maps.append(m)
    res = run_bass_kernel_spmd(nc, in_maps, core_ids=list(range(n_cores)))
    outs = [np.asarray(r["yout"]).reshape(NB, SEQ, D) for r in res.results]
    return np.concatenate(outs, 0).astype(np.float32)


def kernel(**inputs):
    return run(inputs, DEPTH=4, n_cores=8)
```

```python
import contextlib
import numpy as np
import concourse.bass as bass
import concourse.mybir as mybir

F32 = mybir.dt.float32
BF16 = mybir.dt.bfloat16
I32 = mybir.dt.int32
AF = mybir.ActivationFunctionType
ALU = mybir.AluOpType

COMPUTE = ("pe", "act", "dve", "pool")
NSLOT = 8


class Buf:
    __slots__ = ("name", "lw", "rd")

    def __init__(self, name):
        self.name = name
        self.lw = None
        self.rd = []


class Op:
    __slots__ = ("eng", "fn", "reads", "writes", "deps", "signal", "sigval", "waits",
                 "dma", "slot", "gen", "fence")

    def __init__(self, eng, fn, reads, writes, dma):
        self.eng, self.fn, self.reads, self.writes, self.dma = eng, fn, reads, writes, dma
        self.deps = []
        self.signal = False
        self.sigval = 0
        self.waits = []
        self.slot = None
        self.gen = 0
        self.fence = False


class Prog:
    def __init__(self, nc):
        self.nc = nc
        self.ops = []
        self.bufs = {}

    def buf(self, name):
        b = self.bufs.get(name)
        if b is None:
            b = self.bufs[name] = Buf(name)
        return b

    def _bl(self, xs):
        out = []
        for x in xs:
            if isinstance(x, (list, tuple)):
                out.extend(self._bl(x))
            elif isinstance(x, str):
                out.append(self.buf(x))
            elif x is not None:
                out.append(x)
        return out

    def add(self, eng, fn, reads=(), writes=(), dma=False):
        op = Op(eng, fn, self._bl(reads), self._bl(writes), dma)
        self.ops.append(op)
        return op

    def pe(self, fn, r=(), w=()):
        return self.add("pe", fn, r, w)

    def act(self, fn, r=(), w=()):
        return self.add("act", fn, r, w)

    def dve(self, fn, r=(), w=()):
        return self.add("dve", fn, r, w)

    def pool(self, fn, r=(), w=()):
        return self.add("pool", fn, r, w)

    def dma(self, out, in_, r=(), w=(), q="sp", **kw):
        return self.add(q, lambda e: e.dma_start(out=out, in_=in_, **kw), r, w, dma=True)

    def fence(self):
        op = Op("sp", None, [], [], False)
        op.fence = True
        self.ops.append(op)

    def finalize(self, stack):
        nc = self.nc
        ops = self.ops
        last_on_eng = {}
        dmas_since = []
        fence_deps = []
        first_after = {}
        for op in ops:
            if op.fence:
                fence_deps = list(last_on_eng.values()) + list(dmas_since)
                dmas_since = []
                first_after = {}
                continue
            raw = set(b.lw for b in op.reads if b.lw is not None)
            deps = set(raw)
            for b in op.writes:
                if b.lw is not None:
                    deps.add(b.lw)
                deps.update(b.rd)
            fdeps = ()
            if op.eng not in first_after:
                first_after[op.eng] = True
                fdeps = fence_deps
            for b in op.reads:
                b.rd.append(op)
            for b in op.writes:
                b.lw = op
                b.rd = []
            keep = []
            for d in list(deps) + list(fdeps):
                if d is op or d in keep:
                    continue
                if d.eng == op.eng and not d.dma and not op.dma:
                    if op.eng == "pe" or d not in raw:
                        continue
                keep.append(d)
            op.deps = keep
            for d in keep:
                d.signal = True
            if op.dma:
                dmas_since.append(op)
            else:
                last_on_eng[op.eng] = op
        cnt = {e: 0 for e in COMPUTE}
        slot_next = {}
        slot_gen = {}
        self.sems = {}
        for e in COMPUTE:
            self.sems[e] = stack.enter_context(nc.semaphore("s_" + e))
        self.dma_sems = {}
        for op in ops:
            if op.fence:
                continue
            if op.dma:
                q = op.eng
                if q not in slot_next:
                    slot_next[q] = 0
                    for i in range(NSLOT):
                        self.dma_sems[(q, i)] = stack.enter_context(nc.semaphore("d_%s%d" % (q, i)))
                        slot_gen[(q, i)] = 0
                s = slot_next[q]
                slot_next[q] = (s + 1) % NSLOT
                slot_gen[(q, s)] += 1
                op.slot = (q, s)
                op.gen = slot_gen[(q, s)]
            elif op.signal:
                cnt[op.eng] += 1
                op.sigval = cnt[op.eng]
        self.slot_gen = slot_gen
        waited = {}
        for op in ops:
            if op.fence:
                continue
            w = waited.setdefault(op.eng, {})
            need = {}
            for d in op.deps:
                if d.dma:
                    key = ("d",) + d.slot
                    val = 16 * d.gen
                else:
                    key = ("c", d.eng)
                    val = d.sigval
                if val > need.get(key, 0):
                    need[key] = val
            if op.dma and op.gen > 1:
                key = ("d",) + op.slot
                val = 16 * (op.gen - 1)
                if val > need.get(key, 0):
                    need[key] = val
            for key, val in need.items():
                if w.get(key, 0) < val:
                    w[key] = val
                    op.waits.append((key, val))
        self.n_ops = len(ops)

    def _sem(self, key):
        if key[0] == "c":
            return self.sems[key[1]]
        return self.dma_sems[(key[1], key[2])]

    def emit(self):
        nc = self.nc
        by_eng = {}
        for op in self.ops:
            if not op.fence:
                by_eng.setdefault(op.eng, []).append(op)
        handles = {"pe": "tensor", "act": "scalar", "dve": "vector", "pool": "gpsimd", "sp": "sync"}
        with nc.Block() as block:
            for eng, name in handles.items():
                lst = by_eng.get(eng, [])

                def body(e, lst=lst, eng=eng):
                    for op in lst:
                        for key, val in op.waits:
                            e.wait_ge(self._sem(key), val)
                        ins = op.fn(e)
                        if op.dma:
                            ins.then_inc(self.dma_sems[op.slot], 16)
                        elif op.signal:
                            ins.then_inc(self.sems[eng], 1)
                    if eng == "sp":
                        for (q, s), g in self.slot_gen.items():
                            if g > 0:
                                e.wait_ge(self.dma_sems[(q, s)], 16 * g)

                getattr(block, name)(body)


class Arena:
    def __init__(self, nc, stack, nbytes, name="arena"):
        self.nbytes = nbytes
        self.t = stack.enter_context(nc.sbuf_tensor(name, [128, nbytes // 4], F32))
        self.off = 0
        self.mark = 0

    def reset(self, to=None):
        self.off = self.mark if to is None else to

    def alloc(self, shape, dtype):
        n = int(np.prod(shape))
        esz = mybir.dt.size(dtype)
        nb = (n * esz + 31) // 32 * 32
        assert self.off + nb <= self.nbytes, ("SBUF arena overflow", self.off, nb, self.nbytes)
        w0 = self.off // 4
        ap = self.t[:, w0:w0 + nb // 4]
        self.off += nb
        if dtype != F32:
            ap = ap.bitcast(dtype)
        ap = ap[:, 0:n]
        if len(shape) == 2:
            ap = ap.rearrange("p (a b) -> p a b", a=shape[0])
        elif len(shape) == 3:
            ap = ap.rearrange("p (a b c) -> p a b c", a=shape[0], b=shape[1])
        return ap

import math
import ml_dtypes
from concourse.bass_utils import run_bass_kernel_spmd

D = 1024
DFF = 4096
EPS = 1e-6
TWO_PI = 2.0 * math.pi
CW1 = 6.28125
CW2 = TWO_PI - CW1


def build(NB, SEQ, CTX, DEPTH, GRID_W=64):
    T = CTX + SEQ
    NT = T // 128
    NTC = CTX // 128
    NCX = SEQ // 512
    NR = NB + 1
    chunks = [(0, CTX, True)] + [(CTX + i * 512, 512, False) for i in range(NCX)]
    n_even = (DEPTH + 1) // 2
    n_odd = DEPTH // 2
    nc = bass.Bass("TRN2", target_bir_lowering=False)

    def din(name, shape, dt=F32):
        return nc.dram_tensor(name, list(shape), dt, kind="ExternalInput").ap()

    def dscr(name, shape, dt=F32):
        return nc.dram_tensor(name, list(shape), dt, kind="Internal").ap()

    xin = din("xin", [NB * T, D])
    condT = din("condT", [128, 8, NR])
    norm_g = din("norm_g", [DEPTH, 2, D])
    ada_w = din("ada_w", [DEPTH, D, 6 * D])
    ada_b = din("ada_b", [DEPTH, 6 * D])
    mlp_w1 = din("mlp_w1", [DEPTH, D, DFF])
    mlp_w2 = din("mlp_w2", [DEPTH, DFF, D])
    ident_d = din("ident", [128, 128])
    if n_even:
        ev_w_in = din("ev_w_in", [n_even, D, D])
        ev_w_out = din("ev_w_out", [n_even, D, D])
        ssm_sc = din("ssm_sc", [n_even, 3, 128, 32])
        ssm_bblk = din("ssm_bblk", [n_even, 2, 32, 128, 128])
        ssm_cblk = din("ssm_cblk", [n_even, 2, 32, 128, 128])
        ssm_d = din("ssm_d", [n_even, 128, 4])
        ssm_bglu = din("ssm_bglu", [n_even, 128, 4])
        ssm_wglu = din("ssm_wglu", [n_even, 512, 512])
        csc_d = din("csc", [128, 256])
        dftx = din("dftx", [NCX, SEQ // 128, 128, 2, 512], BF16)
        dftc = din("dftc", [1, NTC, 128, 2, CTX], BF16)
        iota_d = din("iota", [1, 256])
    if n_odd:
        od_w_in = din("od_w_in", [n_odd, D, 3 * D])
        od_w_out = din("od_w_out", [n_odd, D, D])
        od_qk = din("od_qk", [n_odd, 128, 2])
        od_lam = din("od_lam", [n_odd, 256])
        od_hn = din("od_hn", [n_odd, 128, 1])
        rope_d = din("rope", [2, 128, SEQ])
        rmat_d = din("rmat", [128, 128])
        bones_d = din("bones", [128, 128])
    yout = nc.dram_tensor("yout", [NB * SEQ, D], F32, kind="ExternalOutput").ap()
    xs = dscr("xs", [NB * T, D])
    mods = dscr("mods", [DEPTH, NR, 6 * D])
    vtok = dscr("vtok", [T, D], BF16)
    NCH = T // 256
    cstab = dscr("cstab", [8, NCH, 128, 4 * 2 * 256], BF16)

    st = contextlib.ExitStack()
    P = Prog(nc)
    ar = Arena(nc, st, 209000)
    psall = st.enter_context(nc.psum_tensor("psall", [128, 4096], F32))
    ps = [psall[:, i * 512:(i + 1) * 512] for i in range(8)]
    uid = [0]

    def U(s):
        uid[0] += 1
        return "%s#%d" % (s, uid[0])

    ident = ar.alloc([128], BF16)
    ones = ar.alloc([128], BF16)
    halfpi = ar.alloc([1], F32)
    epsc = ar.alloc([1], F32)
    P.dma(ident, ident_d, w=["ident"], q="pool")
    P.dve(lambda e: e.memset(ones, 1.0), w=["ones"])
    P.dve(lambda e: e.memset(halfpi, math.pi / 2), w=["halfpi"])
    P.dve(lambda e: e.memset(epsc, EPS), w=["epsc"])
    if n_even:
        csc = ar.alloc([256], BF16)
        P.dma(csc, csc_d, w=["csc"], q="pool")
        iot = ar.alloc([256], F32)
        P.dma(iot, iota_d[0, :].partition_broadcast(128), w=["iot"])
    if n_odd:
        rmat = ar.alloc([128], BF16)
        bones = ar.alloc([128], BF16)
        P.dma(rmat, rmat_d, w=["rmat"], q="pool")
        P.dma(bones, bones_d, w=["bones"], q="pool")
    ar.mark = ar.off

    def stage_adaln():
        ar.reset()
        ct = ar.alloc([8, NR], F32)
        cs_ = ar.alloc([8, NR], F32)
        P.dma(ct, condT, w=["ct"])
        P.act(lambda e: e.activation(out=cs_, in_=ct, func=AF.Silu), r=["ct"], w=["cs"])
        wt = [ar.alloc([8, 512], F32) for _ in range(2)]
        bt = [ar.alloc([512], F32) for _ in range(2)]
        ot = [ar.alloc([512], F32) for _ in range(2)]
        it = 0
        for l in range(DEPTH):
            for n in range(12):
                s = it % 2
                it += 1
                P.dma(wt[s], ada_w[l, :, n * 512:(n + 1) * 512].rearrange("(k p) n -> p k n", p=128),
                      w=["aw%d" % s])
                P.dma(bt[s][0:NR, :], ada_b[l, n * 512:(n + 1) * 512].partition_broadcast(NR), w=["ab%d" % s])
                for k in range(8):
                    P.pe(lambda e, s=s, k=k: e.matmul(ps[s][0:NR, :], lhsT=cs_[:, k, :], rhs=wt[s][:, k, :],
                                                      start=(k == 0), stop=(k == 7)),
                         r=["cs", "aw%d" % s], w=["ps%d" % s])
                P.dve(lambda e, s=s: e.tensor_tensor(out=ot[s][0:NR, :], in0=ps[s][0:NR, :], in1=bt[s][0:NR, :],
                                                     op=ALU.add), r=["ps%d" % s, "ab%d" % s], w=["ao%d" % s])
                P.dma(mods[l, :, n * 512:(n + 1) * 512], ot[s][0:NR, :], r=["ao%d" % s], w=["mods"])
        P.fence()

    ch256 = [(s0, 256, s0 < CTX) for s0 in range(0, T, 256)]
    PN = lambda b: "ps%d" % b
    XN = lambda r0: "xs%d" % r0

    def load_mod_tiles(l, row, which, pref, ngb, need):
        base = which * 3 * D
        out = {}
        if "G" in need:
            G = ar.alloc([D], F32)
            P.dma(ngb, norm_g[l, which, :].partition_broadcast(128), w=["ngb"])
            P.dma(G, mods[l, row, base + D:base + 2 * D].partition_broadcast(128), r=["mods"], w=[pref + "G"])
            P.dve(lambda e: e.scalar_tensor_tensor(out=G, in0=G, scalar=1.0, in1=ngb, op0=ALU.add, op1=ALU.mult),
                  r=[pref + "G", "ngb"], w=[pref + "G"])
            out["G"] = G
        if "S" in need:
            S = ar.alloc([D], F32)
            P.dma(S, mods[l, row, base:base + D].partition_broadcast(128), r=["mods"], w=[pref + "S"])
            out["S"] = S
        if "g" in need:
            Gt = ar.alloc([D], F32)
            P.dma(Gt, mods[l, row, base + 2 * D:base + 3 * D].partition_broadcast(128), r=["mods"], w=[pref + "Gt"])
            out["Gt"] = Gt
        return out

    class NormBufs:
        def __init__(self, with_xt=True):
            self.xt = [ar.alloc([D], F32) for _ in range(2)] if with_xt else None
            self.junk = ar.alloc([D], BF16)
            self.t1 = ar.alloc([D], F32)
            self.hb = [ar.alloc([D], BF16) for _ in range(2)]
            self.st = [ar.alloc([4], F32) for _ in range(2)]
            self.ng = ar.alloc([D], F32)
            self.i = 0

    def norm_tile_a(nb, src_rows, srcname, G, S, gname, keep_x=None):
        s = nb.i % 2
        nb.i += 1
        xt = nb.xt[s] if keep_x is None else keep_x[0]
        xn = "xt%d" % s if keep_x is None else keep_x[1]
        stt = nb.st[s]
        sn = "nst%d" % s
        P.dma(xt, src_rows, r=[srcname], w=[xn])
        P.act(lambda e: e.activation(out=nb.junk, in_=xt, func=AF.Square, accum_out=stt[:, 0:1]),
              r=[xn], w=["njunk", sn])
        P.dve(lambda e: e.tensor_scalar(out=stt[:, 1:2], in0=stt[:, 0:1], scalar1=1.0 / D, scalar2=EPS,
                                        op0=ALU.mult, op1=ALU.add), r=[sn], w=[sn])
        P.act(lambda e: e.activation(out=stt[:, 2:3], in_=stt[:, 1:2], func=AF.Sqrt), r=[sn], w=[sn])
        P.dve(lambda e: e.reciprocal(out=stt[:, 3:4], in_=stt[:, 2:3]), r=[sn], w=[sn])
        P.dve(lambda e: e.scalar_tensor_tensor(out=nb.t1, in0=xt, scalar=stt[:, 3:4], in1=G,
                                               op0=ALU.mult, op1=ALU.mult), r=[xn, sn, gname + "G"], w=["nt1"])
        hb = nb.hb[s]
        P.pool(lambda e: e.tensor_tensor(out=hb, in0=nb.t1, in1=S, op=ALU.add),
               r=["nt1", gname + "S"], w=["nhb%d" % s])
        return s

    def norm_tile_b(nb, s, hT_dst, hname):
        hb = nb.hb[s]
        pb = 6 + s
        pst = ps[pb][:, :].bitcast(BF16).rearrange("p (k t) -> p k t", k=8)
        for k in range(8):
            P.pe(lambda e, k=k: e.transpose(out=pst[:, k, :], in_=hb[:, k * 128:(k + 1) * 128], identity=ident),
                 r=["nhb%d" % s, "ident"], w=[PN(pb)])
        P.act(lambda e: e.activation(out=hT_dst, in_=pst, func=AF.Copy), r=[PN(pb)], w=[hname])

    def norm_tile(nb, src_rows, srcname, G, S, gname, hT_dst, hname, keep_x=None):
        s = norm_tile_a(nb, src_rows, srcname, G, S, gname, keep_x)
        norm_tile_b(nb, s, hT_dst, hname)
        return s

    def load_w(dst, src, name, kt):
        N = src.shape[1]
        v = src.rearrange("(k p) n -> p k n", p=128)
        cb = min(N, 2048)
        for c0 in range(0, N, cb):
            c1 = min(N, c0 + cb)
            P.dma(dst[:, :, c0:c1], v[:, :, c0:c1], w=[name], q="pool")

    def out_proj(l, b, last, yT, yname, wsrc):
        mk = ar.off
        wout = ar.alloc([8, D], BF16)
        load_w(wout, wsrc, "wout", 8)
        nb = NormBufs()
        Gtx = load_mod_tiles(l, b, 0, "x", nb.ng, "g")["Gt"]
        Gtc = None if last else load_mod_tiles(l, NB, 0, "c", nb.ng, "g")["Gt"]
        for tt in range(NT):
            isc = tt < NTC
            if last and isc:
                continue
            Gt = Gtc if isc else Gtx
            gn = "c" if isc else "x"
            s = nb.i % 2
            nb.i += 1
            xt = nb.xt[s]
            xn = "xt%d" % s
            r0 = b * T + tt * 128
            P.dma(xt, xs[r0:r0 + 128, :], r=[XN(r0)], w=[xn])
            for hf in range(2):
                pb = 4 + hf
                for k in range(8):
                    P.pe(lambda e, k=k, hf=hf, pb=pb, tt=tt: e.matmul(
                        ps[pb][:, :], lhsT=yT[:, k, tt * 128:(tt + 1) * 128], rhs=wout[:, k, hf * 512:(hf + 1) * 512],
                        start=(k == 0), stop=(k == 7)), r=[yname, "wout"], w=[PN(pb)])
                P.dve(lambda e, hf=hf, pb=pb, Gt=Gt: e.tensor_tensor(
                    out=nb.t1[:, hf * 512:(hf + 1) * 512], in0=ps[pb][:, :], in1=Gt[:, hf * 512:(hf + 1) * 512],
                    op=ALU.mult), r=[PN(pb), gn + "Gt"], w=["ot1%d" % hf])
                P.pool(lambda e, hf=hf, xt=xt: e.tensor_tensor(
                    out=xt[:, hf * 512:(hf + 1) * 512], in0=xt[:, hf * 512:(hf + 1) * 512],
                    in1=nb.t1[:, hf * 512:(hf + 1) * 512], op=ALU.add), r=["ot1%d" % hf, xn], w=[xn])
            P.dma(xs[r0:r0 + 128, :], xt, r=[xn], w=[XN(r0)])
        P.fence()
        ar.reset(mk)

    def ssm_prep(j, w):
        rho = ar.alloc([32], F32)
        th = ar.alloc([32], F32)
        nth = ar.alloc([32], F32)
        BT = ar.alloc([2, 32, 128], BF16)
        CT = ar.alloc([2, 32, 128], BF16)
        mk = ar.off
        sc3 = ar.alloc([3, 32], F32)
        P.dma(sc3, ssm_sc[j].rearrange("a p t -> p a t"), w=["sc3"])
        tmp = ar.alloc([16, 32], F32)
        ti = ar.alloc([32], I32)
        cre = ar.alloc([32], F32)
        cim = ar.alloc([32], F32)
        ncim = ar.alloc([32], F32)
        a_re, a_im = sc3[:, 0, :], sc3[:, 1, :]
        t = lambda i: tmp[:, i, :]
        R, W = ["sc3", "stmp"], ["stmp"]
        P.act(lambda e: e.activation(out=t(0), in_=sc3[:, 2, :], func=AF.Exp), r=R, w=W)
        P.dve(lambda e: e.tensor_tensor(out=t(1), in0=a_re, in1=t(0), op=ALU.mult), r=R, w=W)
        P.dve(lambda e: e.tensor_tensor(out=th, in0=a_im, in1=t(0), op=ALU.mult), r=R, w=["sth"])
        P.dve(lambda e: e.tensor_scalar(out=nth, in0=th, scalar1=-1.0, scalar2=None, op0=ALU.mult),
              r=["sth"], w=["snth"])
        P.act(lambda e: e.activation(out=rho, in_=t(1), func=AF.Exp), r=R, w=["srho"])
        P.dve(lambda e: e.tensor_scalar(out=ti, in0=th, scalar1=1.0 / TWO_PI, scalar2=None, op0=ALU.mult),
              r=["sth"], w=["sti"])
        P.dve(lambda e: e.scalar_tensor_tensor(out=t(2), in0=ti, scalar=-CW1, in1=th, op0=ALU.mult, op1=ALU.add),
              r=["sti", "sth"], w=W)
        P.dve(lambda e: e.scalar_tensor_tensor(out=t(2), in0=ti, scalar=-CW2, in1=t(2), op0=ALU.mult, op1=ALU.add),
              r=["sti"] + R, w=W)
        P.dve(lambda e: e.tensor_scalar(out=t(2), in0=t(2), scalar1=math.pi, scalar2=-math.pi, op0=ALU.min,
                                        op1=ALU.max), r=R, w=W)
        P.act(lambda e: e.activation(out=t(3), in_=t(2), func=AF.Sin), r=R, w=W)
        P.act(lambda e: e.activation(out=t(4), in_=t(2), func=AF.Abs), r=R, w=W)
        P.act(lambda e: e.activation(out=t(5), in_=t(4), func=AF.Sin, scale=-1.0, bias=halfpi[:, 0:1]),
              r=R + ["halfpi"], w=W)
        P.dve(lambda e: e.tensor_tensor(out=t(6), in0=rho, in1=t(5), op=ALU.mult), r=R + ["srho"], w=W)
        P.dve(lambda e: e.tensor_tensor(out=t(7), in0=rho, in1=t(3), op=ALU.mult), r=R + ["srho"], w=W)
        P.dve(lambda e: e.tensor_scalar(out=t(6), in0=t(6), scalar1=-1.0, scalar2=None, op0=ALU.add), r=R, w=W)
        P.dve(lambda e: e.tensor_tensor(out=t(8), in0=a_re, in1=a_re, op=ALU.mult), r=R, w=W)
        P.dve(lambda e: e.tensor_tensor(out=t(9), in0=a_im, in1=a_im, op=ALU.mult), r=R, w=W)
        P.dve(lambda e: e.tensor_tensor(out=t(8), in0=t(8), in1=t(9), op=ALU.add), r=R, w=W)
        P.dve(lambda e: e.reciprocal(out=t(8), in_=t(8)), r=R, w=W)
        P.dve(lambda e: e.tensor_tensor(out=t(9), in0=t(6), in1=a_re, op=ALU.mult), r=R, w=W)
        P.dve(lambda e: e.tensor_tensor(out=t(10), in0=t(7), in1=a_im, op=ALU.mult), r=R, w=W)
        P.dve(lambda e: e.tensor_tensor(out=t(9), in0=t(9), in1=t(10), op=ALU.add), r=R, w=W)
        P.dve(lambda e: e.tensor_tensor(out=cre, in0=t(9), in1=t(8), op=ALU.mult), r=R, w=["scre"])
        P.dve(lambda e: e.tensor_tensor(out=t(11), in0=t(7), in1=a_re, op=ALU.mult), r=R, w=W)
        P.dve(lambda e: e.tensor_tensor(out=t(12), in0=t(6), in1=a_im, op=ALU.mult), r=R, w=W)
        P.dve(lambda e: e.tensor_tensor(out=t(11), in0=t(11), in1=t(12), op=ALU.subtract), r=R, w=W)
        P.dve(lambda e: e.tensor_tensor(out=cim, in0=t(11), in1=t(8), op=ALU.mult), r=R, w=["scim"])
        P.dve(lambda e: e.tensor_scalar(out=ncim, in0=cim, scalar1=-1.0, scalar2=None, op0=ALU.mult),
              r=["scim"], w=["sncim"])
        for c in range(2):
            for r0 in range(0, 32, 8):
                P.dma(CT[:, c, r0:r0 + 8, :], ssm_cblk[j, c, r0:r0 + 8].rearrange("t p n -> p t n"),
                      w=["sCT"], q="pool")
        braw = [ar.alloc([2, 128], F32) for _ in range(2)]
        bb = [ar.alloc([2, 128], BF16) for _ in range(2)]
        bt1 = [ar.alloc([128], F32) for _ in range(2)]
        for r in range(32):
            s = r % 2
            P.dma(braw[s], ssm_bblk[j, :, r].rearrange("c p n -> p c n"), w=["braw%d" % s])
            P.dve(lambda e, s=s, r=r: e.tensor_scalar(out=bt1[s], in0=braw[s][:, 1, :], scalar1=ncim[:, r:r + 1],
                                                      scalar2=None, op0=ALU.mult),
                  r=["braw%d" % s, "sncim"], w=["bt1%d" % s])
            P.dve(lambda e, s=s, r=r: e.scalar_tensor_tensor(out=bb[s][:, 0, :], in0=braw[s][:, 0, :],
                                                             scalar=cre[:, r:r + 1], in1=bt1[s], op0=ALU.mult,
                                                             op1=ALU.add),
                  r=["braw%d" % s, "scre", "bt1%d" % s], w=["bb%d" % s])
            P.dve(lambda e, s=s, r=r: e.tensor_scalar(out=bt1[s], in0=braw[s][:, 0, :], scalar1=cim[:, r:r + 1],
                                                      scalar2=None, op0=ALU.mult),
                  r=["braw%d" % s, "scim", "bb%d" % s], w=["bt1%d" % s])
            P.dve(lambda e, s=s, r=r: e.scalar_tensor_tensor(out=bb[s][:, 1, :], in0=braw[s][:, 1, :],
                                                             scalar=cre[:, r:r + 1], in1=bt1[s], op0=ALU.mult,
                                                             op1=ALU.add),
                  r=["braw%d" % s, "scre", "bt1%d" % s], w=["bb%d" % s])
            pst = ps[s][:, 0:128].bitcast(BF16).rearrange("p (c n) -> p c n", c=2)
            for c in range(2):
                P.pe(lambda e, s=s, c=c, pst=pst: e.transpose(out=pst[:, c, :], in_=bb[s][:, c, :], identity=ident),
                     r=["bb%d" % s, "ident"], w=[PN(s)])
            P.act(lambda e, r=r, pst=pst: e.activation(out=BT[:, :, r, :], in_=pst, func=AF.Copy),
                  r=[PN(s)], w=["sBT"])
        stg = [ar.alloc([2, 256], BF16) for _ in range(2)]
        tw = [dict(ph=ar.alloc([256], F32), ki=ar.alloc([256], I32), ab=ar.alloc([256], F32)) for _ in range(2)]
        tht = ar.alloc([2, NCH, 32], F32)
        for d in range(2):
            for oi, (s0, _, isc) in enumerate(ch256):
                tau0 = float(s0) if d == 0 else (float(CTX - 1 - s0) if isc else float(CTX + T - 1 - s0))
                P.dve(lambda e, d=d, oi=oi, tau0=tau0: e.tensor_scalar(
                    out=tht[:, d, oi, :], in0=th, scalar1=tau0, scalar2=None, op0=ALU.mult),
                    r=["sth"], w=["tht"])
        gi = 0
        for r in range(32):
            d = r // 16
            thd = th if d == 0 else nth
            for cidx in range(NCH):
                s_ = gi % 2
                gi += 1
                ph, ki, ab = tw[s_]["ph"], tw[s_]["ki"], tw[s_]["ab"]
                tn = "tw%d" % s_
                P.dve(lambda e, r=r, d=d, cidx=cidx, ph=ph, thd=thd: e.tensor_scalar(
                    out=ph, in0=iot, scalar1=thd[:, r:r + 1], scalar2=tht[:, d, cidx, r:r + 1],
                    op0=ALU.mult, op1=ALU.add), r=["iot", "sth", "snth", "tht"], w=[tn + "ph"])
                P.dve(lambda e, ph=ph, ki=ki: e.tensor_scalar(out=ki, in0=ph, scalar1=1.0 / TWO_PI, scalar2=None,
                                                              op0=ALU.mult), r=[tn + "ph"], w=[tn + "ki"])
                P.dve(lambda e, ph=ph, ki=ki: e.scalar_tensor_tensor(out=ph, in0=ki, scalar=-CW1, in1=ph,
                                                                     op0=ALU.mult, op1=ALU.add),
                      r=[tn + "ph", tn + "ki"], w=[tn + "ph"])
                P.dve(lambda e, ph=ph, ki=ki: e.scalar_tensor_tensor(out=ph, in0=ki, scalar=-CW2, in1=ph,
                                                                     op0=ALU.mult, op1=ALU.add),
                      r=[tn + "ph", tn + "ki"], w=[tn + "ph"])
                P.pool(lambda e, ph=ph: e.tensor_scalar(out=ph, in0=ph, scalar1=math.pi, scalar2=-math.pi,
                                                        op0=ALU.min, op1=ALU.max), r=[tn + "ph"], w=[tn + "ph"])
                P.act(lambda e, ph=ph, s_=s_: e.activation(out=stg[s_][:, 1, :], in_=ph, func=AF.Sin),
                      r=[tn + "ph"], w=["stg%d" % s_])
                P.act(lambda e, ph=ph, ab=ab: e.activation(out=ab, in_=ph, func=AF.Abs), r=[tn + "ph"], w=[tn + "ab"])
                P.act(lambda e, ab=ab, s_=s_: e.activation(out=stg[s_][:, 0, :], in_=ab, func=AF.Sin, scale=-1.0,
                                                           bias=halfpi[:, 0:1]),
                      r=[tn + "ab", "halfpi"], w=["stg%d" % s_])
                dstv = cstab[r // 4, cidx].rearrange("p (a c t) -> p a c t", a=4, c=2)[:, r % 4, :, :]
                P.dma(dstv, stg[s_], r=["stg%d" % s_], w=["cstab"])
        P.fence()
        ar.reset(mk)
        w.update(rho=rho, th=th, nth=nth, BT=BT, CT=CT)

    def even_mixer(l, j, b, last, w):
        mk0 = ar.off
        zbuf = ar.alloc([8 * T], BF16)
        zT = zbuf.rearrange("p (a b) -> p a b", a=8)
        yT = ar.alloc([8, T], BF16)
        mark2 = ar.off
        win = ar.alloc([8, D], BF16)
        load_w(win, ev_w_in[j], "win", 8)
        nb = NormBufs()
        mx = load_mod_tiles(l, b, 0, "x", nb.ng, "GS")
        mc = load_mod_tiles(l, NB, 0, "c", nb.ng, "GS")
        hTc = [ar.alloc([8, 512], BF16) for _ in range(2)]
        src = xin if l == 0 else xs
        for ci, (s0, L, isc) in enumerate(chunks):
            hs = ci % 2
            mm_ = mc if isc else mx
            for ti in range(L // 128):
                r0 = b * T + s0 + ti * 128
                s = norm_tile(nb, src[r0:r0 + 128, :], XN(r0), mm_["G"], mm_["S"], "c" if isc else "x",
                              hTc[hs][:, :, ti * 128:(ti + 1) * 128], "hTc%d" % hs)
                if l == 0:
                    P.dma(xs[r0:r0 + 128, :], nb.xt[s], r=["xt%d" % s], w=[XN(r0)])
            for m in range(8):
                pb = m % 4
                for k in range(8):
                    P.pe(lambda e, m=m, k=k, pb=pb, hs=hs, L=L: e.matmul(
                        ps[pb][:, 0:L], lhsT=win[:, k, m * 128:(m + 1) * 128], rhs=hTc[hs][:, k, 0:L],
                        start=(k == 0), stop=(k == 7)), r=["win", "hTc%d" % hs], w=[PN(pb)])
                if m % 2 == 0:
                    P.act(lambda e, m=m, pb=pb, s0=s0, L=L: e.activation(out=zT[:, m, s0:s0 + L], in_=ps[pb][:, 0:L],
                                                                         func=AF.Copy),
                          r=[PN(pb)], w=["zT%d.%d" % (m, ci)])
                else:
                    P.dve(lambda e, m=m, pb=pb, s0=s0, L=L: e.tensor_copy(out=zT[:, m, s0:s0 + L], in_=ps[pb][:, 0:L]),
                          r=[PN(pb)], w=["zT%d.%d" % (m, ci)])
        P.fence()
        ar.reset(mark2)
        ABt = ar.alloc([NT, 4, 256], BF16)
        dring = [ar.alloc([2, 512], BF16) for _ in range(6)]
        zall = ["zT%d.%d" % (m, ci) for m in range(8) for ci in range(len(chunks))]
        for tt in range(NT):
            for gp in range(2):
                pb = 4 + gp
                pv = ps[pb][:, :].rearrange("p (g n) -> p g n", g=2)
                for gi in range(2):
                    g = gp * 2 + gi
                    P.pe(lambda e, g=g, gi=gi, tt=tt, pv=pv: e.matmul(
                        pv[:, gi, :], lhsT=zT[:, g, tt * 128:(tt + 1) * 128], rhs=csc, start=True, stop=True),
                        r=zall + ["csc"], w=[PN(pb)])
                if gp == 0:
                    P.act(lambda e, tt=tt, gp=gp, pv=pv: e.activation(out=ABt[:, tt, gp * 2:gp * 2 + 2, :], in_=pv,
                                                                      func=AF.Copy),
                          r=[PN(pb)], w=["ABt"])
                else:
                    P.dve(lambda e, tt=tt, gp=gp, pv=pv: e.tensor_copy(out=ABt[:, tt, gp * 2:gp * 2 + 2, :], in_=pv),
                          r=[PN(pb)], w=["ABt"])
        di = 0
        for (s0, L, isc) in chunks:
            if last and isc:
                continue
            tts = list(range(NTC)) if isc else list(range(NTC, NT))
            ntt = len(tts)
            for ii, tt in enumerate(tts):
                sl = di % 6
                di += 1
                srcd = dftc[0, ii] if isc else dftx[(s0 - CTX) // 512, ii]
                P.dma(dring[sl][:, :, 0:L], srcd, w=["dr%d" % sl])
                for g in range(4):
                    for c in range(2):
                        P.pe(lambda e, g=g, c=c, tt=tt, sl=sl, L=L, ii=ii, ntt=ntt: e.matmul(
                            ps[g][:, 0:L], lhsT=ABt[:, tt, g, c * 128:(c + 1) * 128], rhs=dring[sl][:, c, 0:L],
                            start=(ii == 0 and c == 0), stop=(ii == ntt - 1 and c == 1)),
                            r=["ABt", "dr%d" % sl], w=[PN(g)])
            for g in range(4):
                if g % 2 == 0:
                    P.act(lambda e, g=g, s0=s0, L=L: e.activation(out=yT[:, g, s0:s0 + L], in_=ps[g][:, 0:L],
                                                                  func=AF.Copy), r=[PN(g)], w=["yT"])
                else:
                    P.dve(lambda e, g=g, s0=s0, L=L: e.tensor_copy(out=yT[:, g, s0:s0 + L], in_=ps[g][:, 0:L]),
                          r=[PN(g)], w=["yT"])
        P.fence()
        ar.reset(mark2)
        yacc = ar.alloc([4, T], F32)
        wglu = ar.alloc([4, 512], BF16)
        load_w(wglu, ssm_wglu[j], "wglu", 4)
        dsk = ar.alloc([4], F32)
        bglu = ar.alloc([4], F32)
        P.dma(dsk, ssm_d[j], w=["dsk"])
        P.dma(bglu, ssm_bglu[j], w=["bglu"])
        L = 256
        NW = 2
        wk = []
        for i in range(NW):
            wk.append(dict(tab=ar.alloc([4, 2, L], BF16), pb=[ar.alloc([4, L], BF16) for _ in range(2)],
                           m=[ar.alloc([4, L], BF16) for _ in range(4)],
                           hr=ar.alloc([4, L], BF16), hi=ar.alloc([4, L], BF16)))
        gst = [ar.alloc([2, 4, L], BF16) for _ in range(2)]
        BT, CT, rho = w["BT"], w["CT"], w["rho"]
        P1 = psall[:, 0:1024].rearrange("p (a t) -> p a t", a=4)
        P2 = psall[:, 1024:2048].rearrange("p (a t) -> p a t", a=4)
        PN1, PN2 = [PN(0), PN(1)], [PN(2), PN(3)]
        nctx = CTX // 256
        groups = []
        for d in range(2):
            idx = list(range(NCH))
            order = idx if d == 0 else idx[:nctx][::-1] + idx[nctx:][::-1]
            for kt in range(4):
                for oi, cidx in enumerate(order):
                    groups.append((d, kt, oi, cidx))

        def ph_a(gi):
            d, kt, oi, cidx = groups[gi]
            s0 = ch256[cidx][0]
            k_ = wk[gi % NW]
            wn = "wk%d" % (gi % NW)
            tab, pb, mm = k_["tab"], k_["pb"], k_["m"]
            cs, sn = tab[:, :, 0, :], tab[:, :, 1, :]
            P.dma(tab.rearrange("p a c t -> p (a c t)"), cstab[d * 4 + kt, cidx], r=["cstab"], w=[wn + "tab"])
            for rr in range(4):
                r = d * 16 + kt * 4 + rr
                P.pe(lambda e, r=r, rr=rr, kt=kt, s0=s0: e.matmul(
                    P1[:, rr, :], lhsT=BT[:, 0, r, :], rhs=zT[:, 4 + kt, s0:s0 + L], start=True, stop=True),
                    r=["sBT"] + zall, w=[PN(rr // 2)])
                P.pe(lambda e, r=r, rr=rr, kt=kt, s0=s0: e.matmul(
                    P2[:, rr, :], lhsT=BT[:, 1, r, :], rhs=zT[:, 4 + kt, s0:s0 + L], start=True, stop=True),
                    r=["sBT"] + zall, w=[PN(2 + rr // 2)])
            P.act(lambda e, pb=pb: e.activation(out=pb[0], in_=P1, func=AF.Copy), r=PN1, w=[wn + "pb0"])
            P.act(lambda e, pb=pb: e.activation(out=pb[1], in_=P2, func=AF.Copy), r=PN2, w=[wn + "pb1"])

        def ph_a2(gi):
            k_ = wk[gi % NW]
            wn = "wk%d" % (gi % NW)
            tab, pb, mm = k_["tab"], k_["pb"], k_["m"]
            cs, sn = tab[:, :, 0, :], tab[:, :, 1, :]
            P.dve(lambda e, cs=cs, mm=mm, pb=pb: e.tensor_tensor(out=mm[0], in0=pb[0], in1=cs, op=ALU.mult),
                  r=[wn + "pb0", wn + "tab"], w=[wn + "m0"])
            P.dve(lambda e, sn=sn, mm=mm, pb=pb: e.tensor_tensor(out=mm[1], in0=pb[1], in1=sn, op=ALU.mult),
                  r=[wn + "pb1", wn + "tab"], w=[wn + "m1"])
            P.dve(lambda e, mm=mm: e.tensor_tensor(out=mm[0], in0=mm[0], in1=mm[1], op=ALU.add),
                  r=[wn + "m0", wn + "m1"], w=[wn + "m0"])
            P.dve(lambda e, cs=cs, mm=mm, pb=pb: e.tensor_tensor(out=mm[2], in0=pb[1], in1=cs, op=ALU.mult),
                  r=[wn + "pb1", wn + "tab"], w=[wn + "m2"])
            P.dve(lambda e, sn=sn, mm=mm, pb=pb: e.tensor_tensor(out=mm[3], in0=pb[0], in1=sn, op=ALU.mult),
                  r=[wn + "pb0", wn + "tab"], w=[wn + "m3"])
            P.dve(lambda e, mm=mm: e.tensor_tensor(out=mm[2], in0=mm[2], in1=mm[3], op=ALU.subtract),
                  r=[wn + "m2", wn + "m3"], w=[wn + "m2"])

        def ph_b(gi):
            d, kt, oi, cidx = groups[gi]
            k_ = wk[gi % NW]
            wn = "wk%d" % (gi % NW)
            mm = k_["m"]
            g = gst[gi % 2]
            gn = "gst%d" % (gi % 2)
            rv = (lambda a: a) if d == 0 else (lambda a: a[:, ::-1])
            for rr in range(4):
                r = d * 16 + kt * 4 + rr
                for c in range(2):
                    if oi == 0:
                        init = 0.0
                        rdi = []
                    else:
                        pg = gst[(gi - 1) % 2]
                        init = (pg[:, c, rr, L - 1:L] if d == 0 else pg[:, c, rr, 0:1])
                        rdi = ["gst%d" % ((gi - 1) % 2)]
                    P.dve(lambda e, c=c, rr=rr, g=g, mm=mm, init=init, r=r, rv=rv: e.tensor_tensor_scan(
                        out=rv(g[:, c, rr, :]), data0=rho[:, r:r + 1].to_broadcast([128, L]),
                        data1=rv(mm[2 * c][:, rr, :]), initial=init, op0=ALU.mult, op1=ALU.add),
                        r=["srho", wn + "m%d" % (2 * c)] + rdi, w=[gn])

        def ph_c(gi):
            d, kt, oi, cidx = groups[gi]
            s0 = ch256[cidx][0]
            k_ = wk[gi % NW]
            wn = "wk%d" % (gi % NW)
            tab, mm, hr, hi = k_["tab"], k_["m"], k_["hr"], k_["hi"]
            cs, sn = tab[:, :, 0, :], tab[:, :, 1, :]
            g = gst[gi % 2]
            gn = "gst%d" % (gi % 2)
            ypb = 4 + (gi % 2)
            P.dve(lambda e, g=g, cs=cs, mm=mm: e.tensor_tensor(out=mm[1], in0=g[:, 0], in1=cs, op=ALU.mult),
                  r=[gn, wn + "tab", wn + "m1"], w=[wn + "m1"])
            P.dve(lambda e, g=g, sn=sn, mm=mm: e.tensor_tensor(out=mm[3], in0=g[:, 1], in1=sn, op=ALU.mult),
                  r=[gn, wn + "tab", wn + "m3"], w=[wn + "m3"])
            P.dve(lambda e, mm=mm, hr=hr: e.tensor_tensor(out=hr, in0=mm[1], in1=mm[3], op=ALU.subtract),
                  r=[wn + "m1", wn + "m3"], w=[wn + "hr"])
            P.dve(lambda e, g=g, sn=sn, mm=mm: e.tensor_tensor(out=mm[0], in0=g[:, 0], in1=sn, op=ALU.mult),
                  r=[gn, wn + "tab", wn + "m0"], w=[wn + "m0"])
            P.dve(lambda e, g=g, cs=cs, mm=mm: e.tensor_tensor(out=mm[2], in0=g[:, 1], in1=cs, op=ALU.mult),
                  r=[gn, wn + "tab", wn + "m2"], w=[wn + "m2"])
            P.dve(lambda e, mm=mm, hi=hi: e.scalar_tensor_tensor(
                out=hi, in0=mm[0], scalar=-1.0, in1=mm[2], op0=ALU.mult, op1=ALU.subtract),
                r=[wn + "m0", wn + "m2"], w=[wn + "hi"])
            for rr in range(4):
                r = d * 16 + kt * 4 + rr
                P.pe(lambda e, r=r, hr=hr, rr=rr, ypb=ypb: e.matmul(
                    ps[ypb][:, 0:L], lhsT=CT[:, 0, r, :], rhs=hr[:, rr, :], start=(rr == 0), stop=False),
                    r=["sCT", wn + "hr"], w=[PN(ypb)])
                P.pe(lambda e, r=r, hi=hi, rr=rr, ypb=ypb: e.matmul(
                    ps[ypb][:, 0:L], lhsT=CT[:, 1, r, :], rhs=hi[:, rr, :], start=False, stop=(rr == 3)),
                    r=["sCT", wn + "hi"], w=[PN(ypb)])
            yn = "yacc%d" % kt
            if d == 0:
                P.act(lambda e, kt=kt, s0=s0, ypb=ypb: e.activation(
                    out=yacc[:, kt, s0:s0 + L], in_=ps[ypb][:, 0:L], func=AF.Copy),
                    r=[PN(ypb)], w=[yn])
            else:
                P.dve(lambda e, kt=kt, s0=s0, ypb=ypb: e.tensor_tensor(
                    out=yacc[:, kt, s0:s0 + L], in0=ps[ypb][:, 0:L], in1=yacc[:, kt, s0:s0 + L], op=ALU.add),
                    r=[PN(ypb), yn], w=[yn])

        ph_a(0)
        ph_a2(0)
        for gi in range(len(groups)):
            if gi + 1 < len(groups):
                ph_a(gi + 1)
            ph_b(gi)
            if gi + 1 < len(groups):
                ph_a2(gi + 1)
            ph_c(gi)
        yb = yT[:, 4:8, :]
        for kt in range(4):
            P.dve(lambda e, kt=kt: e.scalar_tensor_tensor(
                out=yacc[:, kt, :], in0=zT[:, 4 + kt, :], scalar=dsk[:, kt:kt + 1], in1=yacc[:, kt, :],
                op0=ALU.mult, op1=ALU.add), r=zall + ["dsk", "yacc%d" % kt], w=["yacc%d" % kt])
            P.act(lambda e, kt=kt: e.activation(out=yb[:, kt, :], in_=yacc[:, kt, :], func=AF.Gelu_apprx_tanh),
                  r=["yacc%d" % kt], w=["yT"])
        sg = [ar.alloc([512], BF16) for _ in range(4)]
        for (s0, Lc, isc) in chunks:
            if last and isc:
                continue
            for m in range(4):
                pb = m
                for k in range(4):
                    P.pe(lambda e, m=m, k=k, pb=pb, s0=s0, Lc=Lc: e.matmul(
                        ps[pb][:, 0:Lc], lhsT=wglu[:, k, m * 128:(m + 1) * 128], rhs=yb[:, k, s0:s0 + Lc],
                        start=(k == 0), stop=(k == 3)), r=["wglu", "yT"], w=[PN(pb)])
                P.act(lambda e, m=m, pb=pb, Lc=Lc: e.activation(out=sg[m][:, 0:Lc], in_=ps[pb][:, 0:Lc],
                                                                func=AF.Sigmoid, bias=bglu[:, m:m + 1]),
                      r=[PN(pb), "bglu"], w=["sg%d" % m])
            for m in range(4):
                P.dve(lambda e, m=m, s0=s0, Lc=Lc: e.tensor_tensor(
                    out=yb[:, m, s0:s0 + Lc], in0=yb[:, m, s0:s0 + Lc], in1=sg[m][:, 0:Lc], op=ALU.mult),
                    r=["yT"] + ["sg%d" % i for i in range(4)], w=["yT"])
        P.fence()
        ar.reset(mark2)
        out_proj(l, b, last, yT, "yT", ev_w_out[j])
        ar.reset(mk0)

    def odd_mixer(l, j, b, last, w):
        mk0 = ar.off
        qk = ar.alloc([16, T], BF16)
        mark2 = ar.off
        win = ar.alloc([8, 3 * D], BF16)
        load_w(win, od_w_in[j], "win", 8)
        nb = NormBufs()
        mx = load_mod_tiles(l, b, 0, "x", nb.ng, "GS")
        mc = load_mod_tiles(l, NB, 0, "c", nb.ng, "GS")
        hTc = [ar.alloc([8, 256], BF16) for _ in range(2)]
        sq = [ar.alloc([256], BF16) for _ in range(2)]
        rs = [ar.alloc([256], F32) for _ in range(2)]
        qn = [ar.alloc([256], BF16) for _ in range(2)]
        tq = [ar.alloc([256], F32) for _ in range(2)]
        uq = [ar.alloc([256], F32) for _ in range(2)]
        vst = [ar.alloc([1024], BF16) for _ in range(2)]
        rope = w["rope"]
        it = 0
        L = 256
        for ci, (s0, _, isc) in enumerate(ch256):
            hs = ci % 2
            mm_ = mc if isc else mx
            for ti in range(2):
                r0 = b * T + s0 + ti * 128
                norm_tile(nb, xs[r0:r0 + 128, :], XN(r0), mm_["G"], mm_["S"], "c" if isc else "x",
                          hTc[hs][:, :, ti * 128:(ti + 1) * 128], "hTc%d" % hs)
            for ti in range(2):
                vs = (s0 // 128 + ti) % 2
                for hf in range(2):
                    pb = 4 + hf
                    for k in range(8):
                        P.pe(lambda e, k=k, hf=hf, pb=pb, hs=hs, ti=ti: e.matmul(
                            ps[pb][:, :], lhsT=hTc[hs][:, k, ti * 128:(ti + 1) * 128],
                            rhs=win[:, k, 2048 + hf * 512:2048 + (hf + 1) * 512], start=(k == 0), stop=(k == 7)),
                            r=["win", "hTc%d" % hs], w=[PN(pb)])
                    if hf == 0:
                        P.act(lambda e, vs=vs, pb=pb: e.activation(out=vst[vs][:, 0:512], in_=ps[pb][:, :],
                                                                   func=AF.Copy), r=[PN(pb)], w=["vst%d" % vs])
                    else:
                        P.dve(lambda e, vs=vs, pb=pb: e.tensor_copy(out=vst[vs][:, 512:1024], in_=ps[pb][:, :]),
                              r=[PN(pb)], w=["vst%d" % vs])
                t0 = s0 + ti * 128
                P.dma(vtok[t0:t0 + 128, :], vst[vs], r=["vst%d" % vs], w=["vtok"])
            def q_proj(m, hs=hs):
                pq = m % 2
                for k in range(8):
                    P.pe(lambda e, m=m, k=k, pq=pq, hs=hs: e.matmul(
                        ps[pq][:, 0:L], lhsT=win[:, k, m * 128:(m + 1) * 128], rhs=hTc[hs][:, k, :],
                        start=(k == 0), stop=(k == 7)), r=["win", "hTc%d" % hs], w=[PN(pq)])

            def q_norm(m, ci=ci, s0=s0, isc=isc):
                s = m % 2
                pq = s
                pn_ = 2 + s
                P.act(lambda e, s=s, pq=pq: e.activation(out=sq[s], in_=ps[pq][:, 0:L], func=AF.Square),
                      r=[PN(pq)], w=["sq%d" % s])
                P.pe(lambda e, s=s, pn_=pn_: e.matmul(ps[pn_][:, 0:L], lhsT=bones, rhs=sq[s], start=True, stop=True),
                     r=["bones", "sq%d" % s], w=[PN(pn_)])
                P.dve(lambda e, s=s, pn_=pn_: e.tensor_scalar(out=rs[s], in0=ps[pn_][:, 0:L], scalar1=1.0 / 64,
                                                              scalar2=EPS, op0=ALU.mult, op1=ALU.add),
                      r=[PN(pn_)], w=["rs%d" % s])
                P.act(lambda e, s=s: e.activation(out=rs[s], in_=rs[s], func=AF.Sqrt), r=["rs%d" % s], w=["rs%d" % s])
                P.dve(lambda e, s=s: e.reciprocal(out=rs[s], in_=rs[s]), r=["rs%d" % s], w=["rs%d" % s])
                gcol = 0 if m < 8 else 1
                dst = qk[:, m, s0:s0 + L]
                dn = "qk%d.%d" % (m, ci)
                tgt = dst if isc else qn[s]
                tn = dn if isc else "qn%d" % s
                P.dve(lambda e, s=s, pq=pq, gcol=gcol, tgt=tgt: e.scalar_tensor_tensor(
                    out=tgt, in0=ps[pq][:, 0:L], scalar=w["gqk"][:, gcol:gcol + 1], in1=rs[s],
                    op0=ALU.mult, op1=ALU.mult), r=[PN(pq), "gqk", "rs%d" % s], w=[tn])

            def q_rope(m, ci=ci, s0=s0, isc=isc):
                if isc:
                    return
                s = m % 2
                pn_ = 2 + s
                dst = qk[:, m, s0:s0 + L]
                dn = "qk%d.%d" % (m, ci)
                x0 = s0 - CTX
                P.pe(lambda e, s=s, pn_=pn_: e.matmul(ps[pn_][:, 0:L], lhsT=rmat, rhs=qn[s], start=True, stop=True),
                     r=["rmat", "qn%d" % s], w=[PN(pn_)])
                P.dve(lambda e, s=s, pn_=pn_, x0=x0: e.tensor_tensor(
                    out=tq[s], in0=ps[pn_][:, 0:L], in1=rope[:, 1, x0:x0 + L], op=ALU.mult),
                    r=[PN(pn_), "rope"], w=["tq%d" % s])
                P.pool(lambda e, s=s, x0=x0: e.tensor_tensor(
                    out=uq[s], in0=qn[s], in1=rope[:, 0, x0:x0 + L], op=ALU.mult),
                    r=["qn%d" % s, "rope"], w=["uq%d" % s])
                P.pool(lambda e, s=s, dst=dst: e.tensor_tensor(out=dst, in0=tq[s], in1=uq[s], op=ALU.add),
                       r=["tq%d" % s, "uq%d" % s], w=[dn])

            q_proj(0)
            for m in range(16):
                if m + 1 < 16:
                    q_proj(m + 1)
                q_norm(m)
                if m >= 1:
                    q_rope(m - 1)
            q_rope(15)
        P.fence()
        ar.reset(mark2)
        yT = ar.alloc([8, T], BF16)
        mark3 = ar.off
        vh = [ar.alloc([NT, 128], BF16) for _ in range(2)]
        eb = [ar.alloc([512], BF16) for _ in range(8)]
        acc = [[[ar.alloc([512], F32) for _ in range(2)] for _ in range(2)] for _ in range(2)]
        accb = [ar.alloc([512], BF16) for _ in range(2)]
        rc = [ar.alloc([512], F32) for _ in range(2)]
        oa = [ar.alloc([512], F32) for _ in range(2)]
        of = ar.alloc([512], F32)
        o2 = ar.alloc([512], BF16)
        rs2 = ar.alloc([512], F32)
        qall = ["qk%d.%d" % (m, ci) for m in range(16) for ci in range(len(ch256))]
        ei = 0
        vt3 = vtok.rearrange("(t p) f -> p t f", p=128)
        units = [(h, s0, Lc, isc) for h in range(8) for (s0, Lc, isc) in chunks if not (last and isc)]

        def make_epi(u, h, s0, Lc):
            up = u % 2
            pvb = [2, 3] if up == 0 else [4, 5]

            def part1():
                for m in range(2):
                    P.pool(lambda e, m=m: e.tensor_tensor(out=accb[m][:, 0:Lc], in0=acc[up][m][0][:, 0:Lc],
                                                          in1=acc[up][m][1][:, 0:Lc], op=ALU.add),
                           r=["acc%d.%d.0" % (up, m), "acc%d.%d.1" % (up, m)], w=["accb%d" % m])
                    P.pe(lambda e, m=m: e.matmul(ps[6 + m][:, 0:Lc], lhsT=ones, rhs=accb[m][:, 0:Lc],
                                                 start=True, stop=True),
                         r=["ones", "accb%d" % m], w=[PN(6 + m)])
                for m in range(2):
                    P.act(lambda e, m=m: e.activation(out=rc[m][:, 0:Lc], in_=ps[6 + m][:, 0:Lc], func=AF.Ln),
                          r=[PN(6 + m)], w=["rc%d" % m])
                    P.act(lambda e, m=m: e.activation(out=rc[m][:, 0:Lc], in_=rc[m][:, 0:Lc], func=AF.Exp,
                                                      scale=-1.0), r=["rc%d" % m], w=["rc%d" % m])
                    P.dve(lambda e, m=m: e.tensor_tensor(out=oa[m][:, 0:Lc], in0=ps[pvb[m]][:, 0:Lc],
                                                         in1=rc[m][:, 0:Lc], op=ALU.mult),
                          r=[PN(pvb[m]), "rc%d" % m], w=["oa%d" % m])
                P.dve(lambda e: e.scalar_tensor_tensor(out=of[:, 0:Lc], in0=oa[1][:, 0:Lc],
                                                       scalar=w["nlam"][:, 0:1], in1=oa[0][:, 0:Lc],
                                                       op0=ALU.mult, op1=ALU.add),
                      r=["oa0", "oa1", "nlam"], w=["of"])
                P.pool(lambda e: e.tensor_tensor(out=o2[:, 0:Lc], in0=of[:, 0:Lc], in1=of[:, 0:Lc], op=ALU.mult),
                       r=["of"], w=["o2"])

            def part2():
                P.pe(lambda e: e.matmul(ps[6][:, 0:Lc], lhsT=ones, rhs=o2[:, 0:Lc], start=True, stop=True),
                     r=["ones", "o2"], w=[PN(6)])
                P.act(lambda e: e.activation(out=rs2[:, 0:Lc], in_=ps[6][:, 0:Lc], func=AF.Ln, scale=1.0 / 128,
                                             bias=epsc[:, 0:1]), r=[PN(6), "epsc"], w=["rs2"])
                P.act(lambda e: e.activation(out=rs2[:, 0:Lc], in_=rs2[:, 0:Lc], func=AF.Exp, scale=-0.5),
                      r=["rs2"], w=["rs2"])
                P.dve(lambda e: e.scalar_tensor_tensor(
                    out=yT[:, h, s0:s0 + Lc], in0=of[:, 0:Lc], scalar=w["ghs"][:, 0:1], in1=rs2[:, 0:Lc],
                    op0=ALU.mult, op1=ALU.mult), r=["of", "ghs", "rs2"], w=["yT"])

            return [part1, part2]

        pending = []
        cur_h = -1
        for u, (h, s0, Lc, isc) in enumerate(units):
            vs = h % 2
            up = u % 2
            pvb = [2, 3] if up == 0 else [4, 5]
            if h != cur_h:
                cur_h = h
                P.dma(vh[vs], vt3[:, :, h * 128:(h + 1) * 128], r=["vtok"], w=["vh%d" % vs])
            kts = list(range(NTC)) if isc else list(range(NT))
            nk = len(kts)
            items = [(m, ki, kt) for m in range(2) for ki, kt in enumerate(kts)]

            def s_mm(ii, h=h, s0=s0, Lc=Lc, items=items):
                m, ki, kt = items[ii]
                sb = ii % 2
                P.pe(lambda e, m=m, kt=kt, sb=sb: e.matmul(
                    ps[sb][:, 0:Lc], lhsT=qk[64 * m:64 * m + 64, 8 + h, kt * 128:(kt + 1) * 128],
                    rhs=qk[64 * m:64 * m + 64, h, s0:s0 + Lc], start=True, stop=True),
                    r=qall, w=[PN(sb)])

            s_mm(0)
            for ii, (m, ki, kt) in enumerate(items):
                if ii + 1 < len(items):
                    s_mm(ii + 1)
                if pending and ii == 2:
                    pending.pop(0)()
                if pending and ii == 20:
                    pending.pop(0)()
                sb = ii % 2
                es = ei % 8
                ei += 1
                P.act(lambda e, sb=sb, es=es, Lc=Lc: e.activation(out=eb[es][:, 0:Lc], in_=ps[sb][:, 0:Lc],
                                                                  func=AF.Exp, scale=0.125),
                      r=[PN(sb)], w=["eb%d" % es])
                P.pe(lambda e, m=m, kt=kt, es=es, vs=vs, Lc=Lc, ki=ki, nk=nk, pvb=pvb: e.matmul(
                    ps[pvb[m]][:, 0:Lc], lhsT=vh[vs][:, kt, :], rhs=eb[es][:, 0:Lc], start=(ki == 0),
                    stop=(ki == nk - 1)), r=["vh%d" % vs, "eb%d" % es], w=[PN(pvb[m])])
                a_ = acc[up][m][ki % 2]
                an = "acc%d.%d.%d" % (up, m, ki % 2)
                eng = P.pool if ki % 2 == 0 else P.dve
                if ki < 2:
                    eng(lambda e, a_=a_, es=es, Lc=Lc: e.tensor_copy(out=a_[:, 0:Lc], in_=eb[es][:, 0:Lc]),
                        r=["eb%d" % es], w=[an])
                else:
                    eng(lambda e, a_=a_, es=es, Lc=Lc: e.tensor_tensor(out=a_[:, 0:Lc], in0=a_[:, 0:Lc],
                                                                      in1=eb[es][:, 0:Lc], op=ALU.add),
                        r=["eb%d" % es, an], w=[an])
            while pending:
                pending.pop(0)()
            pending = make_epi(u, h, s0, Lc)
        while pending:
            pending.pop(0)()
        P.fence()
        ar.reset(mark3)
        out_proj(l, b, last, yT, "yT", od_w_out[j])
        ar.reset(mk0)

    def mlp_stage(l, last):
        ar.reset()
        w1 = ar.alloc([8, DFF], BF16)
        w2 = ar.alloc([32, D], BF16)
        load_w(w1, mlp_w1[l], "w1", 8)
        load_w(w2, mlp_w2[l], "w2", 32)
        nb = NormBufs(with_xt=False)
        xk = [[ar.alloc([D], F32) for _ in range(2)] for _ in range(2)]
        hTc = [ar.alloc([8, 256], BF16) for _ in range(2)]
        hid = ar.alloc([32, 256], BF16)
        rl = [ar.alloc([256], F32) for _ in range(2)]
        mt = ar.alloc([D], F32)
        mk = ar.off
        ci = 0
        for b in range(NB):
            for isc in (True, False):
                if last and isc:
                    continue
                ar.reset(mk)
                md = load_mod_tiles(l, NB if isc else b, 1, "mm", nb.ng, "GSg")
                G, S, Gt = md["G"], md["S"], md["Gt"]
                t_lo, t_hi = (0, CTX) if isc else (CTX, T)
                starts = list(range(t_lo, t_hi, 256))

                def do_a(c0, cs_, b=b, G=G, S=S):
                    ss = []
                    for ti in range(2):
                        r0 = b * T + c0 + ti * 128
                        ss.append(norm_tile_a(nb, xs[r0:r0 + 128, :], XN(r0), G, S, "mm",
                                              keep_x=(xk[cs_][ti], "xk%d.%d" % (cs_, ti))))
                    return ss

                ss_next = do_a(starts[0], ci % 2)
                for idx, c0 in enumerate(starts):
                    cs_ = ci % 2
                    ci += 1
                    ss = ss_next
                    for ti in range(2):
                        norm_tile_b(nb, ss[ti], hTc[cs_][:, :, ti * 128:(ti + 1) * 128], "mh%d" % cs_)
                    for jf in range(32):
                        pb = jf % 2
                        for k in range(8):
                            P.pe(lambda e, jf=jf, k=k, pb=pb, cs_=cs_: e.matmul(
                                ps[pb][:, 0:256], lhsT=w1[:, k, jf * 128:(jf + 1) * 128], rhs=hTc[cs_][:, k, :],
                                start=(k == 0), stop=(k == 7)), r=["w1", "mh%d" % cs_], w=[PN(pb)])
                        P.act(lambda e, pb=pb: e.activation(out=rl[pb], in_=ps[pb][:, 0:256], func=AF.Relu),
                              r=[PN(pb)], w=["rl%d" % pb])
                        P.dve(lambda e, pb=pb, jf=jf: e.tensor_tensor(out=hid[:, jf, :], in0=ps[pb][:, 0:256],
                                                                      in1=rl[pb], op=ALU.mult),
                              r=[PN(pb), "rl%d" % pb], w=["hid%d" % jf])
                    hall = ["hid%d" % jf for jf in range(32)]
                    if idx + 1 < len(starts):
                        ss_next = do_a(starts[idx + 1], ci % 2)
                    for ti in range(2):
                        xt = xk[cs_][ti]
                        xn = "xk%d.%d" % (cs_, ti)
                        r0 = b * T + c0 + ti * 128
                        for hf in range(2):
                            pb = 2 + (ti * 2 + hf) % 4
                            for jf in range(32):
                                P.pe(lambda e, jf=jf, hf=hf, pb=pb, ti=ti: e.matmul(
                                    ps[pb][:, :], lhsT=hid[:, jf, ti * 128:(ti + 1) * 128],
                                    rhs=w2[:, jf, hf * 512:(hf + 1) * 512], start=(jf == 0), stop=(jf == 31)),
                                    r=hall + ["w2"], w=[PN(pb)])
                            P.dve(lambda e, hf=hf, pb=pb, Gt=Gt: e.tensor_tensor(
                                out=mt[:, hf * 512:(hf + 1) * 512], in0=ps[pb][:, :],
                                in1=Gt[:, hf * 512:(hf + 1) * 512], op=ALU.mult),
                                r=[PN(pb), "mmGt"], w=["mt1%d" % hf])
                            P.pool(lambda e, hf=hf, xt=xt: e.tensor_tensor(
                                out=xt[:, hf * 512:(hf + 1) * 512], in0=xt[:, hf * 512:(hf + 1) * 512],
                                in1=mt[:, hf * 512:(hf + 1) * 512], op=ALU.add), r=["mt1%d" % hf, xn], w=[xn])
                        if last:
                            o0 = b * SEQ + (c0 - CTX) + ti * 128
                            P.dma(yout[o0:o0 + 128, :], xt, r=[xn], w=["youtd"])
                        else:
                            P.dma(xs[r0:r0 + 128, :], xt, r=[xn], w=[XN(r0)])
        P.fence()

    stage_adaln()
    for l in range(DEPTH):
        last = l == DEPTH - 1
        j = l // 2
        even = l % 2 == 0
        ar.reset()
        w = {}
        if even:
            ssm_prep(j, w)
        else:
            w["rope"] = ar.alloc([2, SEQ], BF16)
            P.dma(w["rope"], rope_d.rearrange("c p t -> p c t"), w=["rope"], q="pool")
            w["gqk"] = ar.alloc([2], F32)
            w["ghs"] = ar.alloc([1], F32)
            w["nlam"] = ar.alloc([1], F32)
            P.dma(w["gqk"], od_qk[j], w=["gqk"])
            P.dma(w["ghs"], od_hn[j], w=["ghs"])
            lp = ar.alloc([256], F32)
            lt = ar.alloc([8], F32)
            P.dma(lp, od_lam[j, :].partition_broadcast(128), w=["lp"])
            lam_init = 0.8 - 0.6 * math.exp(-0.3 * l)
            P.dve(lambda e, lp=lp: e.tensor_tensor(out=lp[:, 0:64], in0=lp[:, 0:64], in1=lp[:, 64:128], op=ALU.mult),
                  r=["lp"], w=["lp"])
            P.dve(lambda e, lp=lp: e.tensor_tensor(out=lp[:, 128:192], in0=lp[:, 128:192], in1=lp[:, 192:256], op=ALU.mult),
                  r=["lp"], w=["lp"])
            P.dve(lambda e, lp=lp, lt=lt: e.tensor_reduce(out=lt[:, 0:1], in_=lp[:, 0:64], op=ALU.add,
                                            axis=mybir.AxisListType.X), r=["lp"], w=["lt"])
            P.dve(lambda e, lp=lp, lt=lt: e.tensor_reduce(out=lt[:, 1:2], in_=lp[:, 128:192], op=ALU.add,
                                            axis=mybir.AxisListType.X), r=["lp"], w=["lt"])
            P.act(lambda e, lt=lt: e.activation(out=lt[:, 2:4], in_=lt[:, 0:2], func=AF.Exp), r=["lt"], w=["lt"])
            P.dve(lambda e, lt=lt: e.tensor_tensor(out=lt[:, 4:5], in0=lt[:, 3:4], in1=lt[:, 2:3], op=ALU.subtract),
                  r=["lt"], w=["lt"])
            P.dve(lambda e, lam_init=lam_init, w=w, lt=lt: e.tensor_scalar(out=w["nlam"], in0=lt[:, 4:5], scalar1=-lam_init,
                                                               scalar2=None, op0=ALU.add), r=["lt"], w=["nlam"])
            P.dve(lambda e, lam_init=lam_init, w=w: e.tensor_scalar(out=w["ghs"], in0=w["ghs"], scalar1=1.0 - lam_init,
                                                               scalar2=None, op0=ALU.mult), r=["ghs"], w=["ghs"])
        for b in range(NB):
            (even_mixer if even else odd_mixer)(l, j, b, last, w)
        P.fence()
        mlp_stage(l, last)
    P.finalize(st)
    P.emit()
    st.close()
    _STATS['ops'] = len(P.ops)
    return nc


def _consts(SEQ, CTX, GRID_W=64):
    bf = ml_dtypes.bfloat16
    c = np.arange(128)
    ang = 2 * np.pi * np.outer(c, c) / 128
    csc = np.concatenate([np.cos(ang), np.sin(ang)], axis=1).astype(np.float32)

    def dft(L):
        t = np.arange(L, dtype=np.float64)
        a = 2 * np.pi * (np.outer(t, t) % L) / L
        nrm = 1.0 / math.sqrt(L * 128)
        return np.cos(a) * nrm, -np.sin(a) * nrm

    Cx, Sx = dft(SEQ)
    ncx = SEQ // 512
    dftx = np.stack([Cx, Sx], 0).reshape(2, SEQ // 128, 128, ncx, 512).transpose(3, 1, 2, 0, 4)
    Cc, Sc = dft(CTX)
    dftc = np.stack([Cc, Sc], 0).reshape(2, CTX // 128, 128, 1, CTX).transpose(3, 1, 2, 0, 4)
    T = CTX + SEQ
    j = np.arange(T)
    tau_f = j.astype(np.float32)
    tau_b = np.where(j < CTX, CTX - 1 - j, CTX + (T - 1 - j)).astype(np.float32)
    tau = np.stack([tau_f, tau_b], 0)
    half = 32
    inv = 10000.0 ** (-np.arange(0, half, 2, dtype=np.float32) / half)
    t = np.arange(SEQ)
    row = (t // GRID_W).astype(np.float32)
    col = (t % GRID_W).astype(np.float32)
    ar_ = row[None, :] * inv[:, None]
    ac_ = col[None, :] * inv[:, None]
    a64 = np.concatenate([ar_, ar_, ac_, ac_], 0)
    a128 = np.concatenate([a64, a64], 0).astype(np.float32)
    rope = np.stack([np.cos(a128), np.sin(a128)], 0).astype(np.float32)
    R = np.zeros((128, 128), np.float32)
    for blk in range(4):
        o = blk * 32
        for i in range(16):
            R[o + i, o + i + 16] = -1.0
            R[o + i + 16, o + i] = 1.0
    rmat = np.ascontiguousarray(R.T)
    bones = np.zeros((128, 128), np.float32)
    bones[:64, :64] = 1
    bones[64:, 64:] = 1
    return dict(csc=csc, dftx=np.ascontiguousarray(dftx).astype(bf), dftc=np.ascontiguousarray(dftc).astype(bf),
                iota=np.arange(256, dtype=np.float32)[None, :], rope=rope, rmat=rmat, bones=bones, ident=np.eye(128, dtype=np.float32))


def _prep_weights(inp, DEPTH):
    f = lambda a: np.ascontiguousarray(np.asarray(a, dtype=np.float32))
    n_even = (DEPTH + 1) // 2
    n_odd = DEPTH // 2
    out = dict(norm_g=f(np.stack([inp["norm1_g"][:DEPTH], inp["norm2_g"][:DEPTH]], 1)),
               ada_w=f(inp["ada_w"][:DEPTH]), ada_b=f(inp["ada_b"][:DEPTH]),
               mlp_w1=f(inp["mlp_w1"][:DEPTH]), mlp_w2=f(inp["mlp_w2"][:DEPTH]))
    if n_even:
        out["ev_w_in"] = f(inp["ev_w_in"][:n_even])
        out["ev_w_out"] = f(inp["ev_w_out"][:n_even])
        are = np.asarray(inp["ssm_a_re"])[:n_even].reshape(n_even, 32, 128).transpose(0, 2, 1)
        aim = np.asarray(inp["ssm_a_im"])[:n_even].reshape(n_even, 32, 128).transpose(0, 2, 1)
        ldt = np.repeat(np.asarray(inp["ssm_log_dt"])[:n_even].reshape(n_even, 64), 64, axis=1)
        ldt = ldt.reshape(n_even, 32, 128).transpose(0, 2, 1)
        out["ssm_sc"] = f(np.stack([are, aim, ldt], 1))
        bblk = np.zeros((n_even, 2, 32, 128, 128), np.float32)
        cblk = np.zeros((n_even, 2, 32, 128, 128), np.float32)
        for c, (bn, cn) in enumerate((("ssm_b_re", "ssm_c_re"), ("ssm_b_im", "ssm_c_im"))):
            B = np.asarray(inp[bn])[:n_even]
            C = np.asarray(inp[cn])[:n_even]
            for d in range(2):
                for g in range(32):
                    r = d * 16 + g // 2
                    p0 = (g % 2) * 64
                    c0 = (g % 8) * 16
                    bblk[:, c, r, p0:p0 + 64, c0:c0 + 16] = B[:, d, g]
                    cblk[:, c, r, p0:p0 + 64, c0:c0 + 16] = C[:, d, g].transpose(0, 2, 1)
        out["ssm_bblk"] = bblk
        out["ssm_cblk"] = cblk
        out["ssm_d"] = f(np.asarray(inp["ssm_d"])[:n_even].reshape(n_even, 4, 128).transpose(0, 2, 1))
        out["ssm_bglu"] = f(np.asarray(inp["ssm_b_glu"])[:n_even].reshape(n_even, 4, 128).transpose(0, 2, 1))
        out["ssm_wglu"] = f(inp["ssm_w_glu"][:n_even])
    if n_odd:
        out["od_w_in"] = f(inp["od_w_in"][:n_odd])
        out["od_w_out"] = f(inp["od_w_out"][:n_odd])
        gq = np.tile(np.asarray(inp["od_q_norm"])[:n_odd], (1, 2))
        gk = np.tile(np.asarray(inp["od_k_norm"])[:n_odd], (1, 2))
        out["od_qk"] = f(np.stack([gq, gk], -1))
        out["od_lam"] = f(np.asarray(inp["od_lambda"])[:n_odd].reshape(n_odd, 256))
        out["od_hn"] = f(np.asarray(inp["od_head_norm"])[:n_odd].reshape(n_odd, 128, 1))
    return out


_CACHE = {}
_STATS = {}


def run(inputs, DEPTH=4, n_cores=8, GRID_W=64):
    x = np.asarray(inputs["x"], dtype=np.float32)
    ctx = np.asarray(inputs["ctx"], dtype=np.float32)
    c = np.asarray(inputs["c"], dtype=np.float32)
    c_ctx = np.asarray(inputs["c_ctx"], dtype=np.float32)
    B, SEQ, _ = x.shape
    CTX = ctx.shape[1]
    NB = B // n_cores
    T = SEQ + CTX
    key = (NB, SEQ, CTX, DEPTH)
    if key not in _CACHE:
        _CACHE[key] = build(NB, SEQ, CTX, DEPTH, GRID_W)
    nc = _CACHE[key]
    shared = _prep_weights(inputs, DEPTH)
    consts = _consts(SEQ, CTX, GRID_W)
    n_even = (DEPTH + 1) // 2
    n_odd = DEPTH // 2
    if not n_even:
        for k in ("csc", "dftx", "dftc", "iota"):
            consts.pop(k)
    if not n_odd:
        for k in ("rope", "rmat", "bones"):
            consts.pop(k)
    shared.update(consts)
    in_maps = []
    for i in range(n_cores):
        sl = slice(i * NB, (i + 1) * NB)
        xin = np.concatenate([ctx[sl], x[sl]], axis=1).reshape(NB * T, D)
        cond = np.concatenate([c[sl], c_ctx[None, :]], 0)
        condT = np.ascontiguousarray(cond.T.reshape(8, 128, NB + 1).transpose(1, 0, 2))
        m = dict(shared)
        m["xin"] = np.ascontiguousarray(xin)
        m["condT"] = condT
        in_maps.append(m)
    res = run_bass_kernel_spmd(nc, in_maps, core_ids=list(range(n_cores)))
    outs = [np.asarray(r["yout"]).reshape(NB, SEQ, D) for r in res.results]
    return np.concatenate(outs, 0).astype(np.float32)


def kernel(**inputs):
    return run(inputs, DEPTH=4, n_cores=8)
```

```python
import contextlib
import numpy as np
import concourse.bass as bass
import concourse.mybir as mybir

F32 = mybir.dt.float32
BF16 = mybir.dt.bfloat16
I32 = mybir.dt.int32
AF = mybir.ActivationFunctionType
ALU = mybir.AluOpType

COMPUTE = ("pe", "act", "dve", "pool")
NSLOT = 8


class Buf:
    __slots__ = ("name", "lw", "rd")

    def __init__(self, name):
        self.name = name
        self.lw = None
        self.rd = []


class Op:
    __slots__ = ("eng", "fn", "reads", "writes", "deps", "signal", "sigval", "waits",
                 "dma", "slot", "gen", "fence")

    def __init__(self, eng, fn, reads, writes, dma):
        self.eng, self.fn, self.reads, self.writes, self.dma = eng, fn, reads, writes, dma
        self.deps = []
        self.signal = False
        self.sigval = 0
        self.waits = []
        self.slot = None
        self.gen = 0
        self.fence = False


class Prog:
    def __init__(self, nc):
        self.nc = nc
        self.ops = []
        self.bufs = {}

    def buf(self, name):
        b = self.bufs.get(name)
        if b is None:
            b = self.bufs[name] = Buf(name)
        return b

    def _bl(self, xs):
        out = []
        for x in xs:
            if isinstance(x, (list, tuple)):
                out.extend(self._bl(x))
            elif isinstance(x, str):
                out.append(self.buf(x))
            elif x is not None:
                out.append(x)
        return out

    def add(self, eng, fn, reads=(), writes=(), dma=False):
        op = Op(eng, fn, self._bl(reads), self._bl(writes), dma)
        self.ops.append(op)
        return op

    def pe(self, fn, r=(), w=()):
        return self.add("pe", fn, r, w)

    def act(self, fn, r=(), w=()):
        return self.add("act", fn, r, w)

    def dve(self, fn, r=(), w=()):
        return self.add("dve", fn, r, w)

    def pool(self, fn, r=(), w=()):
        return self.add("pool", fn, r, w)

    def dma(self, out, in_, r=(), w=(), q="sp", **kw):
        return self.add(q, lambda e: e.dma_start(out=out, in_=in_, **kw), r, w, dma=True)

    def fence(self):
        op = Op("sp", None, [], [], False)
        op.fence = True
        self.ops.append(op)

    def finalize(self, stack):
        nc = self.nc
        ops = self.ops
        last_on_eng = {}
        dmas_since = []
        fence_deps = []
        first_after = {}
        for op in ops:
            if op.fence:
                fence_deps = list(last_on_eng.values()) + list(dmas_since)
                dmas_since = []
                first_after = {}
                continue
            raw = set(b.lw for b in op.reads if b.lw is not None)
            deps = set(raw)
            for b in op.writes:
                if b.lw is not None:
                    deps.add(b.lw)
                deps.update(b.rd)
            fdeps = ()
            if op.eng not in first_after:
                first_after[op.eng] = True
                fdeps = fence_deps
            for b in op.reads:
                b.rd.append(op)
            for b in op.writes:
                b.lw = op
                b.rd = []
            keep = []
            for d in list(deps) + list(fdeps):
                if d is op or d in keep:
                    continue
                if d.eng == op.eng and not d.dma and not op.dma:
                    if op.eng == "pe" or d not in raw:
                        continue
                keep.append(d)
            op.deps = keep
            for d in keep:
                d.signal = True
            if op.dma:
                dmas_since.append(op)
            else:
                last_on_eng[op.eng] = op
        cnt = {e: 0 for e in COMPUTE}
        slot_next = {}
        slot_gen = {}
        self.sems = {}
        for e in COMPUTE:
            self.sems[e] = stack.enter_context(nc.semaphore("s_" + e))
        self.dma_sems = {}
        for op in ops:
            if op.fence:
                continue
            if op.dma:
                q = op.eng
                if q not in slot_next:
                    slot_next[q] = 0
                    for i in range(NSLOT):
                        self.dma_sems[(q, i)] = stack.enter_context(nc.semaphore("d_%s%d" % (q, i)))
                        slot_gen[(q, i)] = 0
                s = slot_next[q]
                slot_next[q] = (s + 1) % NSLOT
                slot_gen[(q, s)] += 1
                op.slot = (q, s)
                op.gen = slot_gen[(q, s)]
            elif op.signal:
                cnt[op.eng] += 1
                op.sigval = cnt[op.eng]
        self.slot_gen = slot_gen
        waited = {}
        for op in ops:
            if op.fence:
                continue
            w = waited.setdefault(op.eng, {})
            need = {}
            for d in op.deps:
                if d.dma:
                    key = ("d",) + d.slot
                    val = 16 * d.gen
                else:
                    key = ("c", d.eng)
                    val = d.sigval
                if val > need.get(key, 0):
                    need[key] = val
            if op.dma and op.gen > 1:
                key = ("d",) + op.slot
                val = 16 * (op.gen - 1)
                if val > need.get(key, 0):
                    need[key] = val
            for key, val in need.items():
                if w.get(key, 0) < val:
                    w[key] = val
                    op.waits.append((key, val))
        self.n_ops = len(ops)

    def _sem(self, key):
        if key[0] == "c":
            return self.sems[key[1]]
        return self.dma_sems[(key[1], key[2])]

    def emit(self):
        nc = self.nc
        by_eng = {}
        for op in self.ops:
            if not op.fence:
                by_eng.setdefault(op.eng, []).append(op)
        handles = {"pe": "tensor", "act": "scalar", "dve": "vector", "pool": "gpsimd", "sp": "sync"}
        with nc.Block() as block:
            for eng, name in handles.items():
                lst = by_eng.get(eng, [])

                def body(e, lst=lst, eng=eng):
                    for op in lst:
                        for key, val in op.waits:
                            e.wait_ge(self._sem(key), val)
                        ins = op.fn(e)
                        if op.dma:
                            ins.then_inc(self.dma_sems[op.slot], 16)
                        elif op.signal:
                            ins.then_inc(self.sems[eng], 1)
                    if eng == "sp":
                        for (q, s), g in self.slot_gen.items():
                            if g > 0:
                                e.wait_ge(self.dma_sems[(q, s)], 16 * g)

                getattr(block, name)(body)


class Arena:
    def __init__(self, nc, stack, nbytes, name="arena"):
        self.nbytes = nbytes
        self.t = stack.enter_context(nc.sbuf_tensor(name, [128, nbytes // 4], F32))
        self.off = 0
        self.mark = 0

    def reset(self, to=None):
        self.off = self.mark if to is None else to

    def alloc(self, shape, dtype):
        n = int(np.prod(shape))
        esz = mybir.dt.size(dtype)
        nb = (n * esz + 31) // 32 * 32
        assert self.off + nb <= self.nbytes, ("SBUF arena overflow", self.off, nb, self.nbytes)
        w0 = self.off // 4
        ap = self.t[:, w0:w0 + nb // 4]
        self.off += nb
        if dtype != F32:
            ap = ap.bitcast(dtype)
        ap = ap[:, 0:n]
        if len(shape) == 2:
            ap = ap.rearrange("p (a b) -> p a b", a=shape[0])
        elif len(shape) == 3:
            ap = ap.rearrange("p (a b c) -> p a b c", a=shape[0], b=shape[1])
        return ap

import math
import ml_dtypes
from concourse.bass_utils import run_bass_kernel_spmd

D = 1024
DFF = 4096
EPS = 1e-6
TWO_PI = 2.0 * math.pi
CW1 = 6.28125
CW2 = TWO_PI - CW1


def build(NB, SEQ, CTX, DEPTH, GRID_W=64):
    T = CTX + SEQ
    NT = T // 128
    NTC = CTX // 128
    NCX = SEQ // 512
    NR = NB + 1
    chunks = [(0, CTX, True)] + [(CTX + i * 512, 512, False) for i in range(NCX)]
    n_even = (DEPTH + 1) // 2
    n_odd = DEPTH // 2
    nc = bass.Bass("TRN2", target_bir_lowering=False)

    def din(name, shape, dt=F32):
        return nc.dram_tensor(name, list(shape), dt, kind="ExternalInput").ap()

    def dscr(name, shape, dt=F32):
        return nc.dram_tensor(name, list(shape), dt, kind="Internal").ap()

    xin = din("xin", [NB * T, D])
    condT = din("condT", [128, 8, NR])
    norm_g = din("norm_g", [DEPTH, 2, D])
    ada_w = din("ada_w", [DEPTH, D, 6 * D])
    ada_b = din("ada_b", [DEPTH, 6 * D])
    mlp_w1 = din("mlp_w1", [DEPTH, D, DFF])
    mlp_w2 = din("mlp_w2", [DEPTH, DFF, D])
    ident_d = din("ident", [128, 128])
    if n_even:
        ev_w_in = din("ev_w_in", [n_even, D, D])
        ev_w_out = din("ev_w_out", [n_even, D, D])
        ssm_sc = din("ssm_sc", [n_even, 3, 128, 32])
        ssm_bblk = din("ssm_bblk", [n_even, 2, 32, 128, 128])
        ssm_cblk = din("ssm_cblk", [n_even, 2, 32, 128, 128])
        ssm_d = din("ssm_d", [n_even, 128, 4])
        ssm_bglu = din("ssm_bglu", [n_even, 128, 4])
        ssm_wglu = din("ssm_wglu", [n_even, 512, 512])
        csc_d = din("csc", [128, 256])
        dftx = din("dftx", [NCX, SEQ // 128, 128, 2, 512], BF16)
        dftc = din("dftc", [1, NTC, 128, 2, CTX], BF16)
        iota_d = din("iota", [1, 256])
    if n_odd:
        od_w_in = din("od_w_in", [n_odd, D, 3 * D])
        od_w_out = din("od_w_out", [n_odd, D, D])
        od_qk = din("od_qk", [n_odd, 128, 2])
        od_lam = din("od_lam", [n_odd, 256])
        od_hn = din("od_hn", [n_odd, 128, 1])
        rope_d = din("rope", [2, 128, SEQ])
        rmat_d = din("rmat", [128, 128])
        bones_d = din("bones", [128, 128])
    yout = nc.dram_tensor("yout", [NB * SEQ, D], F32, kind="ExternalOutput").ap()
    xs = dscr("xs", [NB * T, D])
    mods = dscr("mods", [DEPTH, NR, 6 * D])
    vtok = dscr("vtok", [T, D], BF16)
    NCH = T // 256
    cstab = dscr("cstab", [8, NCH, 128, 4 * 2 * 256], BF16)

    st = contextlib.ExitStack()
    P = Prog(nc)
    ar = Arena(nc, st, 209000)
    psall = st.enter_context(nc.psum_tensor("psall", [128, 4096], F32))
    ps = [psall[:, i * 512:(i + 1) * 512] for i in range(8)]
    uid = [0]

    def U(s):
        uid[0] += 1
        return "%s#%d" % (s, uid[0])

    ident = ar.alloc([128], BF16)
    ones = ar.alloc([128], BF16)
    halfpi = ar.alloc([1], F32)
    epsc = ar.alloc([1], F32)
    P.dma(ident, ident_d, w=["ident"], q="pool")
    P.dve(lambda e: e.memset(ones, 1.0), w=["ones"])
    P.dve(lambda e: e.memset(halfpi, math.pi / 2), w=["halfpi"])
    P.dve(lambda e: e.memset(epsc, EPS), w=["epsc"])
    if n_even:
        csc = ar.alloc([256], BF16)
        P.dma(csc, csc_d, w=["csc"], q="pool")
        iot = ar.alloc([256], F32)
        P.dma(iot, iota_d[0, :].partition_broadcast(128), w=["iot"])
    if n_odd:
        rmat = ar.alloc([128], BF16)
        bones = ar.alloc([128], BF16)
        P.dma(rmat, rmat_d, w=["rmat"], q="pool")
        P.dma(bones, bones_d, w=["bones"], q="pool")
    ar.mark = ar.off

    def stage_adaln():
        ar.reset()
        ct = ar.alloc([8, NR], F32)
        cs_ = ar.alloc([8, NR], F32)
        P.dma(ct, condT, w=["ct"])
        P.act(lambda e: e.activation(out=cs_, in_=ct, func=AF.Silu), r=["ct"], w=["cs"])
        wt = [ar.alloc([8, 512], F32) for _ in range(2)]
        bt = [ar.alloc([512], F32) for _ in range(2)]
        ot = [ar.alloc([512], F32) for _ in range(2)]
        it = 0
        for l in range(DEPTH):
            for n in range(12):
                s = it % 2
                it += 1
                P.dma(wt[s], ada_w[l, :, n * 512:(n + 1) * 512].rearrange("(k p) n -> p k n", p=128),
                      w=["aw%d" % s])
                P.dma(bt[s][0:NR, :], ada_b[l, n * 512:(n + 1) * 512].partition_broadcast(NR), w=["ab%d" % s])
                for k in range(8):
                    P.pe(lambda e, s=s, k=k: e.matmul(ps[s][0:NR, :], lhsT=cs_[:, k, :], rhs=wt[s][:, k, :],
                                                      start=(k == 0), stop=(k == 7)),
                         r=["cs", "aw%d" % s], w=["ps%d" % s])
                P.dve(lambda e, s=s: e.tensor_tensor(out=ot[s][0:NR, :], in0=ps[s][0:NR, :], in1=bt[s][0:NR, :],
                                                     op=ALU.add), r=["ps%d" % s, "ab%d" % s], w=["ao%d" % s])
                P.dma(mods[l, :, n * 512:(n + 1) * 512], ot[s][0:NR, :], r=["ao%d" % s], w=["mods"])
        P.fence()

    ch256 = [(s0, 256, s0 < CTX) for s0 in range(0, T, 256)]
    PN = lambda b: "ps%d" % b
    XN = lambda r0: "xs%d" % r0

    def load_mod_tiles(l, row, which, pref, ngb, need):
        base = which * 3 * D
        out = {}
        if "G" in need:
            G = ar.alloc([D], F32)
            P.dma(ngb, norm_g[l, which, :].partition_broadcast(128), w=["ngb"])
            P.dma(G, mods[l, row, base + D:base + 2 * D].partition_broadcast(128), r=["mods"], w=[pref + "G"])
            P.dve(lambda e: e.scalar_tensor_tensor(out=G, in0=G, scalar=1.0, in1=ngb, op0=ALU.add, op1=ALU.mult),
                  r=[pref + "G", "ngb"], w=[pref + "G"])
            out["G"] = G
        if "S" in need:
            S = ar.alloc([D], F32)
            P.dma(S, mods[l, row, base:base + D].partition_broadcast(128), r=["mods"], w=[pref + "S"])
            out["S"] = S
        if "g" in need:
            Gt = ar.alloc([D], F32)
            P.dma(Gt, mods[l, row, base + 2 * D:base + 3 * D].partition_broadcast(128), r=["mods"], w=[pref + "Gt"])
            out["Gt"] = Gt
        return out

    class NormBufs:
        def __init__(self, with_xt=True, nhb=2):
            self.xt = [ar.alloc([D], F32) for _ in range(2)] if with_xt else None
            self.junk = ar.alloc([D], BF16)
            self.t1 = ar.alloc([D], F32)
            self.nhb = nhb
            self.hb = [ar.alloc([D], BF16) for _ in range(nhb)]
            self.st = [ar.alloc([4], F32) for _ in range(2)]
            self.ng = ar.alloc([D], F32)
            self.i = 0

    def norm_tile_a(nb, src_rows, srcname, G, S, gname, keep_x=None):
        s = nb.i % 2
        hbi = nb.i % nb.nhb
        nb.i += 1
        xt = nb.xt[s] if keep_x is None else keep_x[0]
        xn = "xt%d" % s if keep_x is None else keep_x[1]
        stt = nb.st[s]
        sn = "nst%d" % s
        P.dma(xt, src_rows, r=[srcname], w=[xn])
        P.act(lambda e: e.activation(out=nb.junk, in_=xt, func=AF.Square, accum_out=stt[:, 0:1]),
              r=[xn], w=["njunk", sn])
        P.dve(lambda e: e.tensor_scalar(out=stt[:, 1:2], in0=stt[:, 0:1], scalar1=1.0 / D, scalar2=EPS,
                                        op0=ALU.mult, op1=ALU.add), r=[sn], w=[sn])
        P.act(lambda e: e.activation(out=stt[:, 2:3], in_=stt[:, 1:2], func=AF.Sqrt), r=[sn], w=[sn])
        P.dve(lambda e: e.reciprocal(out=stt[:, 3:4], in_=stt[:, 2:3]), r=[sn], w=[sn])
        P.dve(lambda e: e.scalar_tensor_tensor(out=nb.t1, in0=xt, scalar=stt[:, 3:4], in1=G,
                                               op0=ALU.mult, op1=ALU.mult), r=[xn, sn, gname + "G"], w=["nt1"])
        hb = nb.hb[hbi]
        P.pool(lambda e: e.tensor_tensor(out=hb, in0=nb.t1, in1=S, op=ALU.add),
               r=["nt1", gname + "S"], w=["nhb%d" % hbi])
        return (s, hbi)

    def norm_tile_b(nb, s, hT_dst, hname):
        s, hbi = s
        hb = nb.hb[hbi]
        pb = 6 + s
        pst = ps[pb][:, :].bitcast(BF16).rearrange("p (k t) -> p k t", k=8)
        for k in range(8):
            P.pe(lambda e, k=k: e.transpose(out=pst[:, k, :], in_=hb[:, k * 128:(k + 1) * 128], identity=ident),
                 r=["nhb%d" % hbi, "ident"], w=[PN(pb)])
        P.act(lambda e: e.activation(out=hT_dst, in_=pst, func=AF.Copy), r=[PN(pb)], w=[hname])

    def norm_tile(nb, src_rows, srcname, G, S, gname, hT_dst, hname, keep_x=None):
        s = norm_tile_a(nb, src_rows, srcname, G, S, gname, keep_x)
        norm_tile_b(nb, s, hT_dst, hname)
        return s[0]

    def load_w(dst, src, name, kt):
        N = src.shape[1]
        v = src.rearrange("(k p) n -> p k n", p=128)
        cb = min(N, 2048)
        for c0 in range(0, N, cb):
            c1 = min(N, c0 + cb)
            P.dma(dst[:, :, c0:c1], v[:, :, c0:c1], w=[name], q="pool")

    def out_proj(l, b, last, yT, yname, wsrc):
        mk = ar.off
        wout = ar.alloc([8, D], BF16)
        load_w(wout, wsrc, "wout", 8)
        nb = NormBufs()
        Gtx = load_mod_tiles(l, b, 0, "x", nb.ng, "g")["Gt"]
        Gtc = None if last else load_mod_tiles(l, NB, 0, "c", nb.ng, "g")["Gt"]
        for tt in range(NT):
            isc = tt < NTC
            if last and isc:
                continue
            Gt = Gtc if isc else Gtx
            gn = "c" if isc else "x"
            s = nb.i % 2
            nb.i += 1
            xt = nb.xt[s]
            xn = "xt%d" % s
            r0 = b * T + tt * 128
            P.dma(xt, xs[r0:r0 + 128, :], r=[XN(r0)], w=[xn])
            for hf in range(2):
                pb = 4 + hf
                for k in range(8):
                    P.pe(lambda e, k=k, hf=hf, pb=pb, tt=tt: e.matmul(
                        ps[pb][:, :], lhsT=yT[:, k, tt * 128:(tt + 1) * 128], rhs=wout[:, k, hf * 512:(hf + 1) * 512],
                        start=(k == 0), stop=(k == 7)), r=[yname, "wout"], w=[PN(pb)])
                P.dve(lambda e, hf=hf, pb=pb, Gt=Gt: e.tensor_tensor(
                    out=nb.t1[:, hf * 512:(hf + 1) * 512], in0=ps[pb][:, :], in1=Gt[:, hf * 512:(hf + 1) * 512],
                    op=ALU.mult), r=[PN(pb), gn + "Gt"], w=["ot1%d" % hf])
                P.pool(lambda e, hf=hf, xt=xt: e.tensor_tensor(
                    out=xt[:, hf * 512:(hf + 1) * 512], in0=xt[:, hf * 512:(hf + 1) * 512],
                    in1=nb.t1[:, hf * 512:(hf + 1) * 512], op=ALU.add), r=["ot1%d" % hf, xn], w=[xn])
            P.dma(xs[r0:r0 + 128, :], xt, r=[xn], w=[XN(r0)])
        P.fence()
        ar.reset(mk)

    def ssm_prep(j, w):
        rho = ar.alloc([32], F32)
        th = ar.alloc([32], F32)
        nth = ar.alloc([32], F32)
        BT = ar.alloc([2, 32, 128], BF16)
        CT = ar.alloc([2, 32, 128], BF16)
        mk = ar.off
        sc3 = ar.alloc([3, 32], F32)
        P.dma(sc3, ssm_sc[j].rearrange("a p t -> p a t"), w=["sc3"])
        tmp = ar.alloc([16, 32], F32)
        ti = ar.alloc([32], I32)
        cre = ar.alloc([32], F32)
        cim = ar.alloc([32], F32)
        ncim = ar.alloc([32], F32)
        a_re, a_im = sc3[:, 0, :], sc3[:, 1, :]
        t = lambda i: tmp[:, i, :]
        R, W = ["sc3", "stmp"], ["stmp"]
        P.act(lambda e: e.activation(out=t(0), in_=sc3[:, 2, :], func=AF.Exp), r=R, w=W)
        P.dve(lambda e: e.tensor_tensor(out=t(1), in0=a_re, in1=t(0), op=ALU.mult), r=R, w=W)
        P.dve(lambda e: e.tensor_tensor(out=th, in0=a_im, in1=t(0), op=ALU.mult), r=R, w=["sth"])
        P.dve(lambda e: e.tensor_scalar(out=nth, in0=th, scalar1=-1.0, scalar2=None, op0=ALU.mult),
              r=["sth"], w=["snth"])
        P.act(lambda e: e.activation(out=rho, in_=t(1), func=AF.Exp), r=R, w=["srho"])
        P.dve(lambda e: e.tensor_scalar(out=ti, in0=th, scalar1=1.0 / TWO_PI, scalar2=None, op0=ALU.mult),
              r=["sth"], w=["sti"])
        P.dve(lambda e: e.scalar_tensor_tensor(out=t(2), in0=ti, scalar=-CW1, in1=th, op0=ALU.mult, op1=ALU.add),
              r=["sti", "sth"], w=W)
        P.dve(lambda e: e.scalar_tensor_tensor(out=t(2), in0=ti, scalar=-CW2, in1=t(2), op0=ALU.mult, op1=ALU.add),
              r=["sti"] + R, w=W)
        P.dve(lambda e: e.tensor_scalar(out=t(2), in0=t(2), scalar1=math.pi, scalar2=-math.pi, op0=ALU.min,
                                        op1=ALU.max), r=R, w=W)
        P.act(lambda e: e.activation(out=t(3), in_=t(2), func=AF.Sin), r=R, w=W)
        P.act(lambda e: e.activation(out=t(4), in_=t(2), func=AF.Abs), r=R, w=W)
        P.act(lambda e: e.activation(out=t(5), in_=t(4), func=AF.Sin, scale=-1.0, bias=halfpi[:, 0:1]),
              r=R + ["halfpi"], w=W)
        P.dve(lambda e: e.tensor_tensor(out=t(6), in0=rho, in1=t(5), op=ALU.mult), r=R + ["srho"], w=W)
        P.dve(lambda e: e.tensor_tensor(out=t(7), in0=rho, in1=t(3), op=ALU.mult), r=R + ["srho"], w=W)
        P.dve(lambda e: e.tensor_scalar(out=t(6), in0=t(6), scalar1=-1.0, scalar2=None, op0=ALU.add), r=R, w=W)
        P.dve(lambda e: e.tensor_tensor(out=t(8), in0=a_re, in1=a_re, op=ALU.mult), r=R, w=W)
        P.dve(lambda e: e.tensor_tensor(out=t(9), in0=a_im, in1=a_im, op=ALU.mult), r=R, w=W)
        P.dve(lambda e: e.tensor_tensor(out=t(8), in0=t(8), in1=t(9), op=ALU.add), r=R, w=W)
        P.dve(lambda e: e.reciprocal(out=t(8), in_=t(8)), r=R, w=W)
        P.dve(lambda e: e.tensor_tensor(out=t(9), in0=t(6), in1=a_re, op=ALU.mult), r=R, w=W)
        P.dve(lambda e: e.tensor_tensor(out=t(10), in0=t(7), in1=a_im, op=ALU.mult), r=R, w=W)
        P.dve(lambda e: e.tensor_tensor(out=t(9), in0=t(9), in1=t(10), op=ALU.add), r=R, w=W)
        P.dve(lambda e: e.tensor_tensor(out=cre, in0=t(9), in1=t(8), op=ALU.mult), r=R, w=["scre"])
        P.dve(lambda e: e.tensor_tensor(out=t(11), in0=t(7), in1=a_re, op=ALU.mult), r=R, w=W)
        P.dve(lambda e: e.tensor_tensor(out=t(12), in0=t(6), in1=a_im, op=ALU.mult), r=R, w=W)
        P.dve(lambda e: e.tensor_tensor(out=t(11), in0=t(11), in1=t(12), op=ALU.subtract), r=R, w=W)
        P.dve(lambda e: e.tensor_tensor(out=cim, in0=t(11), in1=t(8), op=ALU.mult), r=R, w=["scim"])
        P.dve(lambda e: e.tensor_scalar(out=ncim, in0=cim, scalar1=-1.0, scalar2=None, op0=ALU.mult),
              r=["scim"], w=["sncim"])
        for c in range(2):
            for r0 in range(0, 32, 8):
                P.dma(CT[:, c, r0:r0 + 8, :], ssm_cblk[j, c, r0:r0 + 8].rearrange("t p n -> p t n"),
                      w=["sCT"], q="pool")
        braw = [ar.alloc([2, 128], F32) for _ in range(2)]
        bb = [ar.alloc([2, 128], BF16) for _ in range(2)]
        bt1 = [ar.alloc([128], F32) for _ in range(2)]
        for r in range(32):
            s = r % 2
            P.dma(braw[s], ssm_bblk[j, :, r].rearrange("c p n -> p c n"), w=["braw%d" % s])
            P.dve(lambda e, s=s, r=r: e.tensor_scalar(out=bt1[s], in0=braw[s][:, 1, :], scalar1=ncim[:, r:r + 1],
                                                      scalar2=None, op0=ALU.mult),
                  r=["braw%d" % s, "sncim"], w=["bt1%d" % s])
            P.dve(lambda e, s=s, r=r: e.scalar_tensor_tensor(out=bb[s][:, 0, :], in0=braw[s][:, 0, :],
                                                             scalar=cre[:, r:r + 1], in1=bt1[s], op0=ALU.mult,
                                                             op1=ALU.add),
                  r=["braw%d" % s, "scre", "bt1%d" % s], w=["bb%d" % s])
            P.dve(lambda e, s=s, r=r: e.tensor_scalar(out=bt1[s], in0=braw[s][:, 0, :], scalar1=cim[:, r:r + 1],
                                                      scalar2=None, op0=ALU.mult),
                  r=["braw%d" % s, "scim", "bb%d" % s], w=["bt1%d" % s])
            P.dve(lambda e, s=s, r=r: e.scalar_tensor_tensor(out=bb[s][:, 1, :], in0=braw[s][:, 1, :],
                                                             scalar=cre[:, r:r + 1], in1=bt1[s], op0=ALU.mult,
                                                             op1=ALU.add),
                  r=["braw%d" % s, "scre", "bt1%d" % s], w=["bb%d" % s])
            pst = ps[s][:, 0:128].bitcast(BF16).rearrange("p (c n) -> p c n", c=2)
            for c in range(2):
                P.pe(lambda e, s=s, c=c, pst=pst: e.transpose(out=pst[:, c, :], in_=bb[s][:, c, :], identity=ident),
                     r=["bb%d" % s, "ident"], w=[PN(s)])
            P.act(lambda e, r=r, pst=pst: e.activation(out=BT[:, :, r, :], in_=pst, func=AF.Copy),
                  r=[PN(s)], w=["sBT"])
        stg = [ar.alloc([2, 256], BF16) for _ in range(2)]
        tw = [dict(ph=ar.alloc([256], F32), ki=ar.alloc([256], I32), ab=ar.alloc([256], F32)) for _ in range(2)]
        tht = ar.alloc([2, NCH, 32], F32)
        for d in range(2):
            for oi, (s0, _, isc) in enumerate(ch256):
                tau0 = float(s0) if d == 0 else (float(CTX - 1 - s0) if isc else float(CTX + T - 1 - s0))
                P.dve(lambda e, d=d, oi=oi, tau0=tau0: e.tensor_scalar(
                    out=tht[:, d, oi, :], in0=th, scalar1=tau0, scalar2=None, op0=ALU.mult),
                    r=["sth"], w=["tht"])
        gi = 0
        for r in range(32):
            d = r // 16
            thd = th if d == 0 else nth
            for cidx in range(NCH):
                s_ = gi % 2
                gi += 1
                ph, ki, ab = tw[s_]["ph"], tw[s_]["ki"], tw[s_]["ab"]
                tn = "tw%d" % s_
                P.dve(lambda e, r=r, d=d, cidx=cidx, ph=ph, thd=thd: e.tensor_scalar(
                    out=ph, in0=iot, scalar1=thd[:, r:r + 1], scalar2=tht[:, d, cidx, r:r + 1],
                    op0=ALU.mult, op1=ALU.add), r=["iot", "sth", "snth", "tht"], w=[tn + "ph"])
                P.dve(lambda e, ph=ph, ki=ki: e.tensor_scalar(out=ki, in0=ph, scalar1=1.0 / TWO_PI, scalar2=None,
                                                              op0=ALU.mult), r=[tn + "ph"], w=[tn + "ki"])
                P.dve(lambda e, ph=ph, ki=ki: e.scalar_tensor_tensor(out=ph, in0=ki, scalar=-CW1, in1=ph,
                                                                     op0=ALU.mult, op1=ALU.add),
                      r=[tn + "ph", tn + "ki"], w=[tn + "ph"])
                P.dve(lambda e, ph=ph, ki=ki: e.scalar_tensor_tensor(out=ph, in0=ki, scalar=-CW2, in1=ph,
                                                                     op0=ALU.mult, op1=ALU.add),
                      r=[tn + "ph", tn + "ki"], w=[tn + "ph"])
                P.pool(lambda e, ph=ph: e.tensor_scalar(out=ph, in0=ph, scalar1=math.pi, scalar2=-math.pi,
                                                        op0=ALU.min, op1=ALU.max), r=[tn + "ph"], w=[tn + "ph"])
                P.act(lambda e, ph=ph, s_=s_: e.activation(out=stg[s_][:, 1, :], in_=ph, func=AF.Sin),
                      r=[tn + "ph"], w=["stg%d" % s_])
                P.act(lambda e, ph=ph, ab=ab: e.activation(out=ab, in_=ph, func=AF.Abs), r=[tn + "ph"], w=[tn + "ab"])
                P.act(lambda e, ab=ab, s_=s_: e.activation(out=stg[s_][:, 0, :], in_=ab, func=AF.Sin, scale=-1.0,
                                                           bias=halfpi[:, 0:1]),
                      r=[tn + "ab", "halfpi"], w=["stg%d" % s_])
                dstv = cstab[r // 4, cidx].rearrange("p (a c t) -> p a c t", a=4, c=2)[:, r % 4, :, :]
                P.dma(dstv, stg[s_], r=["stg%d" % s_], w=["cstab"])
        P.fence()
        ar.reset(mk)
        w.update(rho=rho, th=th, nth=nth, BT=BT, CT=CT)

    def even_mixer(l, j, b, last, w):
        mk0 = ar.off
        zbuf = ar.alloc([8 * T], BF16)
        zT = zbuf.rearrange("p (a b) -> p a b", a=8)
        yT = ar.alloc([8, T], BF16)
        mark2 = ar.off
        win = ar.alloc([8, D], BF16)
        load_w(win, ev_w_in[j], "win", 8)
        nb = NormBufs()
        mx = load_mod_tiles(l, b, 0, "x", nb.ng, "GS")
        mc = load_mod_tiles(l, NB, 0, "c", nb.ng, "GS")
        hTc = [ar.alloc([8, 512], BF16) for _ in range(2)]
        src = xin if l == 0 else xs
        for ci, (s0, L, isc) in enumerate(chunks):
            hs = ci % 2
            mm_ = mc if isc else mx
            for ti in range(L // 128):
                r0 = b * T + s0 + ti * 128
                s = norm_tile(nb, src[r0:r0 + 128, :], XN(r0), mm_["G"], mm_["S"], "c" if isc else "x",
                              hTc[hs][:, :, ti * 128:(ti + 1) * 128], "hTc%d" % hs)
                if l == 0:
                    P.dma(xs[r0:r0 + 128, :], nb.xt[s], r=["xt%d" % s], w=[XN(r0)])
            for m in range(8):
                pb = m % 4
                for k in range(8):
                    P.pe(lambda e, m=m, k=k, pb=pb, hs=hs, L=L: e.matmul(
                        ps[pb][:, 0:L], lhsT=win[:, k, m * 128:(m + 1) * 128], rhs=hTc[hs][:, k, 0:L],
                        start=(k == 0), stop=(k == 7)), r=["win", "hTc%d" % hs], w=[PN(pb)])
                if m % 2 == 0:
                    P.act(lambda e, m=m, pb=pb, s0=s0, L=L: e.activation(out=zT[:, m, s0:s0 + L], in_=ps[pb][:, 0:L],
                                                                         func=AF.Copy),
                          r=[PN(pb)], w=["zT%d.%d" % (m, ci)])
                else:
                    P.dve(lambda e, m=m, pb=pb, s0=s0, L=L: e.tensor_copy(out=zT[:, m, s0:s0 + L], in_=ps[pb][:, 0:L]),
                          r=[PN(pb)], w=["zT%d.%d" % (m, ci)])
        P.fence()
        ar.reset(mark2)
        ABt = ar.alloc([NT, 4, 256], BF16)
        dring = [ar.alloc([2, 512], BF16) for _ in range(6)]
        zall = ["zT%d.%d" % (m, ci) for m in range(8) for ci in range(len(chunks))]
        for tt in range(NT):
            for gp in range(2):
                pb = 4 + gp
                pv = ps[pb][:, :].rearrange("p (g n) -> p g n", g=2)
                for gi in range(2):
                    g = gp * 2 + gi
                    P.pe(lambda e, g=g, gi=gi, tt=tt, pv=pv: e.matmul(
                        pv[:, gi, :], lhsT=zT[:, g, tt * 128:(tt + 1) * 128], rhs=csc, start=True, stop=True),
                        r=zall + ["csc"], w=[PN(pb)])
                if gp == 0:
                    P.act(lambda e, tt=tt, gp=gp, pv=pv: e.activation(out=ABt[:, tt, gp * 2:gp * 2 + 2, :], in_=pv,
                                                                      func=AF.Copy),
                          r=[PN(pb)], w=["ABt"])
                else:
                    P.dve(lambda e, tt=tt, gp=gp, pv=pv: e.tensor_copy(out=ABt[:, tt, gp * 2:gp * 2 + 2, :], in_=pv),
                          r=[PN(pb)], w=["ABt"])
        di = 0
        for (s0, L, isc) in chunks:
            if last and isc:
                continue
            tts = list(range(NTC)) if isc else list(range(NTC, NT))
            ntt = len(tts)
            for ii, tt in enumerate(tts):
                sl = di % 6
                di += 1
                srcd = dftc[0, ii] if isc else dftx[(s0 - CTX) // 512, ii]
                P.dma(dring[sl][:, :, 0:L], srcd, w=["dr%d" % sl])
                for g in range(4):
                    for c in range(2):
                        P.pe(lambda e, g=g, c=c, tt=tt, sl=sl, L=L, ii=ii, ntt=ntt: e.matmul(
                            ps[g][:, 0:L], lhsT=ABt[:, tt, g, c * 128:(c + 1) * 128], rhs=dring[sl][:, c, 0:L],
                            start=(ii == 0 and c == 0), stop=(ii == ntt - 1 and c == 1)),
                            r=["ABt", "dr%d" % sl], w=[PN(g)])
            for g in range(4):
                if g % 2 == 0:
                    P.act(lambda e, g=g, s0=s0, L=L: e.activation(out=yT[:, g, s0:s0 + L], in_=ps[g][:, 0:L],
                                                                  func=AF.Copy), r=[PN(g)], w=["yT"])
                else:
                    P.dve(lambda e, g=g, s0=s0, L=L: e.tensor_copy(out=yT[:, g, s0:s0 + L], in_=ps[g][:, 0:L]),
                          r=[PN(g)], w=["yT"])
        P.fence()
        ar.reset(mark2)
        yacc = ar.alloc([4, T], F32)
        wglu = ar.alloc([4, 512], BF16)
        load_w(wglu, ssm_wglu[j], "wglu", 4)
        dsk = ar.alloc([4], F32)
        bglu = ar.alloc([4], F32)
        P.dma(dsk, ssm_d[j], w=["dsk"])
        P.dma(bglu, ssm_bglu[j], w=["bglu"])
        L = 256
        NW = 2
        wk = []
        for i in range(NW):
            wk.append(dict(tab=ar.alloc([4, 2, L], BF16), pb=[ar.alloc([4, L], BF16) for _ in range(2)],
                           m=[ar.alloc([4, L], BF16) for _ in range(4)],
                           hr=ar.alloc([4, L], BF16), hi=ar.alloc([4, L], BF16)))
        gst = [ar.alloc([2, 4, L], BF16) for _ in range(2)]
        BT, CT, rho = w["BT"], w["CT"], w["rho"]
        P1 = psall[:, 0:1024].rearrange("p (a t) -> p a t", a=4)
        P2 = psall[:, 1024:2048].rearrange("p (a t) -> p a t", a=4)
        PN1, PN2 = [PN(0), PN(1)], [PN(2), PN(3)]
        nctx = CTX // 256
        groups = []
        for d in range(2):
            idx = list(range(NCH))
            order = idx if d == 0 else idx[:nctx][::-1] + idx[nctx:][::-1]
            for kt in range(4):
                for oi, cidx in enumerate(order):
                    groups.append((d, kt, oi, cidx))

        def ph_a(gi):
            d, kt, oi, cidx = groups[gi]
            s0 = ch256[cidx][0]
            k_ = wk[gi % NW]
            wn = "wk%d" % (gi % NW)
            tab, pb, mm = k_["tab"], k_["pb"], k_["m"]
            cs, sn = tab[:, :, 0, :], tab[:, :, 1, :]
            P.dma(tab.rearrange("p a c t -> p (a c t)"), cstab[d * 4 + kt, cidx], r=["cstab"], w=[wn + "tab"])
            for rr in range(4):
                r = d * 16 + kt * 4 + rr
                P.pe(lambda e, r=r, rr=rr, kt=kt, s0=s0: e.matmul(
                    P1[:, rr, :], lhsT=BT[:, 0, r, :], rhs=zT[:, 4 + kt, s0:s0 + L], start=True, stop=True),
                    r=["sBT"] + zall, w=[PN(rr // 2)])
                P.pe(lambda e, r=r, rr=rr, kt=kt, s0=s0: e.matmul(
                    P2[:, rr, :], lhsT=BT[:, 1, r, :], rhs=zT[:, 4 + kt, s0:s0 + L], start=True, stop=True),
                    r=["sBT"] + zall, w=[PN(2 + rr // 2)])
            P.act(lambda e, pb=pb: e.activation(out=pb[0], in_=P1, func=AF.Copy), r=PN1, w=[wn + "pb0"])
            P.act(lambda e, pb=pb: e.activation(out=pb[1], in_=P2, func=AF.Copy), r=PN2, w=[wn + "pb1"])

        def ph_a2(gi):
            k_ = wk[gi % NW]
            wn = "wk%d" % (gi % NW)
            tab, pb, mm = k_["tab"], k_["pb"], k_["m"]
            cs, sn = tab[:, :, 0, :], tab[:, :, 1, :]
            P.dve(lambda e, cs=cs, mm=mm, pb=pb: e.tensor_tensor(out=mm[0], in0=pb[0], in1=cs, op=ALU.mult),
                  r=[wn + "pb0", wn + "tab"], w=[wn + "m0"])
            P.dve(lambda e, sn=sn, mm=mm, pb=pb: e.tensor_tensor(out=mm[1], in0=pb[1], in1=sn, op=ALU.mult),
                  r=[wn + "pb1", wn + "tab"], w=[wn + "m1"])
            P.dve(lambda e, mm=mm: e.tensor_tensor(out=mm[0], in0=mm[0], in1=mm[1], op=ALU.add),
                  r=[wn + "m0", wn + "m1"], w=[wn + "m0"])
            P.dve(lambda e, cs=cs, mm=mm, pb=pb: e.tensor_tensor(out=mm[2], in0=pb[1], in1=cs, op=ALU.mult),
                  r=[wn + "pb1", wn + "tab"], w=[wn + "m2"])
            P.dve(lambda e, sn=sn, mm=mm, pb=pb: e.tensor_tensor(out=mm[3], in0=pb[0], in1=sn, op=ALU.mult),
                  r=[wn + "pb0", wn + "tab"], w=[wn + "m3"])
            P.dve(lambda e, mm=mm: e.tensor_tensor(out=mm[2], in0=mm[2], in1=mm[3], op=ALU.subtract),
                  r=[wn + "m2", wn + "m3"], w=[wn + "m2"])

        def ph_b(gi):
            d, kt, oi, cidx = groups[gi]
            k_ = wk[gi % NW]
            wn = "wk%d" % (gi % NW)
            mm = k_["m"]
            g = gst[gi % 2]
            gn = "gst%d" % (gi % 2)
            rv = (lambda a: a) if d == 0 else (lambda a: a[:, ::-1])
            for rr in range(4):
                r = d * 16 + kt * 4 + rr
                for c in range(2):
                    if oi == 0:
                        init = 0.0
                        rdi = []
                    else:
                        pg = gst[(gi - 1) % 2]
                        init = (pg[:, c, rr, L - 1:L] if d == 0 else pg[:, c, rr, 0:1])
                        rdi = ["gst%d" % ((gi - 1) % 2)]
                    P.dve(lambda e, c=c, rr=rr, g=g, mm=mm, init=init, r=r, rv=rv: e.tensor_tensor_scan(
                        out=rv(g[:, c, rr, :]), data0=rho[:, r:r + 1].to_broadcast([128, L]),
                        data1=rv(mm[2 * c][:, rr, :]), initial=init, op0=ALU.mult, op1=ALU.add),
                        r=["srho", wn + "m%d" % (2 * c)] + rdi, w=[gn])

        def ph_c(gi):
            d, kt, oi, cidx = groups[gi]
            s0 = ch256[cidx][0]
            k_ = wk[gi % NW]
            wn = "wk%d" % (gi % NW)
            tab, mm, hr, hi = k_["tab"], k_["m"], k_["hr"], k_["hi"]
            cs, sn = tab[:, :, 0, :], tab[:, :, 1, :]
            g = gst[gi % 2]
            gn = "gst%d" % (gi % 2)
            ypb = 4 + (gi % 2)
            P.dve(lambda e, g=g, cs=cs, mm=mm: e.tensor_tensor(out=mm[1], in0=g[:, 0], in1=cs, op=ALU.mult),
                  r=[gn, wn + "tab", wn + "m1"], w=[wn + "m1"])
            P.dve(lambda e, g=g, sn=sn, mm=mm: e.tensor_tensor(out=mm[3], in0=g[:, 1], in1=sn, op=ALU.mult),
                  r=[gn, wn + "tab", wn + "m3"], w=[wn + "m3"])
            P.dve(lambda e, mm=mm, hr=hr: e.tensor_tensor(out=hr, in0=mm[1], in1=mm[3], op=ALU.subtract),
                  r=[wn + "m1", wn + "m3"], w=[wn + "hr"])
            P.dve(lambda e, g=g, sn=sn, mm=mm: e.tensor_tensor(out=mm[0], in0=g[:, 0], in1=sn, op=ALU.mult),
                  r=[gn, wn + "tab", wn + "m0"], w=[wn + "m0"])
            P.dve(lambda e, g=g, cs=cs, mm=mm: e.tensor_tensor(out=mm[2], in0=g[:, 1], in1=cs, op=ALU.mult),
                  r=[gn, wn + "tab", wn + "m2"], w=[wn + "m2"])
            P.dve(lambda e, mm=mm, hi=hi: e.scalar_tensor_tensor(
                out=hi, in0=mm[0], scalar=-1.0, in1=mm[2], op0=ALU.mult, op1=ALU.subtract),
                r=[wn + "m0", wn + "m2"], w=[wn + "hi"])
            for rr in range(4):
                r = d * 16 + kt * 4 + rr
                P.pe(lambda e, r=r, hr=hr, rr=rr, ypb=ypb: e.matmul(
                    ps[ypb][:, 0:L], lhsT=CT[:, 0, r, :], rhs=hr[:, rr, :], start=(rr == 0), stop=False),
                    r=["sCT", wn + "hr"], w=[PN(ypb)])
                P.pe(lambda e, r=r, hi=hi, rr=rr, ypb=ypb: e.matmul(
                    ps[ypb][:, 0:L], lhsT=CT[:, 1, r, :], rhs=hi[:, rr, :], start=False, stop=(rr == 3)),
                    r=["sCT", wn + "hi"], w=[PN(ypb)])
            yn = "yacc%d" % kt
            if d == 0:
                P.act(lambda e, kt=kt, s0=s0, ypb=ypb: e.activation(
                    out=yacc[:, kt, s0:s0 + L], in_=ps[ypb][:, 0:L], func=AF.Copy),
                    r=[PN(ypb)], w=[yn])
            else:
                P.dve(lambda e, kt=kt, s0=s0, ypb=ypb: e.tensor_tensor(
                    out=yacc[:, kt, s0:s0 + L], in0=ps[ypb][:, 0:L], in1=yacc[:, kt, s0:s0 + L], op=ALU.add),
                    r=[PN(ypb), yn], w=[yn])

        ph_a(0)
        ph_a2(0)
        for gi in range(len(groups)):
            if gi + 1 < len(groups):
                ph_a(gi + 1)
            ph_b(gi)
            if gi + 1 < len(groups):
                ph_a2(gi + 1)
            ph_c(gi)
        yb = yT[:, 4:8, :]
        for kt in range(4):
            P.dve(lambda e, kt=kt: e.scalar_tensor_tensor(
                out=yacc[:, kt, :], in0=zT[:, 4 + kt, :], scalar=dsk[:, kt:kt + 1], in1=yacc[:, kt, :],
                op0=ALU.mult, op1=ALU.add), r=zall + ["dsk", "yacc%d" % kt], w=["yacc%d" % kt])
            P.act(lambda e, kt=kt: e.activation(out=yb[:, kt, :], in_=yacc[:, kt, :], func=AF.Gelu_apprx_tanh),
                  r=["yacc%d" % kt], w=["yT"])
        sg = [ar.alloc([512], BF16) for _ in range(4)]
        for (s0, Lc, isc) in chunks:
            if last and isc:
                continue
            for m in range(4):
                pb = m
                for k in range(4):
                    P.pe(lambda e, m=m, k=k, pb=pb, s0=s0, Lc=Lc: e.matmul(
                        ps[pb][:, 0:Lc], lhsT=wglu[:, k, m * 128:(m + 1) * 128], rhs=yb[:, k, s0:s0 + Lc],
                        start=(k == 0), stop=(k == 3)), r=["wglu", "yT"], w=[PN(pb)])
                P.act(lambda e, m=m, pb=pb, Lc=Lc: e.activation(out=sg[m][:, 0:Lc], in_=ps[pb][:, 0:Lc],
                                                                func=AF.Sigmoid, bias=bglu[:, m:m + 1]),
                      r=[PN(pb), "bglu"], w=["sg%d" % m])
            for m in range(4):
                P.dve(lambda e, m=m, s0=s0, Lc=Lc: e.tensor_tensor(
                    out=yb[:, m, s0:s0 + Lc], in0=yb[:, m, s0:s0 + Lc], in1=sg[m][:, 0:Lc], op=ALU.mult),
                    r=["yT"] + ["sg%d" % i for i in range(4)], w=["yT"])
        P.fence()
        ar.reset(mark2)
        out_proj(l, b, last, yT, "yT", ev_w_out[j])
        ar.reset(mk0)

    def odd_mixer(l, j, b, last, w):
        mk0 = ar.off
        qk = ar.alloc([16, T], BF16)
        mark2 = ar.off
        win = ar.alloc([8, 3 * D], BF16)
        load_w(win, od_w_in[j], "win", 8)
        nb = NormBufs()
        mx = load_mod_tiles(l, b, 0, "x", nb.ng, "GS")
        mc = load_mod_tiles(l, NB, 0, "c", nb.ng, "GS")
        hTc = [ar.alloc([8, 256], BF16) for _ in range(2)]
        sq = [ar.alloc([256], BF16) for _ in range(2)]
        rs = [ar.alloc([256], F32) for _ in range(2)]
        qn = [ar.alloc([256], BF16) for _ in range(2)]
        tq = [ar.alloc([256], F32) for _ in range(2)]
        uq = [ar.alloc([256], F32) for _ in range(2)]
        vst = [ar.alloc([1024], BF16) for _ in range(2)]
        rope = w["rope"]
        it = 0
        L = 256
        def o_a(ci):
            s0_, _, isc_ = ch256[ci]
            mm_ = mc if isc_ else mx
            out = []
            for ti in range(2):
                r0 = b * T + s0_ + ti * 128
                out.append(norm_tile_a(nb, xs[r0:r0 + 128, :], XN(r0), mm_["G"], mm_["S"], "c" if isc_ else "x"))
            return out

        def o_b(ci, ss):
            hs_ = ci % 2
            for ti in range(2):
                norm_tile_b(nb, ss[ti], hTc[hs_][:, :, ti * 128:(ti + 1) * 128], "hTc%d" % hs_)

        o_b(0, o_a(0))
        for ci, (s0, _, isc) in enumerate(ch256):
            hs = ci % 2
            ss_next = o_a(ci + 1) if ci + 1 < len(ch256) else None
            for ti in range(2):
                vs = (s0 // 128 + ti) % 2
                for hf in range(2):
                    pb = 4 + hf
                    for k in range(8):
                        P.pe(lambda e, k=k, hf=hf, pb=pb, hs=hs, ti=ti: e.matmul(
                            ps[pb][:, :], lhsT=hTc[hs][:, k, ti * 128:(ti + 1) * 128],
                            rhs=win[:, k, 2048 + hf * 512:2048 + (hf + 1) * 512], start=(k == 0), stop=(k == 7)),
                            r=["win", "hTc%d" % hs], w=[PN(pb)])
                    if hf == 0:
                        P.act(lambda e, vs=vs, pb=pb: e.activation(out=vst[vs][:, 0:512], in_=ps[pb][:, :],
                                                                   func=AF.Copy), r=[PN(pb)], w=["vst%d" % vs])
                    else:
                        P.dve(lambda e, vs=vs, pb=pb: e.tensor_copy(out=vst[vs][:, 512:1024], in_=ps[pb][:, :]),
                              r=[PN(pb)], w=["vst%d" % vs])
                t0 = s0 + ti * 128
                P.dma(vtok[t0:t0 + 128, :], vst[vs], r=["vst%d" % vs], w=["vtok"])
            def q_proj(m, hs=hs):
                pq = m % 2
                for k in range(8):
                    P.pe(lambda e, m=m, k=k, pq=pq, hs=hs: e.matmul(
                        ps[pq][:, 0:L], lhsT=win[:, k, m * 128:(m + 1) * 128], rhs=hTc[hs][:, k, :],
                        start=(k == 0), stop=(k == 7)), r=["win", "hTc%d" % hs], w=[PN(pq)])

            def q_norm(m, ci=ci, s0=s0, isc=isc):
                s = m % 2
                pq = s
                pn_ = 2 + s
                P.act(lambda e, s=s, pq=pq: e.activation(out=sq[s], in_=ps[pq][:, 0:L], func=AF.Square),
                      r=[PN(pq)], w=["sq%d" % s])
                P.pe(lambda e, s=s, pn_=pn_: e.matmul(ps[pn_][:, 0:L], lhsT=bones, rhs=sq[s], start=True, stop=True),
                     r=["bones", "sq%d" % s], w=[PN(pn_)])
                P.dve(lambda e, s=s, pn_=pn_: e.tensor_scalar(out=rs[s], in0=ps[pn_][:, 0:L], scalar1=1.0 / 64,
                                                              scalar2=EPS, op0=ALU.mult, op1=ALU.add),
                      r=[PN(pn_)], w=["rs%d" % s])
                P.act(lambda e, s=s: e.activation(out=rs[s], in_=rs[s], func=AF.Sqrt), r=["rs%d" % s], w=["rs%d" % s])
                P.dve(lambda e, s=s: e.reciprocal(out=rs[s], in_=rs[s]), r=["rs%d" % s], w=["rs%d" % s])
                gcol = 0 if m < 8 else 1
                dst = qk[:, m, s0:s0 + L]
                dn = "qk%d.%d" % (m, ci)
                tgt = dst if isc else qn[s]
                tn = dn if isc else "qn%d" % s
                P.dve(lambda e, s=s, pq=pq, gcol=gcol, tgt=tgt: e.scalar_tensor_tensor(
                    out=tgt, in0=ps[pq][:, 0:L], scalar=w["gqk"][:, gcol:gcol + 1], in1=rs[s],
                    op0=ALU.mult, op1=ALU.mult), r=[PN(pq), "gqk", "rs%d" % s], w=[tn])

            def q_rope(m, ci=ci, s0=s0, isc=isc):
                if isc:
                    return
                s = m % 2
                pn_ = 2 + s
                dst = qk[:, m, s0:s0 + L]
                dn = "qk%d.%d" % (m, ci)
                x0 = s0 - CTX
                P.pe(lambda e, s=s, pn_=pn_: e.matmul(ps[pn_][:, 0:L], lhsT=rmat, rhs=qn[s], start=True, stop=True),
                     r=["rmat", "qn%d" % s], w=[PN(pn_)])
                P.dve(lambda e, s=s, pn_=pn_, x0=x0: e.tensor_tensor(
                    out=tq[s], in0=ps[pn_][:, 0:L], in1=rope[:, 1, x0:x0 + L], op=ALU.mult),
                    r=[PN(pn_), "rope"], w=["tq%d" % s])
                P.pool(lambda e, s=s, x0=x0: e.tensor_tensor(
                    out=uq[s], in0=qn[s], in1=rope[:, 0, x0:x0 + L], op=ALU.mult),
                    r=["qn%d" % s, "rope"], w=["uq%d" % s])
                P.pool(lambda e, s=s, dst=dst: e.tensor_tensor(out=dst, in0=tq[s], in1=uq[s], op=ALU.add),
                       r=["tq%d" % s, "uq%d" % s], w=[dn])

            q_proj(0)
            for m in range(16):
                if m + 1 < 16:
                    q_proj(m + 1)
                q_norm(m)
                if m >= 1:
                    q_rope(m - 1)
            q_rope(15)
            if ss_next is not None:
                o_b(ci + 1, ss_next)
        P.fence()
        ar.reset(mark2)
        yT = ar.alloc([8, T], BF16)
        mark3 = ar.off
        vh = [ar.alloc([NT, 128], BF16) for _ in range(2)]
        eb = [ar.alloc([512], BF16) for _ in range(8)]
        acc = [[[ar.alloc([512], F32) for _ in range(2)] for _ in range(2)] for _ in range(2)]
        accb = [ar.alloc([512], BF16) for _ in range(2)]
        rc = [ar.alloc([512], F32) for _ in range(2)]
        oa = [ar.alloc([512], F32) for _ in range(2)]
        of = ar.alloc([512], F32)
        o2 = ar.alloc([512], BF16)
        rs2 = ar.alloc([512], F32)
        qall = ["qk%d.%d" % (m, ci) for m in range(16) for ci in range(len(ch256))]
        ei = 0
        vt3 = vtok.rearrange("(t p) f -> p t f", p=128)
        units = [(h, s0, Lc, isc) for h in range(8) for (s0, Lc, isc) in chunks if not (last and isc)]

        def make_epi(u, h, s0, Lc):
            up = u % 2
            pvb = [2, 3] if up == 0 else [4, 5]

            def part1():
                for m in range(2):
                    P.pool(lambda e, m=m: e.tensor_tensor(out=accb[m][:, 0:Lc], in0=acc[up][m][0][:, 0:Lc],
                                                          in1=acc[up][m][1][:, 0:Lc], op=ALU.add),
                           r=["acc%d.%d.0" % (up, m), "acc%d.%d.1" % (up, m)], w=["accb%d" % m])
                    P.pe(lambda e, m=m: e.matmul(ps[6 + m][:, 0:Lc], lhsT=ones, rhs=accb[m][:, 0:Lc],
                                                 start=True, stop=True),
                         r=["ones", "accb%d" % m], w=[PN(6 + m)])
                for m in range(2):
                    P.act(lambda e, m=m: e.activation(out=rc[m][:, 0:Lc], in_=ps[6 + m][:, 0:Lc], func=AF.Ln),
                          r=[PN(6 + m)], w=["rc%d" % m])
                    P.act(lambda e, m=m: e.activation(out=rc[m][:, 0:Lc], in_=rc[m][:, 0:Lc], func=AF.Exp,
                                                      scale=-1.0), r=["rc%d" % m], w=["rc%d" % m])
                    P.dve(lambda e, m=m: e.tensor_tensor(out=oa[m][:, 0:Lc], in0=ps[pvb[m]][:, 0:Lc],
                                                         in1=rc[m][:, 0:Lc], op=ALU.mult),
                          r=[PN(pvb[m]), "rc%d" % m], w=["oa%d" % m])
                P.dve(lambda e: e.scalar_tensor_tensor(out=of[:, 0:Lc], in0=oa[1][:, 0:Lc],
                                                       scalar=w["nlam"][:, 0:1], in1=oa[0][:, 0:Lc],
                                                       op0=ALU.mult, op1=ALU.add),
                      r=["oa0", "oa1", "nlam"], w=["of"])
                P.pool(lambda e: e.tensor_tensor(out=o2[:, 0:Lc], in0=of[:, 0:Lc], in1=of[:, 0:Lc], op=ALU.mult),
                       r=["of"], w=["o2"])

            def part2():
                P.pe(lambda e: e.matmul(ps[6][:, 0:Lc], lhsT=ones, rhs=o2[:, 0:Lc], start=True, stop=True),
                     r=["ones", "o2"], w=[PN(6)])
                P.act(lambda e: e.activation(out=rs2[:, 0:Lc], in_=ps[6][:, 0:Lc], func=AF.Ln, scale=1.0 / 128,
                                             bias=epsc[:, 0:1]), r=[PN(6), "epsc"], w=["rs2"])
                P.act(lambda e: e.activation(out=rs2[:, 0:Lc], in_=rs2[:, 0:Lc], func=AF.Exp, scale=-0.5),
                      r=["rs2"], w=["rs2"])
                P.dve(lambda e: e.scalar_tensor_tensor(
                    out=yT[:, h, s0:s0 + Lc], in0=of[:, 0:Lc], scalar=w["ghs"][:, 0:1], in1=rs2[:, 0:Lc],
                    op0=ALU.mult, op1=ALU.mult), r=["of", "ghs", "rs2"], w=["yT"])

            return [part1, part2]

        pending = []
        cur_h = -1
        for u, (h, s0, Lc, isc) in enumerate(units):
            vs = h % 2
            up = u % 2
            pvb = [2, 3] if up == 0 else [4, 5]
            if h != cur_h:
                cur_h = h
                P.dma(vh[vs], vt3[:, :, h * 128:(h + 1) * 128], r=["vtok"], w=["vh%d" % vs])
            kts = list(range(NTC)) if isc else list(range(NT))
            nk = len(kts)
            items = [(m, ki, kt) for m in range(2) for ki, kt in enumerate(kts)]

            def s_mm(ii, h=h, s0=s0, Lc=Lc, items=items):
                m, ki, kt = items[ii]
                sb = ii % 2
                P.pe(lambda e, m=m, kt=kt, sb=sb: e.matmul(
                    ps[sb][:, 0:Lc], lhsT=qk[64 * m:64 * m + 64, 8 + h, kt * 128:(kt + 1) * 128],
                    rhs=qk[64 * m:64 * m + 64, h, s0:s0 + Lc], start=True, stop=True),
                    r=qall, w=[PN(sb)])

            s_mm(0)
            for ii, (m, ki, kt) in enumerate(items):
                if ii + 1 < len(items):
                    s_mm(ii + 1)
                if pending and ii == 2:
                    pending.pop(0)()
                if pending and ii == 20:
                    pending.pop(0)()
                sb = ii % 2
                es = ei % 8
                ei += 1
                P.act(lambda e, sb=sb, es=es, Lc=Lc: e.activation(out=eb[es][:, 0:Lc], in_=ps[sb][:, 0:Lc],
                                                                  func=AF.Exp, scale=0.125),
                      r=[PN(sb)], w=["eb%d" % es])
                P.pe(lambda e, m=m, kt=kt, es=es, vs=vs, Lc=Lc, ki=ki, nk=nk, pvb=pvb: e.matmul(
                    ps[pvb[m]][:, 0:Lc], lhsT=vh[vs][:, kt, :], rhs=eb[es][:, 0:Lc], start=(ki == 0),
                    stop=(ki == nk - 1)), r=["vh%d" % vs, "eb%d" % es], w=[PN(pvb[m])])
                a_ = acc[up][m][ki % 2]
                an = "acc%d.%d.%d" % (up, m, ki % 2)
                eng = P.pool if ki % 2 == 0 else P.dve
                if ki < 2:
                    eng(lambda e, a_=a_, es=es, Lc=Lc: e.tensor_copy(out=a_[:, 0:Lc], in_=eb[es][:, 0:Lc]),
                        r=["eb%d" % es], w=[an])
                else:
                    eng(lambda e, a_=a_, es=es, Lc=Lc: e.tensor_tensor(out=a_[:, 0:Lc], in0=a_[:, 0:Lc],
                                                                      in1=eb[es][:, 0:Lc], op=ALU.add),
                        r=["eb%d" % es, an], w=[an])
            while pending:
                pending.pop(0)()
            pending = make_epi(u, h, s0, Lc)
        while pending:
            pending.pop(0)()
        P.fence()
        ar.reset(mark3)
        out_proj(l, b, last, yT, "yT", od_w_out[j])
        ar.reset(mk0)

    def mlp_stage(l, last):
        ar.reset()
        w1 = ar.alloc([8, DFF], BF16)
        w2 = ar.alloc([32, D], BF16)
        load_w(w1, mlp_w1[l], "w1", 8)
        load_w(w2, mlp_w2[l], "w2", 32)
        nb = NormBufs(with_xt=False)
        xk = [[ar.alloc([D], F32) for _ in range(2)] for _ in range(2)]
        hTc = [ar.alloc([8, 256], BF16) for _ in range(2)]
        hid = ar.alloc([32, 256], BF16)
        rl = [ar.alloc([256], F32) for _ in range(2)]
        mt = ar.alloc([D], F32)
        mk = ar.off
        ci = 0
        for b in range(NB):
            for isc in (True, False):
                if last and isc:
                    continue
                ar.reset(mk)
                md = load_mod_tiles(l, NB if isc else b, 1, "mm", nb.ng, "GSg")
                G, S, Gt = md["G"], md["S"], md["Gt"]
                t_lo, t_hi = (0, CTX) if isc else (CTX, T)
                starts = list(range(t_lo, t_hi, 256))

                def do_a(c0, cs_, b=b, G=G, S=S):
                    ss = []
                    for ti in range(2):
                        r0 = b * T + c0 + ti * 128
                        ss.append(norm_tile_a(nb, xs[r0:r0 + 128, :], XN(r0), G, S, "mm",
                                              keep_x=(xk[cs_][ti], "xk%d.%d" % (cs_, ti))))
                    return ss

                ss_next = do_a(starts[0], ci % 2)
                for idx, c0 in enumerate(starts):
                    cs_ = ci % 2
                    ci += 1
                    ss = ss_next
                    for ti in range(2):
                        norm_tile_b(nb, ss[ti], hTc[cs_][:, :, ti * 128:(ti + 1) * 128], "mh%d" % cs_)
                    for jf in range(32):
                        pb = jf % 2
                        for k in range(8):
                            P.pe(lambda e, jf=jf, k=k, pb=pb, cs_=cs_: e.matmul(
                                ps[pb][:, 0:256], lhsT=w1[:, k, jf * 128:(jf + 1) * 128], rhs=hTc[cs_][:, k, :],
                                start=(k == 0), stop=(k == 7)), r=["w1", "mh%d" % cs_], w=[PN(pb)])
                        P.act(lambda e, pb=pb: e.activation(out=rl[pb], in_=ps[pb][:, 0:256], func=AF.Relu),
                              r=[PN(pb)], w=["rl%d" % pb])
                        P.dve(lambda e, pb=pb, jf=jf: e.tensor_tensor(out=hid[:, jf, :], in0=ps[pb][:, 0:256],
                                                                      in1=rl[pb], op=ALU.mult),
                              r=[PN(pb), "rl%d" % pb], w=["hid%d" % jf])
                    hall = ["hid%d" % jf for jf in range(32)]
                    if idx + 1 < len(starts):
                        ss_next = do_a(starts[idx + 1], ci % 2)
                    for ti in range(2):
                        xt = xk[cs_][ti]
                        xn = "xk%d.%d" % (cs_, ti)
                        r0 = b * T + c0 + ti * 128
                        for hf in range(2):
                            pb = 2 + (ti * 2 + hf) % 4
                            for jf in range(32):
                                P.pe(lambda e, jf=jf, hf=hf, pb=pb, ti=ti: e.matmul(
                                    ps[pb][:, :], lhsT=hid[:, jf, ti * 128:(ti + 1) * 128],
                                    rhs=w2[:, jf, hf * 512:(hf + 1) * 512], start=(jf == 0), stop=(jf == 31)),
                                    r=hall + ["w2"], w=[PN(pb)])
                            P.dve(lambda e, hf=hf, pb=pb, Gt=Gt: e.tensor_tensor(
                                out=mt[:, hf * 512:(hf + 1) * 512], in0=ps[pb][:, :],
                                in1=Gt[:, hf * 512:(hf + 1) * 512], op=ALU.mult),
                                r=[PN(pb), "mmGt"], w=["mt1%d" % hf])
                            P.pool(lambda e, hf=hf, xt=xt: e.tensor_tensor(
                                out=xt[:, hf * 512:(hf + 1) * 512], in0=xt[:, hf * 512:(hf + 1) * 512],
                                in1=mt[:, hf * 512:(hf + 1) * 512], op=ALU.add), r=["mt1%d" % hf, xn], w=[xn])
                        if last:
                            o0 = b * SEQ + (c0 - CTX) + ti * 128
                            P.dma(yout[o0:o0 + 128, :], xt, r=[xn], w=["youtd"])
                        else:
                            P.dma(xs[r0:r0 + 128, :], xt, r=[xn], w=[XN(r0)])
        P.fence()

    stage_adaln()
    for l in range(DEPTH):
        last = l == DEPTH - 1
        j = l // 2
        even = l % 2 == 0
        ar.reset()
        w = {}
        if even:
            ssm_prep(j, w)
        else:
            w["rope"] = ar.alloc([2, SEQ], BF16)
            P.dma(w["rope"], rope_d.rearrange("c p t -> p c t"), w=["rope"], q="pool")
            w["gqk"] = ar.alloc([2], F32)
            w["ghs"] = ar.alloc([1], F32)
            w["nlam"] = ar.alloc([1], F32)
            P.dma(w["gqk"], od_qk[j], w=["gqk"])
            P.dma(w["ghs"], od_hn[j], w=["ghs"])
            lp = ar.alloc([256], F32)
            lt = ar.alloc([8], F32)
            P.dma(lp, od_lam[j, :].partition_broadcast(128), w=["lp"])
            lam_init = 0.8 - 0.6 * math.exp(-0.3 * l)
            P.dve(lambda e, lp=lp: e.tensor_tensor(out=lp[:, 0:64], in0=lp[:, 0:64], in1=lp[:, 64:128], op=ALU.mult),
                  r=["lp"], w=["lp"])
            P.dve(lambda e, lp=lp: e.tensor_tensor(out=lp[:, 128:192], in0=lp[:, 128:192], in1=lp[:, 192:256], op=ALU.mult),
                  r=["lp"], w=["lp"])
            P.dve(lambda e, lp=lp, lt=lt: e.tensor_reduce(out=lt[:, 0:1], in_=lp[:, 0:64], op=ALU.add,
                                            axis=mybir.AxisListType.X), r=["lp"], w=["lt"])
            P.dve(lambda e, lp=lp, lt=lt: e.tensor_reduce(out=lt[:, 1:2], in_=lp[:, 128:192], op=ALU.add,
                                            axis=mybir.AxisListType.X), r=["lp"], w=["lt"])
            P.act(lambda e, lt=lt: e.activation(out=lt[:, 2:4], in_=lt[:, 0:2], func=AF.Exp), r=["lt"], w=["lt"])
            P.dve(lambda e, lt=lt: e.tensor_tensor(out=lt[:, 4:5], in0=lt[:, 3:4], in1=lt[:, 2:3], op=ALU.subtract),
                  r=["lt"], w=["lt"])
            P.dve(lambda e, lam_init=lam_init, w=w, lt=lt: e.tensor_scalar(out=w["nlam"], in0=lt[:, 4:5], scalar1=-lam_init,
                                                               scalar2=None, op0=ALU.add), r=["lt"], w=["nlam"])
            P.dve(lambda e, lam_init=lam_init, w=w: e.tensor_scalar(out=w["ghs"], in0=w["ghs"], scalar1=1.0 - lam_init,
                                                               scalar2=None, op0=ALU.mult), r=["ghs"], w=["ghs"])
        for b in range(NB):
            (even_mixer if even else odd_mixer)(l, j, b, last, w)
        P.fence()
        mlp_stage(l, last)
    P.finalize(st)
    P.emit()
    st.close()
    _STATS['ops'] = len(P.ops)
    return nc


def _consts(SEQ, CTX, GRID_W=64):
    bf = ml_dtypes.bfloat16
    c = np.arange(128)
    ang = 2 * np.pi * np.outer(c, c) / 128
    csc = np.concatenate([np.cos(ang), np.sin(ang)], axis=1).astype(np.float32)

    def dft(L):
        t = np.arange(L, dtype=np.float64)
        a = 2 * np.pi * (np.outer(t, t) % L) / L
        nrm = 1.0 / math.sqrt(L * 128)
        return np.cos(a) * nrm, -np.sin(a) * nrm

    Cx, Sx = dft(SEQ)
    ncx = SEQ // 512
    dftx = np.stack([Cx, Sx], 0).reshape(2, SEQ // 128, 128, ncx, 512).transpose(3, 1, 2, 0, 4)
    Cc, Sc = dft(CTX)
    dftc = np.stack([Cc, Sc], 0).reshape(2, CTX // 128, 128, 1, CTX).transpose(3, 1, 2, 0, 4)
    T = CTX + SEQ
    j = np.arange(T)
    tau_f = j.astype(np.float32)
    tau_b = np.where(j < CTX, CTX - 1 - j, CTX + (T - 1 - j)).astype(np.float32)
    tau = np.stack([tau_f, tau_b], 0)
    half = 32
    inv = 10000.0 ** (-np.arange(0, half, 2, dtype=np.float32) / half)
    t = np.arange(SEQ)
    row = (t // GRID_W).astype(np.float32)
    col = (t % GRID_W).astype(np.float32)
    ar_ = row[None, :] * inv[:, None]
    ac_ = col[None, :] * inv[:, None]
    a64 = np.concatenate([ar_, ar_, ac_, ac_], 0)
    a128 = np.concatenate([a64, a64], 0).astype(np.float32)
    rope = np.stack([np.cos(a128), np.sin(a128)], 0).astype(np.float32)
    R = np.zeros((128, 128), np.float32)
    for blk in range(4):
        o = blk * 32
        for i in range(16):
            R[o + i, o + i + 16] = -1.0
            R[o + i + 16, o + i] = 1.0
    rmat = np.ascontiguousarray(R.T)
    bones = np.zeros((128, 128), np.float32)
    bones[:64, :64] = 1
    bones[64:, 64:] = 1
    return dict(csc=csc, dftx=np.ascontiguousarray(dftx).astype(bf), dftc=np.ascontiguousarray(dftc).astype(bf),
                iota=np.arange(256, dtype=np.float32)[None, :], rope=rope, rmat=rmat, bones=bones, ident=np.eye(128, dtype=np.float32))


def _prep_weights(inp, DEPTH):
    f = lambda a: np.ascontiguousarray(np.asarray(a, dtype=np.float32))
    n_even = (DEPTH + 1) // 2
    n_odd = DEPTH // 2
    out = dict(norm_g=f(np.stack([inp["norm1_g"][:DEPTH], inp["norm2_g"][:DEPTH]], 1)),
               ada_w=f(inp["ada_w"][:DEPTH]), ada_b=f(inp["ada_b"][:DEPTH]),
               mlp_w1=f(inp["mlp_w1"][:DEPTH]), mlp_w2=f(inp["mlp_w2"][:DEPTH]))
    if n_even:
        out["ev_w_in"] = f(inp["ev_w_in"][:n_even])
        out["ev_w_out"] = f(inp["ev_w_out"][:n_even])
        are = np.asarray(inp["ssm_a_re"])[:n_even].reshape(n_even, 32, 128).transpose(0, 2, 1)
        aim = np.asarray(inp["ssm_a_im"])[:n_even].reshape(n_even, 32, 128).transpose(0, 2, 1)
        ldt = np.repeat(np.asarray(inp["ssm_log_dt"])[:n_even].reshape(n_even, 64), 64, axis=1)
        ldt = ldt.reshape(n_even, 32, 128).transpose(0, 2, 1)
        out["ssm_sc"] = f(np.stack([are, aim, ldt], 1))
        bblk = np.zeros((n_even, 2, 32, 128, 128), np.float32)
        cblk = np.zeros((n_even, 2, 32, 128, 128), np.float32)
        for c, (bn, cn) in enumerate((("ssm_b_re", "ssm_c_re"), ("ssm_b_im", "ssm_c_im"))):
            B = np.asarray(inp[bn])[:n_even]
            C = np.asarray(inp[cn])[:n_even]
            for d in range(2):
                for g in range(32):
                    r = d * 16 + g // 2
                    p0 = (g % 2) * 64
                    c0 = (g % 8) * 16
                    bblk[:, c, r, p0:p0 + 64, c0:c0 + 16] = B[:, d, g]
                    cblk[:, c, r, p0:p0 + 64, c0:c0 + 16] = C[:, d, g].transpose(0, 2, 1)
        out["ssm_bblk"] = bblk
        out["ssm_cblk"] = cblk
        out["ssm_d"] = f(np.asarray(inp["ssm_d"])[:n_even].reshape(n_even, 4, 128).transpose(0, 2, 1))
        out["ssm_bglu"] = f(np.asarray(inp["ssm_b_glu"])[:n_even].reshape(n_even, 4, 128).transpose(0, 2, 1))
        out["ssm_wglu"] = f(inp["ssm_w_glu"][:n_even])
    if n_odd:
        out["od_w_in"] = f(inp["od_w_in"][:n_odd])
        out["od_w_out"] = f(inp["od_w_out"][:n_odd])
        gq = np.tile(np.asarray(inp["od_q_norm"])[:n_odd], (1, 2))
        gk = np.tile(np.asarray(inp["od_k_norm"])[:n_odd], (1, 2))
        out["od_qk"] = f(np.stack([gq, gk], -1))
        out["od_lam"] = f(np.asarray(inp["od_lambda"])[:n_odd].reshape(n_odd, 256))
        out["od_hn"] = f(np.asarray(inp["od_head_norm"])[:n_odd].reshape(n_odd, 128, 1))
    return out


_CACHE = {}
_STATS = {}


def run(inputs, DEPTH=4, n_cores=8, GRID_W=64):
    x = np.asarray(inputs["x"], dtype=np.float32)
    ctx = np.asarray(inputs["ctx"], dtype=np.float32)
    c = np.asarray(inputs["c"], dtype=np.float32)
    c_ctx = np.asarray(inputs["c_ctx"], dtype=np.float32)
    B, SEQ, _ = x.shape
    CTX = ctx.shape[1]
    NB = B // n_cores
    T = SEQ + CTX
    key = (NB, SEQ, CTX, DEPTH)
    if key not in _CACHE:
        _CACHE[key] = build(NB, SEQ, CTX, DEPTH, GRID_W)
    nc = _CACHE[key]
    shared = _prep_weights(inputs, DEPTH)
    consts = _consts(SEQ, CTX, GRID_W)
    n_even = (DEPTH + 1) // 2
    n_odd = DEPTH // 2
    if not n_even:
        for k in ("csc", "dftx", "dftc", "iota"):
            consts.pop(k)
    if not n_odd:
        for k in ("rope", "rmat", "bones"):
            consts.pop(k)
    shared.update(consts)
    in_maps = []
    for i in range(n_cores):
        sl = slice(i * NB, (i + 1) * NB)
        xin = np.concatenate([ctx[sl], x[sl]], axis=1).reshape(NB * T, D)
        cond = np.concatenate([c[sl], c_ctx[None, :]], 0)
        condT = np.ascontiguousarray(cond.T.reshape(8, 128, NB + 1).transpose(1, 0, 2))
        m = dict(shared)
        m["xin"] = np.ascontiguousarray(xin)
        m["condT"] = condT
        in_maps.append(m)
    res = run_bass_kernel_spmd(nc, in_maps, core_ids=list(range(n_cores)))
    outs = [np.asarray(r["yout"]).reshape(NB, SEQ, D) for r in res.results]
    return np.concatenate(outs, 0).astype(np.float32)


def kernel(**inputs):
    return run(inputs, DEPTH=4, n_cores=8)
```
